# Optimizing a Trainium2 kernel written in Bass

```python
import jax, jax.numpy as jnp
from jax import lax
import numpy as np

D_MODEL = 2048
BATCH = 4
SEQ = 2048
DEPTH = 1
DEC_BATCH = 128
DEC_SEQ = 8
PAST_LEN = 16384
PAGE_SIZE = 128

N_META = 16
D_A = 1536
LRU_BLOCK = 128
N_LRU_BLOCKS = D_A // LRU_BLOCK
CONV_A = 4
LRU_C = 8.0
D_B = 1024
N_SC_GROUPS = 8
CONV_B = 3
D_MIX = D_A + D_B
D_IN = 2 * D_A + 3 * D_B
D_FF = 3 * D_MODEL
CONV_F = 3
EPS = 1e-6

kernel_name = "hymba_rglru_shortconv_convffn_step"


def rmsnorm(x, g):
    xf = x.astype(jnp.float32)
    ms = jnp.mean(xf * xf, axis=-1, keepdims=True)
    return (xf * lax.rsqrt(ms + EPS) * g.astype(jnp.float32)).astype(x.dtype)


def group_rmsnorm(y, g, n_groups):
    shp = y.shape
    yf = y.astype(jnp.float32).reshape(shp[:-1] + (n_groups, shp[-1] // n_groups))
    ms = jnp.mean(yf * yf, axis=-1, keepdims=True)
    yn = (yf * lax.rsqrt(ms + EPS)).reshape(shp)
    return (yn * g.astype(jnp.float32)).astype(y.dtype)


def causal_dwconv(x, buf, w):
    width = w.shape[0]
    t_len = x.shape[1]
    xp = jnp.concatenate([buf.astype(x.dtype), x], axis=1)
    w = w.astype(x.dtype)
    y = xp[:, 0:t_len] * w[0]
    for k in range(1, width):
        y = y + xp[:, k:k + t_len] * w[k]
    return y, xp[:, t_len:]


def rg_lru(x, h0, w_gate_a, b_gate_a, w_gate_x, b_gate_x, lam):
    bsz, t_len, _ = x.shape
    xf = x.astype(jnp.float32)
    xb = xf.reshape(bsz, t_len, N_LRU_BLOCKS, LRU_BLOCK)
    r = jax.nn.sigmoid(jnp.einsum("btnk,nkj->btnj", xb, w_gate_a.astype(jnp.float32)).reshape(bsz, t_len, D_A)
                       + b_gate_a.astype(jnp.float32))
    i = jax.nn.sigmoid(jnp.einsum("btnk,nkj->btnj", xb, w_gate_x.astype(jnp.float32)).reshape(bsz, t_len, D_A)
                       + b_gate_x.astype(jnp.float32))
    log_a = -LRU_C * r * jax.nn.softplus(-lam.astype(jnp.float32))
    a = jnp.exp(log_a)
    u = jnp.sqrt(-jnp.expm1(2.0 * log_a)) * (i * xf)

    def step(h, au):
        a_t, u_t = au
        h = a_t * h + u_t
        return h, h

    h_last, hs = lax.scan(step, h0.astype(jnp.float32),
                          (jnp.swapaxes(a, 0, 1), jnp.swapaxes(u, 0, 1)))
    return jnp.swapaxes(hs, 0, 1), h_last


def hybrid_layer(x, st_h, st_rconv, st_sconv, st_fconv,
                 g_mix, w_in, conv_a_w, conv_a_b, w_gate_a, b_gate_a, w_gate_x, b_gate_x,
                 lru_lambda, conv_b_w, g_out_a, g_out_b, w_o, g_ffn, w_up, conv_f_w,
                 conv_f_b, w_down):
    xn = rmsnorm(x, g_mix)
    z = jnp.einsum("btd,de->bte", xn, w_in)
    xa, ga, gb, gc, vb = jnp.split(z, [D_A, 2 * D_A, 2 * D_A + D_B, 2 * D_A + 2 * D_B], axis=-1)
    xa_c, new_rconv = causal_dwconv(xa, st_rconv, conv_a_w)
    xa_c = xa_c + conv_a_b
    hs, new_h = rg_lru(xa_c, st_h, w_gate_a, b_gate_a, w_gate_x, b_gate_x, lru_lambda)
    y_a = jax.nn.gelu(ga) * hs.astype(x.dtype)
    u = gc * vb
    uc, new_sconv = causal_dwconv(u, st_sconv, conv_b_w)
    y_b = gb * uc
    y_mix = jnp.concatenate([group_rmsnorm(y_a, g_out_a, N_LRU_BLOCKS),
                             group_rmsnorm(y_b, g_out_b, N_SC_GROUPS)], axis=-1)
    x = x + jnp.einsum("bte,ed->btd", y_mix, w_o)
    xn2 = rmsnorm(x, g_ffn)
    up = jnp.einsum("btd,df->btf", xn2, w_up)
    gate, val = jnp.split(up, [D_FF], axis=-1)
    gate_c, new_fconv = causal_dwconv(gate, st_fconv, conv_f_w)
    hid = jax.nn.gelu(gate_c + conv_f_b) * val
    x = x + jnp.einsum("btf,fd->btd", hid, w_down)
    return x, new_h, new_rconv, new_sconv, new_fconv


def setup_inputs(seed: int = 0) -> dict:
    key = jax.random.key(seed)
    ks = jax.random.split(key, 32)
    f32 = jnp.float32
    L = DEPTH

    def nrm(k, shape, scale):
        return jax.random.normal(k, shape, f32) * scale

    a0 = jax.random.uniform(ks[15], (L, D_A), f32, 0.9, 0.999)
    return {
        "x_prompt": nrm(ks[0], (BATCH, SEQ, D_MODEL), 1.0),
        "x_sample": nrm(ks[1], (DEC_BATCH, DEC_SEQ, D_MODEL), 1.0),
        "state_lru_h": nrm(ks[2], (L, DEC_BATCH, D_A), 0.5),
        "state_lru_conv": nrm(ks[3], (L, DEC_BATCH, CONV_A - 1, D_A), 1.0),
        "state_sconv": nrm(ks[4], (L, DEC_BATCH, CONV_B - 1, D_B), 1.0),
        "state_ffn_conv": nrm(ks[5], (L, DEC_BATCH, CONV_F - 1, D_FF), 1.0),
        "meta_tokens": nrm(ks[6], (N_META, D_MODEL), 1.0),
        "g_mix": 1.0 + nrm(ks[7], (L, D_MODEL), 0.02),
        "w_in": nrm(ks[8], (L, D_MODEL, D_IN), D_MODEL ** -0.5),
        "conv_a_w": nrm(ks[9], (L, CONV_A, D_A), CONV_A ** -0.5),
        "conv_a_b": nrm(ks[10], (L, D_A), 0.02),
        "w_gate_a": nrm(ks[11], (L, N_LRU_BLOCKS, LRU_BLOCK, LRU_BLOCK), LRU_BLOCK ** -0.5),
        "b_gate_a": nrm(ks[12], (L, D_A), 0.02),
        "w_gate_x": nrm(ks[13], (L, N_LRU_BLOCKS, LRU_BLOCK, LRU_BLOCK), LRU_BLOCK ** -0.5),
        "b_gate_x": nrm(ks[14], (L, D_A), 0.02),
        "lru_lambda": jnp.log(a0) - jnp.log1p(-a0),
        "conv_b_w": nrm(ks[16], (L, CONV_B, D_B), CONV_B ** -0.5),
        "g_out_a": 1.0 + nrm(ks[17], (L, D_A), 0.02),
        "g_out_b": 1.0 + nrm(ks[18], (L, D_B), 0.02),
        "w_o": nrm(ks[19], (L, D_MIX, D_MODEL), D_MIX ** -0.5),
        "g_ffn": 1.0 + nrm(ks[20], (L, D_MODEL), 0.02),
        "w_up": nrm(ks[21], (L, D_MODEL, 2 * D_FF), D_MODEL ** -0.5),
        "conv_f_w": nrm(ks[22], (L, CONV_F, D_FF), CONV_F ** -0.5),
        "conv_f_b": nrm(ks[23], (L, D_FF), 0.02),
        "w_down": nrm(ks[24], (L, D_FF, D_MODEL), D_FF ** -0.5),
        "g_final": 1.0 + nrm(ks[25], (D_MODEL,), 0.02),
    }


def reference(x_prompt, x_sample, state_lru_h, state_lru_conv, state_sconv, state_ffn_conv,
              meta_tokens, g_mix, w_in, conv_a_w, conv_a_b, w_gate_a, b_gate_a, w_gate_x,
              b_gate_x, lru_lambda, conv_b_w, g_out_a, g_out_b, w_o, g_ffn, w_up, conv_f_w,
              conv_f_b, w_down, g_final):
    dt = x_prompt.dtype
    meta = jnp.broadcast_to(meta_tokens.astype(dt)[None], (BATCH, N_META, D_MODEL))
    xp = jnp.concatenate([meta, x_prompt], axis=1)
    xs = x_sample
    zero_h = jnp.zeros((BATCH, D_A), jnp.float32)
    zero_rconv = jnp.zeros((BATCH, CONV_A - 1, D_A), dt)
    zero_sconv = jnp.zeros((BATCH, CONV_B - 1, D_B), dt)
    zero_fconv = jnp.zeros((BATCH, CONV_F - 1, D_FF), dt)

    p_h, p_rc, p_sc, p_fc = [], [], [], []
    s_h, s_rc, s_sc, s_fc = [], [], [], []
    for l in range(DEPTH):
        lw = (g_mix[l], w_in[l], conv_a_w[l], conv_a_b[l], w_gate_a[l], b_gate_a[l],
              w_gate_x[l], b_gate_x[l], lru_lambda[l], conv_b_w[l], g_out_a[l], g_out_b[l],
              w_o[l], g_ffn[l], w_up[l], conv_f_w[l], conv_f_b[l], w_down[l])
        xp, h1, rc1, sc1, fc1 = hybrid_layer(xp, zero_h, zero_rconv, zero_sconv, zero_fconv, *lw)
        xs, h2, rc2, sc2, fc2 = hybrid_layer(xs, state_lru_h[l], state_lru_conv[l],
                                             state_sconv[l], state_ffn_conv[l], *lw)
        p_h.append(h1); p_rc.append(rc1); p_sc.append(sc1); p_fc.append(fc1)
        s_h.append(h2); s_rc.append(rc2); s_sc.append(sc2); s_fc.append(fc2)

    y_prompt = rmsnorm(xp, g_final)[:, N_META:]
    y_sample = rmsnorm(xs, g_final)
    return (y_prompt, y_sample,
            jnp.stack(p_h), jnp.stack(p_rc), jnp.stack(p_sc), jnp.stack(p_fc),
            jnp.stack(s_h), jnp.stack(s_rc), jnp.stack(s_sc), jnp.stack(s_fc))
```

```python
import numpy as np
from contextlib import ExitStack
import concourse.bass as bass
import concourse.mybir as mybir
from concourse.ap import AP
from concourse.bass_utils import run_bass_kernel_spmd

F32 = mybir.dt.float32
F32R = mybir.dt.float32r
AF = mybir.ActivationFunctionType
ALU = mybir.AluOpType

NCORES = 8
P = 128
D = 2048
KD = 16
DA = 1536
NH = 12
DB = 1024
NG = 8
DFF = 6144
NF = 48
NMIX = 20
DIN = 2 * DA + 3 * DB
HALO = 8
NPRE = 1024
NMAIN = 1040
PM = 520
SQ = 8
NS = 64
NC = 584
NT = 292
WBW = 616
YG = 4
EPS = 1e-6

C_GMIX, C_GFFN, C_GFIN = 0, 16, 32
C_CAB, C_BGA, C_BGX, C_LAM, C_GOA, C_GOB, C_CFB = 48, 60, 72, 84, 96, 108, 116
C_CAW, C_CBW, C_CFW, C_FLAG = 164, 212, 236, 380
NCPAR = 384
DC_HBA, DC_HBX, DC_C, DC_CH, DC_EPS, DC_ONE = 0, 12, 24, 36, 48, 49
NDC = 64


class Res:
    __slots__ = ("name", "w", "r")

    def __init__(self, name):
        self.name = name
        self.w = None
        self.r = []


class Op:
    __slots__ = ("eng", "fn", "deps", "lane", "lane_idx", "sig", "tick", "is_dma")


class Sched:
    ENGS = ("pe", "act", "dve", "pool", "sp")

    def __init__(self):
        self.streams = {e: [] for e in self.ENGS}
        self.lanes = {}
        self.resd = {}

    def R(self, *key):
        r = self.resd.get(key)
        if r is None:
            r = Res(key)
            self.resd[key] = r
        return r

    def _rec(self, op, reads, writes):
        deps = {}
        for r in reads:
            if r.w is not None:
                deps[id(r.w)] = (r.w, True)
        for w in writes:
            if w.w is not None and id(w.w) not in deps:
                deps[id(w.w)] = (w.w, False)
            for rd in w.r:
                if id(rd) not in deps:
                    deps[id(rd)] = (rd, False)
        fin = []
        for p, raw in deps.values():
            if p is op:
                continue
            if (not p.is_dma) and (not op.is_dma) and p.eng == op.eng and not raw:
                continue
            fin.append(p)
            if not p.is_dma:
                p.sig = True
        op.deps = fin
        for r in reads:
            r.r.append(op)
        for w in writes:
            w.w = op
            w.r = []
        self.streams[op.eng].append(op)

    def op(self, eng, fn, reads=(), writes=()):
        if any(r.name[0] == "PS" for r in reads):
            writes = list(writes) + [r for r in reads if r.name[0] == "PS"]
            reads = [r for r in reads if r.name[0] != "PS"]
        o = Op()
        o.eng = eng
        o.fn = fn
        o.is_dma = False
        o.sig = False
        o.tick = None
        o.lane = None
        o.lane_idx = None
        self._rec(o, reads, writes)
        return o

    def dma(self, queue, fn, reads=(), writes=(), lane=None, bulk=False):
        o = Op()
        o.eng = queue
        o.fn = fn
        o.is_dma = True
        o.sig = True
        o.tick = None
        ln = self.lanes.setdefault(lane, [0, bulk])
        o.lane = lane
        o.lane_idx = ln[0]
        ln[0] += 1
        self._rec(o, reads, writes)
        return o

    def emit(self, nc, es):
        for e in self.ENGS:
            t = 0
            for o in self.streams[e]:
                if not o.is_dma and o.sig:
                    t += 1
                    o.tick = t
        esem = {e: es.enter_context(nc.semaphore("sem_" + e)) for e in self.ENGS if e != "sp"}
        lsem = {ln: es.enter_context(nc.semaphore("lane_" + str(ln))) for ln in self.lanes}
        store_lanes = {}
        for e in self.ENGS:
            for o in self.streams[e]:
                if o.is_dma:
                    store_lanes.setdefault(e, set()).add(o.lane)

        def run_stream(ename, eng):
            waited = {}
            for o in self.streams[ename]:
                need = {}
                for p in o.deps:
                    if p.is_dma:
                        cnt, bulk = self.lanes[p.lane]
                        val = 16 * (cnt if bulk else (p.lane_idx + 1))
                        key = ("l", p.lane)
                        sem = lsem[p.lane]
                    else:
                        val = p.tick
                        key = ("e", p.eng)
                        sem = esem[p.eng]
                    if need.get(key, (None, 0))[1] < val:
                        need[key] = (sem, val)
                for key, (sem, val) in need.items():
                    if waited.get(key, 0) >= val:
                        continue
                    eng.wait_ge(sem, val)
                    waited[key] = val
                ins = o.fn(eng)
                if o.is_dma:
                    ins.then_inc(lsem[o.lane], 16)
                elif o.sig:
                    ins.then_inc(esem[ename], 1)
            for ln in sorted(store_lanes.get(ename, ()), key=str):
                val = 16 * self.lanes[ln][0]
                if waited.get(("l", ln), 0) < val:
                    eng.wait_ge(lsem[ln], val)

        block = es.enter_context(nc.Block())

        @block.sync
        def _(e):
            run_stream("sp", e)

        @block.tensor
        def _(e):
            run_stream("pe", e)

        @block.scalar
        def _(e):
            run_stream("act", e)

        @block.vector
        def _(e):
            run_stream("dve", e)

        @block.gpsimd
        def _(e):
            run_stream("pool", e)


def build_program(stop_after=None, dbg=False):
    nc = bass.Bass("TRN2", target_bir_lowering=False)
    nc.dge_precook = False
    S = Sched()
    R = S.R
    es = ExitStack()
    import os as _os
    STQ = _os.environ.get("KSTQ", "pool")

    def din(name, shape, dt=F32):
        return nc.dram_tensor(name, shape, dt, kind="ExternalInput").ap()

    def dout(name, shape, dt=F32):
        return nc.dram_tensor(name, shape, dt, kind="ExternalOutput").ap()

    xw = din("xw", [NPRE + NMAIN, D])
    xs = din("xs", [128, D])
    st_hr = din("st_hr", [64, DA])
    st_sc = din("st_sc", [32, DB])
    st_fc = din("st_fc", [32, DFF])
    cpar = din("cpar", [P, NCPAR])
    identd = din("ident", [P, P])
    wA = din("wA", [48, P, 2048], F32R)
    wG = din("wG", [NH, P, 256], F32R)
    wO = din("wO", [20, P, 2048], F32R)
    wU = din("wU", [96, P, 2048], F32R)
    wD = din("wD", [48, P, 2048], F32R)
    y_p = dout("y_p", [NMAIN, D])
    y_s = dout("y_s", [128, D])
    o_sh = dout("o_sh", [16, DA])
    o_src = dout("o_src", [48, DA])
    o_ssc = dout("o_ssc", [32, DB])
    o_sfc = dout("o_sfc", [32, DFF])
    o_ph = dout("o_ph", [NH, P])
    o_prc = dout("o_prc", [NH * 3, P])
    o_psc = dout("o_psc", [NG * 2, P])
    o_pfc = dout("o_pfc", [NF * 2, P])

    def sb(name, shape, dt=F32):
        return es.enter_context(nc.sbuf_tensor(name, shape, dt))

    X = sb("X", [P, KD, NC])
    N = sb("N", [P, KD, NC], F32R)
    YH = sb("YH", [P, 2 * YG, NC], F32R)
    NWS = 4
    Wsl = [sb(f"W{i}", [P, 2048], F32R) for i in range(NWS)]
    NGS = 4
    GWsl = [sb(f"GW{i}", [P, 256], F32R) for i in range(NGS)]
    NIO = 2
    IO = [sb(f"IO{i}", [P, 2048]) for i in range(NIO)]
    XP = [sb(f"XP{i}", [P, WBW]) for i in range(2)]
    XC = [sb(f"XC{i}", [P, WBW]) for i in range(3)]
    GG = [sb(f"GG{i}", [P, WBW]) for i in range(4)]
    SB_ = [sb(f"SB{i}", [P, WBW]) for i in range(2)]
    AB = [sb(f"AB{i}", [P, WBW]) for i in range(2)]
    UB = [sb(f"UB{i}", [P, WBW]) for i in range(2)]
    SQB = [sb(f"SQB{i}", [P, NC], F32R) for i in range(2)]
    XCR = [sb(f"XCR{i}", [P, NC], F32R) for i in range(3)]
    RB = sb("RB", [P, NC])
    CP = sb("CP", [P, NCPAR])
    DC = sb("DC", [P, NDC])
    IDT = sb("IDT", [P, P])
    ONES = sb("ONES", [P, P], F32R)
    ST_h = sb("ST_h", [P, NH, 16])
    ST_rc = sb("ST_rc", [P, NH, 48])
    ST_sc = sb("ST_sc", [P, NG, 32])
    ST_fc = sb("ST_fc", [P, NF, 32])
    PS_h = sb("PS_h", [P, NH])
    PS_rc = sb("PS_rc", [P, NH, 3])
    PS_sc = sb("PS_sc", [P, NG, 2])
    PS_fc = sb("PS_fc", [P, NF, 2])
    NPS = 4
    PSM = [es.enter_context(nc.psum_tensor(f"PSM{i}", [P, 2, 512], F32)) for i in range(NPS)]

    def pst(t):
        return t[:].ap[0][0]

    def V(t, off, *dims, dt=None):
        a = AP(t, off, [[pst(t), P]] + [list(d) for d in dims])
        if dt is not None:
            a = a.bitcast(dt)
        return a

    def VP(t, off, npart, *dims):
        return AP(t, off, [[pst(t), npart]] + [list(d) for d in dims])

    cnt = {"w": 0, "g": 0, "io": 0, "ps": 0, "sq": 0}

    def nxt(k, n):
        i = cnt[k] % n
        cnt[k] += 1
        return i

    def load_w(src_ap, ncols):
        s = nxt("w", NWS)
        S.dma("sp", lambda e, s=s: e.dma_start(out=Wsl[s][:, 0:ncols], in_=src_ap),
              writes=[R("W", s)], lane=("w", s))
        return s

    def next_ps():
        return nxt("ps", NPS)

    cp = lambda c0, n=1: CP[:, c0:c0 + n]
    dc = lambda c0, n=1: DC[:, c0:c0 + n]
    rCP, rDC, rIDT, rONES = R("CP"), R("DC"), R("IDT"), R("ONES")

    S.dma("sp", lambda e: e.dma_start(out=CP[:], in_=cpar), writes=[rCP], lane="const", bulk=True)
    S.dma("sp", lambda e: e.dma_start(out=IDT[:], in_=identd), writes=[rIDT], lane="const", bulk=True)
    S.op("dve", lambda e: e.memset(RB[:, 0:P], 1.0), writes=[R("RB")])
    S.op("dve", lambda e: e.tensor_copy(out=ONES[:], in_=RB[:, 0:P]), reads=[R("RB")], writes=[rONES])
    S.op("dve", lambda e: e.memset(DC[:, DC_EPS:DC_EPS + 1], EPS), writes=[R("DCe")])
    S.op("dve", lambda e: e.memset(DC[:, DC_ONE:DC_ONE + 1], 1.0), writes=[R("DCo")])
    S.op("dve", lambda e: e.memset(PS_h[:], 0.0), writes=[R("PS_h", n) for n in range(NH)])
    S.op("dve", lambda e: e.memset(PS_rc[:], 0.0), writes=[R("PS_rc", n) for n in range(NH)])
    S.op("dve", lambda e: e.memset(PS_sc[:], 0.0), writes=[R("PS_sc", g) for g in range(NG)])
    S.op("dve", lambda e: e.memset(PS_fc[:], 0.0), writes=[R("PS_fc", j) for j in range(NF)])
    S.op("dve", lambda e: e.tensor_scalar(out=dc(DC_HBA, 24), in0=cp(C_BGA, 24), scalar1=0.5, scalar2=None,
                                          op0=ALU.mult), reads=[rCP], writes=[R("DChb")])
    if "c_act" not in _os.environ.get("KSKIP", "").split(","):
        S.op("act", lambda e: e.activation(out=dc(DC_C, 12), in_=cp(C_LAM, 12), func=AF.Exp, scale=-1.0),
             reads=[rCP], writes=[R("DCc")])
        S.op("act", lambda e: e.activation(out=dc(DC_C, 12), in_=dc(DC_C, 12), func=AF.Ln, bias=dc(DC_ONE), scale=1.0),
             reads=[R("DCc"), R("DCo")], writes=[R("DCc")])
    S.op("dve", lambda e: e.tensor_scalar(out=dc(DC_CH, 12), in0=dc(DC_C, 12), scalar1=-4.0, scalar2=None,
                                          op0=ALU.mult), reads=[R("DCc")], writes=[R("DCch")])
    S.op("dve", lambda e: e.tensor_scalar(out=dc(DC_C, 12), in0=dc(DC_C, 12), scalar1=-8.0, scalar2=None,
                                          op0=ALU.mult), reads=[R("DCc"), R("DCch")], writes=[R("DCc")])
    rCONST = [rCP, R("DChb"), R("DCc"), R("DCch"), R("DCe"), R("DCo")]

    flip = [0]

    def evac_eng():
        flip[0] ^= 1
        return "act" if flip[0] else "dve"

    def copy_op(eng, out, in_, reads, writes):
        if eng == "act":
            S.op("act", lambda e: e.activation(out=out, in_=in_, func=AF.Copy), reads=reads, writes=writes)
        else:
            S.op(eng, lambda e: e.tensor_copy(out=out, in_=in_), reads=reads, writes=writes)

    def load_states():
        KS = _os.environ.get("KSKIP", "").split(",")
        if "hr" in KS:
            return
        s = nxt("io", NIO)
        S.dma("sp", lambda e: e.dma_start(out=IO[s][0:64, 0:DA], in_=st_hr), writes=[R("IO", s)], lane=("io", s))
        ps = next_ps()
        for n in range(NH):
            S.op("pe", lambda e, n=n: e.transpose(out=V(PSM[ps], n * 64, [1, 64]), in_=IO[s][0:64, n * P:(n + 1) * P],
                                                  identity=IDT[0:64, 0:64]),
                 reads=[R("IO", s), rIDT], writes=[R("PS", ps)])
        copy_op("act", ST_h[:], V(PSM[ps], 0, [64, NH], [1, 16]), [R("PS", ps)], [R("ST_h", n) for n in range(NH)])
        if "rc" not in KS:
            copy_op("dve", ST_rc[:], V(PSM[ps], 16, [64, NH], [1, 48]), [R("PS", ps)], [R("ST_rc", n) for n in range(NH)])
        if "sc" in KS:
            return
        s2 = nxt("io", NIO)
        S.dma("sp", lambda e: e.dma_start(out=IO[s2][0:32, 0:DB], in_=st_sc), writes=[R("IO", s2)], lane=("io", s2))
        ps2 = next_ps()
        for g in range(NG):
            S.op("pe", lambda e, g=g: e.transpose(out=V(PSM[ps2], g * 64, [1, 64]), in_=IO[s2][0:64, g * P:(g + 1) * P],
                                                  identity=IDT[0:64, 0:64]),
                 reads=[R("IO", s2), rIDT], writes=[R("PS", ps2)])
        copy_op("act", ST_sc[:], V(PSM[ps2], 0, [64, NG], [1, 32]), [R("PS", ps2)], [R("ST_sc", g) for g in range(NG)])
        if "fc" in KS:
            return
        for q in range(3):
            s3 = nxt("io", NIO)
            S.dma("sp", lambda e, q=q, s3=s3: e.dma_start(out=IO[s3][0:32, :], in_=st_fc[:, q * 2048:(q + 1) * 2048]),
                  writes=[R("IO", s3)], lane=("io", s3))
            for hh in range(2):
                ps3 = next_ps()
                for i in range(8):
                    ii = hh * 8 + i
                    S.op("pe", lambda e, i=i, ii=ii, s3=s3, ps3=ps3: e.transpose(out=V(PSM[ps3], i * 64, [1, 64]),
                                                                                 in_=IO[s3][0:64, ii * P:(ii + 1) * P],
                                                                                 identity=IDT[0:64, 0:64]),
                         reads=[R("IO", s3), rIDT], writes=[R("PS", ps3)])
                j0 = q * 16 + hh * 8
                copy_op(evac_eng(), ST_fc[:, j0:j0 + 8, :], V(PSM[ps3], 0, [64, 8], [1, 32]), [R("PS", ps3)],
                        [R("ST_fc", j) for j in range(j0, j0 + 8)])

    def load_x_tiles(tiles):
        for (col0, nr, parts) in tiles:
            s = nxt("io", NIO)
            for (r0, n_, src) in parts:
                S.dma("sp", lambda e, s=s, r0=r0, n_=n_, src=src: e.dma_start(out=IO[s][r0:r0 + n_, :], in_=src),
                      writes=[R("IO", s)], lane=("io", s))
            for kh in range(2):
                ps = next_ps()
                for i in range(8):
                    k = kh * 8 + i
                    S.op("pe", lambda e, s=s, k=k, i=i, ps=ps, nr=nr: e.transpose(
                        out=V(PSM[ps], i * P, [1, nr]), in_=IO[s][0:nr, k * P:(k + 1) * P], identity=IDT[0:nr, 0:nr]),
                        reads=[R("IO", s), rIDT], writes=[R("PS", ps)])
                copy_op(evac_eng(), X[:, kh * 8:kh * 8 + 8, col0:col0 + nr], V(PSM[ps], 0, [P, 8], [1, nr]),
                        [R("PS", ps)], [R("X", k) for k in range(kh * 8, kh * 8 + 8)])

    def rmsnorm_fm(ncols, nt, gcol, rounded=True):
        nh = ncols // nt
        ps = next_ps()
        for k in range(KD):
            q = nxt("sq", 2)
            S.op("act", lambda e, k=k, q=q: e.activation(out=SQB[q][:, 0:ncols], in_=X[:, k, 0:ncols], func=AF.Square),
                 reads=[R("X", k)], writes=[R("SQ", q)])
            for h in range(nh):
                S.op("pe", lambda e, k=k, q=q, h=h: e.matmul(PSM[ps][:, h, 0:nt], ONES[:], SQB[q][:, h * nt:(h + 1) * nt],
                                                           start=(k == 0), stop=(k == KD - 1)),
                     reads=[R("SQ", q), rONES], writes=[R("PS", ps)])
        S.op("act", lambda e: e.activation(out=V(RB, 0, [nt, nh], [1, nt]), in_=PSM[ps][:, 0:nh, 0:nt], func=AF.Ln,
                                           bias=dc(DC_EPS), scale=1.0 / D),
             reads=[R("PS", ps), R("DCe")], writes=[R("RB")])
        S.op("act", lambda e: e.activation(out=RB[:, 0:ncols], in_=RB[:, 0:ncols], func=AF.Exp, scale=-0.5),
             reads=[R("RB")], writes=[R("RB")])
        for k in range(KD):
            o = N[:, k, 0:ncols] if rounded else X[:, k, 0:ncols]
            S.op("dve", lambda e, k=k, o=o: e.scalar_tensor_tensor(out=o, in0=X[:, k, 0:ncols], scalar=cp(gcol + k),
                                                                 in1=RB[:, 0:ncols], op0=ALU.mult, op1=ALU.mult),
                 reads=[R("X", k), R("RB"), rCP], writes=[R("N", k) if rounded else R("X", k)])

    def mm_group(ps, wslot, wcol0, wkstride, nk, src_fn, src_res, nt, nhalf=2):
        for k in range(nk):
            for h in range(nhalf):
                S.op("pe", lambda e, k=k, h=h: e.matmul(PSM[ps][:, h, 0:nt],
                                                       Wsl[wslot][:, k * wkstride + wcol0:k * wkstride + wcol0 + P],
                                                       src_fn(k, h), start=(k == 0), stop=(k == nk - 1)),
                     reads=[R("W", wslot), src_res(k)], writes=[R("PS", ps)])

    class Geom:
        pass

    ucnt = [0]

    def bufs():
        u = ucnt[0]
        ucnt[0] += 1
        b = Geom()
        i2, i3, i4 = u % 2, u % 3, u % 4
        b.xp, b.rXP, b.rXPt = XP[i2], R("XP", i2), R("XPt", i2)
        b.xcr, b.rXCR = XCR[i3], R("XCR", i3)
        b.sb, b.rS = SB_[i2], R("SBf", i2)
        b.ab, b.rA = AB[i2], R("AB", i2)
        b.ub, b.rU = UB[i2], R("UB", i2)
        b.xc, b.rXC, b.rXCs = XC[i3], R("XC", i3), R("XCs", i3)
        b.gg, b.rGG = GG[i4], R("GG", i4)
        return b

    def norm_phase(g, yb, yres, b, gcol, out_slot):
        nt, ncols, nh = g.nt, g.ncols, g.nh
        q = nxt("sq", 2)
        sqb = SQB[q]
        S.op("act", lambda e: e.activation(out=sqb[:, 0:ncols], in_=yb[:, 0:ncols], func=AF.Square),
             reads=yres, writes=[R("SQ", q)])
        ps = next_ps()
        for h in range(nh):
            S.op("pe", lambda e, h=h: e.matmul(PSM[ps][:, h, 0:nt], ONES[:], sqb[:, h * nt:(h + 1) * nt],
                                               start=True, stop=True),
                 reads=[R("SQ", q), rONES], writes=[R("PS", ps)])
        S.op("act", lambda e: e.activation(out=V(b.ab, 0, [nt, nh], [1, nt]), in_=PSM[ps][:, 0:nh, 0:nt], func=AF.Ln,
                                           bias=dc(DC_EPS), scale=1.0 / P),
             reads=[R("PS", ps), R("DCe")], writes=[b.rA])
        S.op("act", lambda e: e.activation(out=b.ab[:, 0:ncols], in_=b.ab[:, 0:ncols], func=AF.Exp, scale=-0.5),
             reads=[b.rA], writes=[b.rA])
        S.op("dve", lambda e: e.scalar_tensor_tensor(out=YH[:, out_slot, 0:ncols], in0=yb[:, 0:ncols], scalar=cp(gcol),
                                                     in1=b.ab[:, 0:ncols], op0=ALU.mult, op1=ALU.mult),
             reads=yres + [b.rA, rCP], writes=[R("YH", out_slot)])

    def lru_unit(n, g, out_slot):
        b = bufs()
        npc, nt, ncols, nh = g.npc, g.nt, g.ncols, g.nh
        xp, xc, xcr, gg, sbuf_, ab, ub = b.xp, b.xc, b.xcr, b.gg, b.sb, b.ab, b.ub
        rXP, rXPt, rXC, rXCs, rXCR, rGG, rS, rA, rU = b.rXP, b.rXPt, b.rXC, b.rXCs, b.rXCR, b.rGG, b.rS, b.rA, b.rU
        nsx = npc + 3
        st = Geom()
        srcN = lambda k, h: N[:, k, h * nt:(h + 1) * nt]
        resN = lambda k: R("N", k)
        cw = lambda k: cp(C_CAW + n * 4 + k)
        vh = lambda t: V(t, 0, [nt, nh], [1, nt])

        def phA():
            ws_xa = load_w(wA[n], 2048)
            ps_xa = next_ps()
            mm_group(ps_xa, ws_xa, 0, P, KD, srcN, resN, nt, nh)
            if g.main:
                ws_ga = load_w(wA[NH + n], 2048)
                ps_ga = next_ps()
                mm_group(ps_ga, ws_ga, 0, P, KD, srcN, resN, nt, nh)
            st.gs = nxt("g", NGS)
            gs = st.gs
            S.dma("sp", lambda e: e.dma_start(out=GWsl[gs][:], in_=wG[n]), writes=[R("GW", gs)], lane=("g", gs))
            S.op("pool", lambda e: e.tensor_copy(out=xp[:, 0:3], in_=PS_rc[:, n, :]), reads=[R("PS_rc", n)], writes=[rXPt])
            if g.main:
                S.op("pool", lambda e: e.tensor_copy(out=V(xp, nsx, [11, SQ], [1, 3]),
                                                     in_=V(ST_rc, n * 48 + g.p * 24, [3, SQ], [1, 3])),
                     reads=[R("ST_rc", n)], writes=[rXPt])
                S.op("dve", lambda e: e.tensor_copy(out=xp[:, 3:3 + nt], in_=PSM[ps_xa][:, 0, 0:nt]),
                     reads=[R("PS", ps_xa)], writes=[rXP])
                S.op("dve", lambda e: e.tensor_copy(out=xp[:, 3 + nt:3 + npc], in_=PSM[ps_xa][:, 1, 0:npc - nt]),
                     reads=[R("PS", ps_xa)], writes=[rXP])
                S.op("dve", lambda e: e.tensor_copy(out=V(xp, nsx + 3, [11, SQ], [1, 8]),
                                                    in_=V(PSM[ps_xa], 512 + npc - nt, [8, SQ], [1, 8])),
                     reads=[R("PS", ps_xa)], writes=[rXP])
            else:
                S.op("dve", lambda e: e.tensor_copy(out=V(xp, 3, [nt, nh], [1, nt]), in_=PSM[ps_xa][:, 0:nh, 0:nt]),
                     reads=[R("PS", ps_xa)], writes=[rXP])
            S.op("pool", lambda e: e.tensor_copy(out=PS_rc[:, n, :], in_=xp[:, npc:npc + 3]),
                 reads=[rXP, rXPt], writes=[R("PS_rc", n)])
            if g.main:
                S.op("pool", lambda e: e.tensor_copy(out=V(ST_rc, n * 48 + g.p * 24, [3, SQ], [1, 3]),
                                                     in_=V(xp, nsx + 8, [11, SQ], [1, 3])),
                     reads=[rXP, rXPt], writes=[R("ST_rc", n)])
            S.op("act", lambda e: e.activation(out=xc[:, 0:npc], in_=xp[:, 0:npc], func=AF.Identity, bias=cp(C_CAB + n),
                                               scale=cw(0)),
                 reads=[rXP, rXPt, rCP], writes=[rXC])
            for k in range(1, 4):
                o = xcr[:, 0:npc] if k == 3 else xc[:, 0:npc]
                S.op("dve", lambda e, k=k, o=o: e.scalar_tensor_tensor(out=o, in0=xp[:, k:k + npc], scalar=cw(k),
                                                                     in1=xc[:, 0:npc], op0=ALU.mult, op1=ALU.add),
                     reads=[rXP, rXPt, rXC, rCP], writes=[rXCR if k == 3 else rXC])
            if g.main:
                S.op("act", lambda e: e.activation(out=V(xc, npc, [8, SQ], [1, 8]), in_=V(xp, nsx, [11, SQ], [1, 8]),
                                                   func=AF.Identity, bias=cp(C_CAB + n), scale=cw(0)),
                     reads=[rXP, rXPt, rCP], writes=[rXCs])
                for k in range(1, 4):
                    S.op("dve", lambda e, k=k: e.scalar_tensor_tensor(
                        out=V(xcr if k == 3 else xc, npc, [8, SQ], [1, 8]), in0=V(xp, nsx + k, [11, SQ], [1, 8]),
                        scalar=cw(k), in1=V(xc, npc, [8, SQ], [1, 8]), op0=ALU.mult, op1=ALU.add),
                        reads=[rXP, rXPt, rXCs, rCP], writes=[rXCR if k == 3 else rXCs])
                S.op("act", lambda e: e.activation(out=vh(gg), in_=PSM[ps_ga][:, 0:nh, 0:nt], func=AF.Gelu_apprx_tanh),
                     reads=[R("PS", ps_ga)], writes=[rGG])

        def phB():
            gs = st.gs
            ps_r = next_ps()
            ps_i = next_ps()
            for (psx, c0) in ((ps_r, 0), (ps_i, P)):
                for h in range(nh):
                    S.op("pe", lambda e, psx=psx, c0=c0, h=h: e.matmul(PSM[psx][:, h, 0:nt], GWsl[gs][:, c0:c0 + P],
                                                                     xcr[:, h * nt:(h + 1) * nt], start=True, stop=True),
                         reads=[R("GW", gs), rXCR], writes=[R("PS", psx)])
            S.op("act", lambda e: e.activation(out=vh(sbuf_), in_=PSM[ps_r][:, 0:nh, 0:nt], func=AF.Tanh,
                                               bias=dc(DC_HBA + n), scale=0.5),
                 reads=[R("PS", ps_r), R("DChb")], writes=[rS])
            S.op("act", lambda e: e.activation(out=vh(ub), in_=PSM[ps_i][:, 0:nh, 0:nt], func=AF.Tanh,
                                               bias=dc(DC_HBX + n), scale=0.5),
                 reads=[R("PS", ps_i), R("DChb")], writes=[rU])
            S.op("act", lambda e: e.activation(out=ab[:, 0:ncols], in_=sbuf_[:, 0:ncols], func=AF.Exp,
                                               bias=dc(DC_CH + n), scale=dc(DC_CH + n)),
                 reads=[rS, R("DCch")], writes=[rA])
            S.op("act", lambda e: e.activation(out=sbuf_[:, 0:ncols], in_=sbuf_[:, 0:ncols], func=AF.Exp,
                                               bias=dc(DC_C + n), scale=dc(DC_C + n)),
                 reads=[rS, R("DCc")], writes=[rS])
            S.op("act", lambda e: e.activation(out=sbuf_[:, 0:ncols], in_=sbuf_[:, 0:ncols], func=AF.Ln,
                                               bias=dc(DC_ONE), scale=-1.0),
                 reads=[rS, R("DCo")], writes=[rS])
            S.op("act", lambda e: e.activation(out=sbuf_[:, 0:ncols], in_=sbuf_[:, 0:ncols], func=AF.Exp, scale=0.5),
                 reads=[rS], writes=[rS])
            S.op("dve", lambda e: e.scalar_tensor_tensor(out=ub[:, 0:ncols], in0=ub[:, 0:ncols], scalar=1.0,
                                                         in1=xcr[:, 0:ncols], op0=ALU.add, op1=ALU.mult),
                 reads=[rU, rXCR], writes=[rU])
            S.op("dve", lambda e: e.scalar_tensor_tensor(out=ub[:, 0:ncols], in0=ub[:, 0:ncols], scalar=0.5,
                                                         in1=sbuf_[:, 0:ncols], op0=ALU.mult, op1=ALU.mult),
                 reads=[rU, rS], writes=[rU])
            if g.main and g.p == 0:
                S.op("dve", lambda e: e.tensor_scalar(out=ub[:, 0:HALO], in0=ub[:, 0:HALO], scalar1=cp(C_FLAG),
                                                      scalar2=None, op0=ALU.mult),
                     reads=[rU, rCP], writes=[rU])
            S.op("dve", lambda e: e.tensor_tensor_scan(out=xc[:, 0:npc], data0=ab[:, 0:npc], data1=ub[:, 0:npc],
                                                       initial=PS_h[:, n:n + 1], op0=ALU.mult, op1=ALU.add),
                 reads=[rA, rU, R("PS_h", n)], writes=[rXC])
            S.op("dve", lambda e: e.tensor_copy(out=PS_h[:, n:n + 1], in_=xc[:, npc - 1:npc]),
                 reads=[rXC], writes=[R("PS_h", n)])
            if g.main:
                for j in range(SQ):
                    c0 = npc + 8 * j
                    S.op("dve", lambda e, j=j, c0=c0: e.tensor_tensor_scan(
                        out=xc[:, c0:c0 + 8], data0=ab[:, c0:c0 + 8], data1=ub[:, c0:c0 + 8],
                        initial=ST_h[:, n, g.p * SQ + j:g.p * SQ + j + 1], op0=ALU.mult, op1=ALU.add),
                        reads=[rA, rU, R("ST_h", n)], writes=[rXCs])
                S.op("dve", lambda e: e.tensor_copy(out=ST_h[:, n, g.p * SQ:(g.p + 1) * SQ], in_=V(xc, npc + 7, [8, SQ])),
                     reads=[rXCs], writes=[R("ST_h", n)])
                S.op("dve", lambda e: e.tensor_tensor(out=gg[:, 0:ncols], in0=gg[:, 0:ncols], in1=xc[:, 0:ncols],
                                                      op=ALU.mult),
                     reads=[rGG, rXC, rXCs], writes=[rGG])

        def phC():
            norm_phase(g, gg, [rGG], b, C_GOA + n, out_slot)

        return [phA, phB, phC] if g.main else [phA, phB]

    def tails2(xp, rXPt, PSt, rPSt, STt, rSTt, idx, p, npc):
        nsx = npc + 2
        S.op("pool", lambda e: e.tensor_copy(out=xp[:, 0:2], in_=PSt[:, idx, :]), reads=[rPSt], writes=[rXPt])
        S.op("pool", lambda e: e.tensor_copy(out=V(xp, nsx, [10, SQ], [1, 2]),
                                             in_=V(STt, idx * 32 + p * 16, [2, SQ], [1, 2])),
             reads=[rSTt], writes=[rXPt])

    def tails2_save(xp, rXPb, rXPt, PSt, rPSt, STt, rSTt, idx, p, npc):
        nsx = npc + 2
        S.op("pool", lambda e: e.tensor_copy(out=PSt[:, idx, :], in_=xp[:, npc:npc + 2]),
             reads=[rXPb, rXPt], writes=[rPSt])
        S.op("pool", lambda e: e.tensor_copy(out=V(STt, idx * 32 + p * 16, [2, SQ], [1, 2]),
                                             in_=V(xp, nsx + 8, [10, SQ], [1, 2])),
             reads=[rXPb, rXPt], writes=[rSTt])

    def conv3(xp, xc, rXPb, rXPt, rXCb, npc, cwcol, bias_col):
        nsx = npc + 2
        cw = lambda k: cp(cwcol + k)
        if bias_col is None:
            S.op("act", lambda e: e.activation(out=xc[:, 0:npc], in_=xp[:, 0:npc], func=AF.Copy, scale=cw(0)),
                 reads=[rXPb, rXPt, rCP], writes=[rXCb])
            S.op("act", lambda e: e.activation(out=V(xc, npc, [8, SQ], [1, 8]), in_=V(xp, nsx, [10, SQ], [1, 8]),
                                               func=AF.Copy, scale=cw(0)),
                 reads=[rXPb, rXPt, rCP], writes=[rXCb])
        else:
            S.op("act", lambda e: e.activation(out=xc[:, 0:npc], in_=xp[:, 0:npc], func=AF.Identity, bias=cp(bias_col),
                                               scale=cw(0)),
                 reads=[rXPb, rXPt, rCP], writes=[rXCb])
            S.op("act", lambda e: e.activation(out=V(xc, npc, [8, SQ], [1, 8]), in_=V(xp, nsx, [10, SQ], [1, 8]),
                                               func=AF.Identity, bias=cp(bias_col), scale=cw(0)),
                 reads=[rXPb, rXPt, rCP], writes=[rXCb])
        for k in range(1, 3):
            S.op("dve", lambda e, k=k: e.scalar_tensor_tensor(out=xc[:, 0:npc], in0=xp[:, k:k + npc], scalar=cw(k),
                                                             in1=xc[:, 0:npc], op0=ALU.mult, op1=ALU.add),
                 reads=[rXPb, rXPt, rXCb, rCP], writes=[rXCb])
            S.op("dve", lambda e, k=k: e.scalar_tensor_tensor(out=V(xc, npc, [8, SQ], [1, 8]),
                                                             in0=V(xp, nsx + k, [10, SQ], [1, 8]), scalar=cw(k),
                                                             in1=V(xc, npc, [8, SQ], [1, 8]), op0=ALU.mult, op1=ALU.add),
                 reads=[rXPb, rXPt, rXCb, rCP], writes=[rXCb])

    def sconv_unit(gi, g, out_slot):
        b = bufs()
        npc, nt, ncols, nh = g.npc, g.nt, g.ncols, g.nh
        xp, xc, gg, ub = b.xp, b.xc, b.gg, b.ub
        rXP, rXPt, rXC, rGG, rU = b.rXP, b.rXPt, b.rXC, b.rGG, b.rU
        srcN = lambda k, h: N[:, k, h * nt:(h + 1) * nt]
        resN = lambda k: R("N", k)
        nsx = npc + 2
        vh = lambda t: V(t, 0, [nt, nh], [1, nt])

        def phA():
            ws_vb = load_w(wA[40 + gi], 2048)
            ps_vb = next_ps()
            mm_group(ps_vb, ws_vb, 0, P, KD, srcN, resN, nt)
            S.op("act", lambda e: e.activation(out=vh(gg), in_=PSM[ps_vb][:, :, 0:nt], func=AF.Copy),
                 reads=[R("PS", ps_vb)], writes=[rGG])
            ws_gc = load_w(wA[32 + gi], 2048)
            ps_gc = next_ps()
            mm_group(ps_gc, ws_gc, 0, P, KD, srcN, resN, nt)
            tails2(xp, rXPt, PS_sc, R("PS_sc", gi), ST_sc, R("ST_sc", gi), gi, g.p, npc)
            S.op("dve", lambda e: e.tensor_tensor(out=xp[:, 2:2 + nt], in0=PSM[ps_gc][:, 0, 0:nt], in1=gg[:, 0:nt],
                                                  op=ALU.mult),
                 reads=[R("PS", ps_gc), rGG], writes=[rXP])
            S.op("dve", lambda e: e.tensor_tensor(out=xp[:, 2 + nt:2 + npc], in0=PSM[ps_gc][:, 1, 0:npc - nt],
                                                  in1=gg[:, nt:npc], op=ALU.mult),
                 reads=[R("PS", ps_gc), rGG], writes=[rXP])
            S.op("dve", lambda e: e.tensor_tensor(out=V(xp, nsx + 2, [10, SQ], [1, 8]),
                                                  in0=V(PSM[ps_gc], 512 + npc - nt, [8, SQ], [1, 8]),
                                                  in1=V(gg, npc, [8, SQ], [1, 8]), op=ALU.mult),
                 reads=[R("PS", ps_gc), rGG], writes=[rXP])
            tails2_save(xp, rXP, rXPt, PS_sc, R("PS_sc", gi), ST_sc, R("ST_sc", gi), gi, g.p, npc)
            ws_gb = load_w(wA[24 + gi], 2048)
            ps_gb = next_ps()
            mm_group(ps_gb, ws_gb, 0, P, KD, srcN, resN, nt)
            S.op("act", lambda e: e.activation(out=vh(ub), in_=PSM[ps_gb][:, :, 0:nt], func=AF.Copy),
                 reads=[R("PS", ps_gb)], writes=[rU])
            conv3(xp, xc, rXP, rXPt, rXC, npc, C_CBW + gi * 3, None)
            S.op("dve", lambda e: e.tensor_tensor(out=gg[:, 0:ncols], in0=xc[:, 0:ncols], in1=ub[:, 0:ncols], op=ALU.mult),
                 reads=[rXC, rU, rGG], writes=[rGG])

        def phB():
            pass

        def phC():
            norm_phase(g, gg, [rGG], b, C_GOB + gi, out_slot)

        return [phA, phB, phC]

    def wo_partial(grp, nk, g):
        nt = g.nt
        half = (grp % 2) * YG
        for dd in range(4):
            ws = load_w(wO[grp * 4 + dd][:, 0:nk * 512], nk * 512)
            for d4 in range(4):
                d = dd * 4 + d4
                ps = next_ps()
                mm_group(ps, ws, d4 * P, 512, nk, lambda k, h: YH[:, half + k, h * nt:(h + 1) * nt],
                         lambda k: R("YH", half + k), nt)
                S.op("dve", lambda e, d=d, ps=ps: e.tensor_tensor(out=V(X, d * NC, [nt, 2], [1, nt]),
                                                                in0=PSM[ps][:, :, 0:nt],
                                                                in1=V(X, d * NC, [nt, 2], [1, nt]), op=ALU.add),
                     reads=[R("PS", ps), R("X", d)], writes=[R("X", d)])

    def ffn_unit(j, g, out_slot):
        b = bufs()
        npc, nt, ncols, nh = g.npc, g.nt, g.ncols, g.nh
        xp, xc, gg = b.xp, b.xc, b.gg
        rXP, rXPt, rXC, rGG = b.rXP, b.rXPt, b.rXC, b.rGG
        srcN = lambda k, h: N[:, k, h * nt:(h + 1) * nt]
        resN = lambda k: R("N", k)
        nsx = npc + 2
        vh = lambda t: V(t, 0, [nt, nh], [1, nt])

        def phA():
            ws_g = load_w(wU[j], 2048)
            ps_g = next_ps()
            mm_group(ps_g, ws_g, 0, P, KD, srcN, resN, nt)
            tails2(xp, rXPt, PS_fc, R("PS_fc", j), ST_fc, R("ST_fc", j), j, g.p, npc)
            S.op("act", lambda e: e.activation(out=xp[:, 2:2 + nt], in_=PSM[ps_g][:, 0, 0:nt], func=AF.Copy),
                 reads=[R("PS", ps_g)], writes=[rXP])
            S.op("act", lambda e: e.activation(out=xp[:, 2 + nt:2 + npc], in_=PSM[ps_g][:, 1, 0:npc - nt], func=AF.Copy),
                 reads=[R("PS", ps_g)], writes=[rXP])
            S.op("act", lambda e: e.activation(out=V(xp, nsx + 2, [10, SQ], [1, 8]),
                                               in_=V(PSM[ps_g], 512 + npc - nt, [8, SQ], [1, 8]), func=AF.Copy),
                 reads=[R("PS", ps_g)], writes=[rXP])
            tails2_save(xp, rXP, rXPt, PS_fc, R("PS_fc", j), ST_fc, R("ST_fc", j), j, g.p, npc)
            ws_v = load_w(wU[NF + j], 2048)
            ps_v = next_ps()
            mm_group(ps_v, ws_v, 0, P, KD, srcN, resN, nt)
            S.op("act", lambda e: e.activation(out=vh(gg), in_=PSM[ps_v][:, :, 0:nt], func=AF.Copy),
                 reads=[R("PS", ps_v)], writes=[rGG])
            conv3(xp, xc, rXP, rXPt, rXC, npc, C_CFW + j * 3, C_CFB + j)

        def phB():
            S.op("act", lambda e: e.activation(out=xc[:, 0:ncols], in_=xc[:, 0:ncols], func=AF.Gelu_apprx_tanh),
                 reads=[rXC], writes=[rXC])
            S.op("dve", lambda e: e.tensor_tensor(out=YH[:, out_slot, 0:ncols], in0=gg[:, 0:ncols], in1=xc[:, 0:ncols],
                                                  op=ALU.mult),
                 reads=[rGG, rXC], writes=[R("YH", out_slot)])

        return [phA, phB]

    def down_partial(grp, g):
        nt = g.nt
        half = (grp % 2) * YG
        for dd in range(4):
            ws = load_w(wD[grp * 4 + dd], 2048)
            for d4 in range(4):
                d = dd * 4 + d4
                ps = next_ps()
                mm_group(ps, ws, d4 * P, 512, YG, lambda k, h: YH[:, half + k, h * nt:(h + 1) * nt],
                         lambda k: R("YH", half + k), nt)
                S.op("dve", lambda e, d=d, ps=ps: e.tensor_tensor(out=V(X, d * NC, [nt, 2], [1, nt]),
                                                                in0=PSM[ps][:, :, 0:nt],
                                                                in1=V(X, d * NC, [nt, 2], [1, nt]), op=ALU.add),
                     reads=[R("PS", ps), R("X", d)], writes=[R("X", d)])

    def run_pipelined(units, on_done=None, lags=(0, 2, 3)):
        n = len(units)
        for t in range(n + max(lags)):
            done = []
            for ph in range(len(lags)):
                u = t - lags[ph]
                if 0 <= u < n and ph < len(units[u]):
                    units[u][ph]()
                    if ph == len(units[u]) - 1:
                        done.append(u)
            if on_done is not None:
                for u in done:
                    on_done(u)

    def store_y(p):
        tiles = [(i * P, P) for i in range(4)] + [(512, 72)]
        for ti, (c0, nr) in enumerate(tiles):
            s = nxt("io", NIO)
            for kh in range(2):
                ps = next_ps()
                for i in range(8):
                    k = kh * 8 + i
                    S.op("pe", lambda e, k=k, i=i, ps=ps, c0=c0, nr=nr: e.transpose(
                        out=VP(PSM[ps], i * P, nr, [1, P]), in_=X[:, k, c0:c0 + nr], identity=IDT[:]),
                        reads=[R("X", k), rIDT], writes=[R("PS", ps)])
                copy_op(evac_eng(), IO[s][0:nr, kh * 1024:(kh + 1) * 1024], VP(PSM[ps], 0, nr, [1, 1024]),
                        [R("PS", ps)], [R("IO", s)])
            if ti < 4:
                S.dma(STQ, lambda e, s=s, c0=c0, p=p: e.dma_start(out=y_p[p * PM + c0:p * PM + c0 + P, :], in_=IO[s][:, :]),
                      reads=[R("IO", s)], lane=("io", s))
            else:
                S.dma(STQ, lambda e, s=s, p=p: e.dma_start(out=y_p[p * PM + 512:p * PM + 520, :], in_=IO[s][0:8, :]),
                      reads=[R("IO", s)], lane=("io", s))
                S.dma(STQ, lambda e, s=s, p=p: e.dma_start(out=y_s[p * NS:(p + 1) * NS, :], in_=IO[s][8:72, :]),
                      reads=[R("IO", s)], lane=("io", s))

    def store_states():
        def tr_out(src_fn, nblk, nrows, dst_ap_fn, width):
            s = nxt("io", NIO)
            done = 0
            while done < nblk:
                nb = min(8, nblk - done)
                ps = next_ps()
                for i in range(nb):
                    src, rres = src_fn(done + i)
                    S.op("pe", lambda e, i=i, ps=ps, src=src: e.transpose(out=VP(PSM[ps], i * P, nrows, [1, P]),
                                                                        in_=src, identity=IDT[:]),
                         reads=[rres, rIDT], writes=[R("PS", ps)])
                copy_op(evac_eng(), IO[s][0:nrows, done * P:(done + nb) * P], VP(PSM[ps], 0, nrows, [1, nb * P]),
                        [R("PS", ps)], [R("IO", s)])
                done += nb
            dst_ap_fn(s)

        tr_out(lambda n: (ST_h[:, n, :], R("ST_h", n)), NH, 16,
               lambda s: S.dma(STQ, lambda e: e.dma_start(out=o_sh, in_=IO[s][0:16, 0:DA]), reads=[R("IO", s)],
                               lane=("io", s)), DA)
        tr_out(lambda n: (ST_rc[:, n, :], R("ST_rc", n)), NH, 48,
               lambda s: S.dma(STQ, lambda e: e.dma_start(out=o_src, in_=IO[s][0:48, 0:DA]), reads=[R("IO", s)],
                               lane=("io", s)), DA)
        tr_out(lambda gi: (ST_sc[:, gi, :], R("ST_sc", gi)), NG, 32,
               lambda s: S.dma(STQ, lambda e: e.dma_start(out=o_ssc, in_=IO[s][0:32, 0:DB]), reads=[R("IO", s)],
                               lane=("io", s)), DB)
        for q in range(3):
            tr_out(lambda j, q=q: (ST_fc[:, q * 16 + j, :], R("ST_fc", q * 16 + j)), 16, 32,
                   lambda s, q=q: S.dma(STQ, lambda e: e.dma_start(out=o_sfc[:, q * 2048:(q + 1) * 2048],
                                                                      in_=IO[s][0:32, :]),
                                        reads=[R("IO", s)], lane=("io", s)), 2048)
        for (src, nrows, dst, res) in ((PS_h[:, :], NH, o_ph, [R("PS_h", n) for n in range(NH)]),
                                       (V(PS_rc, 0, [1, NH * 3]), NH * 3, o_prc, [R("PS_rc", n) for n in range(NH)]),
                                       (V(PS_sc, 0, [1, NG * 2]), NG * 2, o_psc, [R("PS_sc", n) for n in range(NG)]),
                                       (V(PS_fc, 0, [1, NF * 2]), NF * 2, o_pfc, [R("PS_fc", n) for n in range(NF)])):
            s = nxt("io", NIO)
            ps = next_ps()
            S.op("pe", lambda e, ps=ps, src=src, nrows=nrows: e.transpose(out=VP(PSM[ps], 0, nrows, [1, P]), in_=src,
                                                                         identity=IDT[:]),
                 reads=res + [rIDT], writes=[R("PS", ps)])
            copy_op(evac_eng(), IO[s][0:nrows, 0:P], VP(PSM[ps], 0, nrows, [1, P]), [R("PS", ps)], [R("IO", s)])
            S.dma(STQ, lambda e, s=s, nrows=nrows, dst=dst: e.dma_start(out=dst, in_=IO[s][0:nrows, 0:P]),
                  reads=[R("IO", s)], lane=("io", s))

    load_states()

    gp = Geom()
    gp.npc, gp.nt, gp.ncols, gp.nh, gp.main, gp.p = 512, 512, 512, 1, False, 0
    for q in range(2):
        load_x_tiles([(i * P, P, [(0, P, xw[q * 512 + i * P:q * 512 + (i + 1) * P, :])]) for i in range(4)])
        rmsnorm_fm(512, 512, C_GMIX)
        run_pipelined([lru_unit(n, gp, None) for n in range(NH)], lags=(0, 2))
    S.op("dve", lambda e: e.tensor_scalar(out=PS_h[:], in0=PS_h[:], scalar1=cp(C_FLAG), scalar2=None, op0=ALU.mult),
         reads=[R("PS_h", n) for n in range(NH)] + [rCP], writes=[R("PS_h", n) for n in range(NH)])

    for p in range(2):
        g = Geom()
        g.npc, g.nt, g.ncols, g.nh, g.main, g.p = PM, NT, NC, 2, True, p
        base = NPRE + p * PM
        tiles = [(i * P, P, [(0, P, xw[base + i * P:base + (i + 1) * P, :])]) for i in range(4)]
        tiles.append((512, 72, [(0, 8, xw[base + 512:base + 520, :]), (8, 64, xs[p * NS:(p + 1) * NS, :])]))
        load_x_tiles(tiles)
        rmsnorm_fm(NC, NT, C_GMIX)
        units = []
        for c in range(NMIX):
            slot = c % (2 * YG)
            units.append(lru_unit(c, g, slot) if c < NH else sconv_unit(c - NH, g, slot))

        def mix_done(u, g=g):
            if u % YG == YG - 1:
                wo_partial(u // YG, YG, g)
        run_pipelined(units, mix_done)
        rmsnorm_fm(NC, NT, C_GFFN)
        funits = [ffn_unit(j, g, j % (2 * YG)) for j in range(NF)]

        def ffn_done(u, g=g):
            if u % YG == YG - 1:
                down_partial(u // YG, g)
        run_pipelined(funits, ffn_done, lags=(0, 1))
        rmsnorm_fm(NC, NT, C_GFIN, rounded=False)
        store_y(p)
    store_states()

    S.emit(nc, es)
    es.close()
    return nc


_PROG = {}


def _tile_cols(w, ncb):
    K = w.shape[0]
    return np.ascontiguousarray(w.reshape(K // P, P, ncb, P).transpose(2, 1, 0, 3)).reshape(ncb, P, (K // P) * P)


def _tile_rows(w, gk):
    K = w.shape[0]
    ng = K // (gk * P)
    a = w.reshape(ng, gk, P, 4, 512).transpose(0, 3, 2, 1, 4)
    return np.ascontiguousarray(a).reshape(ng * 4, P, gk * 512)


def _fm(v, n):
    return np.ascontiguousarray(v.reshape(n, P).T)


def kernel(x_prompt, x_sample, state_lru_h, state_lru_conv, state_sconv, state_ffn_conv, meta_tokens, g_mix, w_in,
           conv_a_w, conv_a_b, w_gate_a, b_gate_a, w_gate_x, b_gate_x, lru_lambda, conv_b_w, g_out_a, g_out_b, w_o,
           g_ffn, w_up, conv_f_w, conv_f_b, w_down, g_final):
    f32 = np.float32
    A = lambda a: np.asarray(a, dtype=f32)
    x_prompt, x_sample = A(x_prompt), A(x_sample)
    import os
    stage = os.environ.get("KSTAGE")
    key = ("nc", stage)
    if key not in _PROG:
        _PROG[key] = build_program(stop_after=None if stage is None else int(stage))
    nc = _PROG[key]
    wA_ = _tile_cols(A(w_in)[0], 48)
    wU_ = _tile_cols(A(w_up)[0], 96)
    wO_ = _tile_rows(A(w_o)[0], YG)
    wD_ = _tile_rows(A(w_down)[0], YG)
    wG_ = np.ascontiguousarray(np.concatenate([A(w_gate_a)[0], A(w_gate_x)[0]], axis=2))
    cpar = np.zeros((P, NCPAR), f32)
    cpar[:, C_GMIX:C_GMIX + 16] = _fm(A(g_mix)[0], 16)
    cpar[:, C_GFFN:C_GFFN + 16] = _fm(A(g_ffn)[0], 16)
    cpar[:, C_GFIN:C_GFIN + 16] = _fm(A(g_final), 16)
    cpar[:, C_CAB:C_CAB + 12] = _fm(A(conv_a_b)[0], 12)
    cpar[:, C_BGA:C_BGA + 12] = _fm(A(b_gate_a)[0], 12)
    cpar[:, C_BGX:C_BGX + 12] = _fm(A(b_gate_x)[0], 12)
    cpar[:, C_LAM:C_LAM + 12] = _fm(A(lru_lambda)[0], 12)
    cpar[:, C_GOA:C_GOA + 12] = _fm(A(g_out_a)[0], 12)
    cpar[:, C_GOB:C_GOB + 8] = _fm(A(g_out_b)[0], 8)
    cpar[:, C_CFB:C_CFB + 48] = _fm(A(conv_f_b)[0], 48)
    caw = A(conv_a_w)[0]
    cpar[:, C_CAW:C_CAW + 48] = caw.reshape(4, NH, P).transpose(2, 1, 0).reshape(P, 48)
    cbw = A(conv_b_w)[0]
    cpar[:, C_CBW:C_CBW + 24] = cbw.reshape(3, NG, P).transpose(2, 1, 0).reshape(P, 24)
    cfw = A(conv_f_w)[0]
    cpar[:, C_CFW:C_CFW + 144] = cfw.reshape(3, NF, P).transpose(2, 1, 0).reshape(P, 144)
    ident = np.eye(P, dtype=f32)
    meta = A(meta_tokens)
    in_maps = []
    for c in range(NCORES):
        b, half = c // 2, c % 2
        seq = np.concatenate([meta, x_prompt[b]], axis=0)
        if half == 0:
            xw_ = np.concatenate([np.zeros((1032, D), f32), seq[0:1032]], axis=0)
        else:
            xw_ = seq
        cp_c = cpar.copy()
        cp_c[:, C_FLAG] = float(half)
        sl = slice(16 * c, 16 * c + 16)
        st_hr = np.concatenate([A(state_lru_h)[0, sl], A(state_lru_conv)[0, sl].reshape(48, DA)], axis=0)
        in_maps.append({
            "xw": np.ascontiguousarray(xw_), "xs": np.ascontiguousarray(x_sample[sl].reshape(128, D)),
            "st_hr": np.ascontiguousarray(st_hr),
            "st_sc": np.ascontiguousarray(A(state_sconv)[0, sl].reshape(32, DB)),
            "st_fc": np.ascontiguousarray(A(state_ffn_conv)[0, sl].reshape(32, DFF)),
            "cpar": cp_c, "ident": ident, "wA": wA_, "wG": wG_, "wO": wO_, "wU": wU_, "wD": wD_,
        })
    res = run_bass_kernel_spmd(nc, in_maps, core_ids=list(range(NCORES)))
    r = res.results
    B = x_prompt.shape[0]
    y_prompt = np.zeros((B, 2048, D), f32)
    y_sample = np.zeros((128, 8, D), f32)
    p_h = np.zeros((1, B, DA), f32)
    p_rc = np.zeros((1, B, 3, DA), f32)
    p_sc = np.zeros((1, B, 2, DB), f32)
    p_fc = np.zeros((1, B, 2, DFF), f32)
    s_h = np.zeros((1, 128, DA), f32)
    s_rc = np.zeros((1, 128, 3, DA), f32)
    s_sc = np.zeros((1, 128, 2, DB), f32)
    s_fc = np.zeros((1, 128, 2, DFF), f32)
    for c in range(NCORES):
        b, half = c // 2, c % 2
        yp = r[c]["y_p"][HALO:]
        if half == 0:
            y_prompt[b, 0:1016] = yp[16:1032]
        else:
            y_prompt[b, 1016:2048] = yp
            p_h[0, b] = r[c]["o_ph"].reshape(DA)
            p_rc[0, b] = r[c]["o_prc"].reshape(NH, 3, P).transpose(1, 0, 2).reshape(3, DA)
            p_sc[0, b] = r[c]["o_psc"].reshape(NG, 2, P).transpose(1, 0, 2).reshape(2, DB)
            p_fc[0, b] = r[c]["o_pfc"].reshape(NF, 2, P).transpose(1, 0, 2).reshape(2, DFF)
        sl = slice(16 * c, 16 * c + 16)
        y_sample[sl] = r[c]["y_s"].reshape(16, 8, D)
        s_h[0, sl] = r[c]["o_sh"]
        s_rc[0, sl] = r[c]["o_src"].reshape(16, 3, DA)
        s_sc[0, sl] = r[c]["o_ssc"].reshape(16, 2, DB)
        s_fc[0, sl] = r[c]["o_sfc"].reshape(16, 2, DFF)
    return (y_prompt, y_sample, p_h, p_rc, p_sc, p_fc, s_h, s_rc, s_sc, s_fc)
```

```python
import numpy as np
from contextlib import ExitStack
import concourse.bass as bass
import concourse.mybir as mybir
from concourse.ap import AP
from concourse.bass_utils import run_bass_kernel_spmd

F32 = mybir.dt.float32
F32R = mybir.dt.float32r
AF = mybir.ActivationFunctionType
ALU = mybir.AluOpType

NCORES = 8
P = 128
D = 2048
KD = 16
DA = 1536
NH = 12
DB = 1024
NG = 8
DFF = 6144
NF = 48
NMIX = 20
DIN = 2 * DA + 3 * DB
HALO = 8
NPRE = 1024
NMAIN = 1040
PM = 520
SQ = 8
NS = 64
NC = 584
NT = 292
WBW = 616
YG = 4
EPS = 1e-6

C_GMIX, C_GFFN, C_GFIN = 0, 16, 32
C_CAB, C_BGA, C_BGX, C_LAM, C_GOA, C_GOB, C_CFB = 48, 60, 72, 84, 96, 108, 116
C_CAW, C_CBW, C_CFW, C_FLAG = 164, 212, 236, 380
NCPAR = 384
DC_HBA, DC_HBX, DC_C, DC_CH, DC_EPS, DC_ONE = 0, 12, 24, 36, 48, 49
NDC = 64


class Res:
    __slots__ = ("name", "w", "r")

    def __init__(self, name):
        self.name = name
        self.w = None
        self.r = []


class Op:
    __slots__ = ("eng", "fn", "deps", "lane", "lane_idx", "sig", "tick", "is_dma")


class Sched:
    ENGS = ("pe", "act", "dve", "pool", "sp")

    def __init__(self):
        self.streams = {e: [] for e in self.ENGS}
        self.lanes = {}
        self.resd = {}

    def R(self, *key):
        r = self.resd.get(key)
        if r is None:
            r = Res(key)
            self.resd[key] = r
        return r

    def _rec(self, op, reads, writes):
        deps = {}
        for r in reads:
            if r.w is not None:
                deps[id(r.w)] = (r.w, True)
        for w in writes:
            if w.w is not None and id(w.w) not in deps:
                deps[id(w.w)] = (w.w, False)
            for rd in w.r:
                if id(rd) not in deps:
                    deps[id(rd)] = (rd, False)
        fin = []
        for p, raw in deps.values():
            if p is op:
                continue
            if (not p.is_dma) and (not op.is_dma) and p.eng == op.eng and not raw:
                continue
            fin.append(p)
            if not p.is_dma:
                p.sig = True
        op.deps = fin
        for r in reads:
            r.r.append(op)
        for w in writes:
            w.w = op
            w.r = []
        self.streams[op.eng].append(op)

    def op(self, eng, fn, reads=(), writes=()):
        if any(r.name[0] == "PS" for r in reads):
            writes = list(writes) + [r for r in reads if r.name[0] == "PS"]
            reads = [r for r in reads if r.name[0] != "PS"]
        o = Op()
        o.eng = eng
        o.fn = fn
        o.is_dma = False
        o.sig = False
        o.tick = None
        o.lane = None
        o.lane_idx = None
        self._rec(o, reads, writes)
        return o

    def dma(self, queue, fn, reads=(), writes=(), lane=None, bulk=False):
        o = Op()
        o.eng = queue
        o.fn = fn
        o.is_dma = True
        o.sig = True
        o.tick = None
        ln = self.lanes.setdefault(lane, [0, bulk])
        o.lane = lane
        o.lane_idx = ln[0]
        ln[0] += 1
        self._rec(o, reads, writes)
        return o

    def emit(self, nc, es):
        for e in self.ENGS:
            t = 0
            for o in self.streams[e]:
                if not o.is_dma and o.sig:
                    t += 1
                    o.tick = t
        esem = {e: es.enter_context(nc.semaphore("sem_" + e)) for e in self.ENGS if e != "sp"}
        lsem = {ln: es.enter_context(nc.semaphore("lane_" + str(ln))) for ln in self.lanes}
        store_lanes = {}
        for e in self.ENGS:
            for o in self.streams[e]:
                if o.is_dma:
                    store_lanes.setdefault(e, set()).add(o.lane)

        def run_stream(ename, eng):
            waited = {}
            for o in self.streams[ename]:
                need = {}
                for p in o.deps:
                    if p.is_dma:
                        cnt, bulk = self.lanes[p.lane]
                        val = 16 * (cnt if bulk else (p.lane_idx + 1))
                        key = ("l", p.lane)
                        sem = lsem[p.lane]
                    else:
                        val = p.tick
                        key = ("e", p.eng)
                        sem = esem[p.eng]
                    if need.get(key, (None, 0))[1] < val:
                        need[key] = (sem, val)
                for key, (sem, val) in need.items():
                    if waited.get(key, 0) >= val:
                        continue
                    eng.wait_ge(sem, val)
                    waited[key] = val
                ins = o.fn(eng)
                if o.is_dma:
                    ins.then_inc(lsem[o.lane], 16)
                elif o.sig:
                    ins.then_inc(esem[ename], 1)
            for ln in sorted(store_lanes.get(ename, ()), key=str):
                val = 16 * self.lanes[ln][0]
                if waited.get(("l", ln), 0) < val:
                    eng.wait_ge(lsem[ln], val)

        block = es.enter_context(nc.Block())

        @block.sync
        def _(e):
            run_stream("sp", e)

        @block.tensor
        def _(e):
            run_stream("pe", e)

        @block.scalar
        def _(e):
            run_stream("act", e)

        @block.vector
        def _(e):
            run_stream("dve", e)

        @block.gpsimd
        def _(e):
            run_stream("pool", e)


def build_program(stop_after=None, dbg=False):
    nc = bass.Bass("TRN2", target_bir_lowering=False)
    nc.dge_precook = False
    S = Sched()
    R = S.R
    es = ExitStack()
    import os as _os
    STQ = _os.environ.get("KSTQ", "pool")

    def din(name, shape, dt=F32):
        return nc.dram_tensor(name, shape, dt, kind="ExternalInput").ap()

    def dout(name, shape, dt=F32):
        return nc.dram_tensor(name, shape, dt, kind="ExternalOutput").ap()

    xw = din("xw", [NPRE + NMAIN, D])
    xs = din("xs", [128, D])
    st_hr = din("st_hr", [64, DA])
    st_sc = din("st_sc", [32, DB])
    st_fc = din("st_fc", [32, DFF])
    cpar = din("cpar", [P, NCPAR])
    identd = din("ident", [P, P])
    wA = din("wA", [48, P, 2048], F32R)
    wG = din("wG", [NH, P, 256], F32R)
    wO = din("wO", [20, P, 2048], F32R)
    wU = din("wU", [96, P, 2048], F32R)
    wD = din("wD", [48, P, 2048], F32R)
    y_p = dout("y_p", [NMAIN, D])
    y_s = dout("y_s", [128, D])
    o_sh = dout("o_sh", [16, DA])
    o_src = dout("o_src", [48, DA])
    o_ssc = dout("o_ssc", [32, DB])
    o_sfc = dout("o_sfc", [32, DFF])
    o_ph = dout("o_ph", [NH, P])
    o_prc = dout("o_prc", [NH * 3, P])
    o_psc = dout("o_psc", [NG * 2, P])
    o_pfc = dout("o_pfc", [NF * 2, P])

    def sb(name, shape, dt=F32):
        return es.enter_context(nc.sbuf_tensor(name, shape, dt))

    X = sb("X", [P, KD, NC])
    N = sb("N", [P, KD, NC], F32R)
    YH = sb("YH", [P, 2 * YG, NC], F32R)
    NWS = 4
    Wsl = [sb(f"W{i}", [P, 2048], F32R) for i in range(NWS)]
    NGS = 4
    GWsl = [sb(f"GW{i}", [P, 256], F32R) for i in range(NGS)]
    NIO = 2
    IO = [sb(f"IO{i}", [P, 2048]) for i in range(NIO)]
    XP = [sb(f"XP{i}", [P, WBW]) for i in range(2)]
    XC = [sb(f"XC{i}", [P, WBW]) for i in range(3)]
    GG = [sb(f"GG{i}", [P, WBW]) for i in range(4)]
    SB_ = [sb(f"SB{i}", [P, WBW]) for i in range(2)]
    AB = [sb(f"AB{i}", [P, WBW]) for i in range(2)]
    UB = [sb(f"UB{i}", [P, WBW]) for i in range(2)]
    SQB = [sb(f"SQB{i}", [P, NC], F32R) for i in range(2)]
    XCR = [sb(f"XCR{i}", [P, NC], F32R) for i in range(3)]
    RB = sb("RB", [P, NC])
    CP = sb("CP", [P, NCPAR])
    DC = sb("DC", [P, NDC])
    IDT = sb("IDT", [P, P])
    ONES = sb("ONES", [P, P], F32R)
    ST_h = sb("ST_h", [P, NH, 16])
    ST_rc = sb("ST_rc", [P, NH, 48])
    ST_sc = sb("ST_sc", [P, NG, 32])
    ST_fc = sb("ST_fc", [P, NF, 32])
    PS_h = sb("PS_h", [P, NH])
    PS_rc = sb("PS_rc", [P, NH, 3])
    PS_sc = sb("PS_sc", [P, NG, 2])
    PS_fc = sb("PS_fc", [P, NF, 2])
    NPS = 4
    PSM = [es.enter_context(nc.psum_tensor(f"PSM{i}", [P, 2, 512], F32)) for i in range(NPS)]

    def pst(t):
        return t[:].ap[0][0]

    def V(t, off, *dims, dt=None):
        a = AP(t, off, [[pst(t), P]] + [list(d) for d in dims])
        if dt is not None:
            a = a.bitcast(dt)
        return a

    def VP(t, off, npart, *dims):
        return AP(t, off, [[pst(t), npart]] + [list(d) for d in dims])

    cnt = {"w": 0, "g": 0, "io": 0, "ps": 0, "sq": 0}

    def nxt(k, n):
        i = cnt[k] % n
        cnt[k] += 1
        return i

    def load_w(src_ap, ncols):
        s = nxt("w", NWS)
        S.dma("sp", lambda e, s=s: e.dma_start(out=Wsl[s][:, 0:ncols], in_=src_ap),
              writes=[R("W", s)], lane=("w", s))
        return s

    def next_ps():
        return nxt("ps", NPS)

    cp = lambda c0, n=1: CP[:, c0:c0 + n]
    dc = lambda c0, n=1: DC[:, c0:c0 + n]
    rCP, rDC, rIDT, rONES = R("CP"), R("DC"), R("IDT"), R("ONES")

    S.dma("sp", lambda e: e.dma_start(out=CP[:], in_=cpar), writes=[rCP], lane="const", bulk=True)
    S.dma("sp", lambda e: e.dma_start(out=IDT[:], in_=identd), writes=[rIDT], lane="const", bulk=True)
    S.op("dve", lambda e: e.memset(RB[:, 0:P], 1.0), writes=[R("RB")])
    S.op("dve", lambda e: e.tensor_copy(out=ONES[:], in_=RB[:, 0:P]), reads=[R("RB")], writes=[rONES])
    S.op("dve", lambda e: e.memset(DC[:, DC_EPS:DC_EPS + 1], EPS), writes=[R("DCe")])
    S.op("dve", lambda e: e.memset(DC[:, DC_ONE:DC_ONE + 1], 1.0), writes=[R("DCo")])
    S.op("dve", lambda e: e.memset(PS_h[:], 0.0), writes=[R("PS_h", n) for n in range(NH)])
    S.op("dve", lambda e: e.memset(PS_rc[:], 0.0), writes=[R("PS_rc", n) for n in range(NH)])
    S.op("dve", lambda e: e.memset(PS_sc[:], 0.0), writes=[R("PS_sc", g) for g in range(NG)])
    S.op("dve", lambda e: e.memset(PS_fc[:], 0.0), writes=[R("PS_fc", j) for j in range(NF)])
    S.op("dve", lambda e: e.tensor_scalar(out=dc(DC_HBA, 24), in0=cp(C_BGA, 24), scalar1=0.5, scalar2=None,
                                          op0=ALU.mult), reads=[rCP], writes=[R("DChb")])
    if "c_act" not in _os.environ.get("KSKIP", "").split(","):
        S.op("act", lambda e: e.activation(out=dc(DC_C, 12), in_=cp(C_LAM, 12), func=AF.Exp, scale=-1.0),
             reads=[rCP], writes=[R("DCc")])
        S.op("act", lambda e: e.activation(out=dc(DC_C, 12), in_=dc(DC_C, 12), func=AF.Ln, bias=dc(DC_ONE), scale=1.0),
             reads=[R("DCc"), R("DCo")], writes=[R("DCc")])
    S.op("dve", lambda e: e.tensor_scalar(out=dc(DC_CH, 12), in0=dc(DC_C, 12), scalar1=-4.0, scalar2=None,
                                          op0=ALU.mult), reads=[R("DCc")], writes=[R("DCch")])
    S.op("dve", lambda e: e.tensor_scalar(out=dc(DC_C, 12), in0=dc(DC_C, 12), scalar1=-8.0, scalar2=None,
                                          op0=ALU.mult), reads=[R("DCc"), R("DCch")], writes=[R("DCc")])
    rCONST = [rCP, R("DChb"), R("DCc"), R("DCch"), R("DCe"), R("DCo")]

    flip = [0]

    def evac_eng():
        flip[0] ^= 1
        return "act" if flip[0] else "dve"

    def copy_op(eng, out, in_, reads, writes):
        if eng == "act":
            S.op("act", lambda e: e.activation(out=out, in_=in_, func=AF.Copy), reads=reads, writes=writes)
        else:
            S.op(eng, lambda e: e.tensor_copy(out=out, in_=in_), reads=reads, writes=writes)

    def load_states():
        KS = _os.environ.get("KSKIP", "").split(",")
        if "hr" in KS:
            return
        s = nxt("io", NIO)
        S.dma("sp", lambda e: e.dma_start(out=IO[s][0:64, 0:DA], in_=st_hr), writes=[R("IO", s)], lane=("io", s))
        ps = next_ps()
        for n in range(NH):
            S.op("pe", lambda e, n=n: e.transpose(out=V(PSM[ps], n * 64, [1, 64]), in_=IO[s][0:64, n * P:(n + 1) * P],
                                                  identity=IDT[0:64, 0:64]),
                 reads=[R("IO", s), rIDT], writes=[R("PS", ps)])
        copy_op("act", ST_h[:], V(PSM[ps], 0, [64, NH], [1, 16]), [R("PS", ps)], [R("ST_h", n) for n in range(NH)])
        if "rc" not in KS:
            copy_op("dve", ST_rc[:], V(PSM[ps], 16, [64, NH], [1, 48]), [R("PS", ps)], [R("ST_rc", n) for n in range(NH)])
        if "sc" in KS:
            return
        s2 = nxt("io", NIO)
        S.dma("sp", lambda e: e.dma_start(out=IO[s2][0:32, 0:DB], in_=st_sc), writes=[R("IO", s2)], lane=("io", s2))
        ps2 = next_ps()
        for g in range(NG):
            S.op("pe", lambda e, g=g: e.transpose(out=V(PSM[ps2], g * 64, [1, 64]), in_=IO[s2][0:64, g * P:(g + 1) * P],
                                                  identity=IDT[0:64, 0:64]),
                 reads=[R("IO", s2), rIDT], writes=[R("PS", ps2)])
        copy_op("act", ST_sc[:], V(PSM[ps2], 0, [64, NG], [1, 32]), [R("PS", ps2)], [R("ST_sc", g) for g in range(NG)])
        if "fc" in KS:
            return
        for q in range(3):
            s3 = nxt("io", NIO)
            S.dma("sp", lambda e, q=q, s3=s3: e.dma_start(out=IO[s3][0:32, :], in_=st_fc[:, q * 2048:(q + 1) * 2048]),
                  writes=[R("IO", s3)], lane=("io", s3))
            for hh in range(2):
                ps3 = next_ps()
                for i in range(8):
                    ii = hh * 8 + i
                    S.op("pe", lambda e, i=i, ii=ii, s3=s3, ps3=ps3: e.transpose(out=V(PSM[ps3], i * 64, [1, 64]),
                                                                                 in_=IO[s3][0:64, ii * P:(ii + 1) * P],
                                                                                 identity=IDT[0:64, 0:64]),
                         reads=[R("IO", s3), rIDT], writes=[R("PS", ps3)])
                j0 = q * 16 + hh * 8
                copy_op(evac_eng(), ST_fc[:, j0:j0 + 8, :], V(PSM[ps3], 0, [64, 8], [1, 32]), [R("PS", ps3)],
                        [R("ST_fc", j) for j in range(j0, j0 + 8)])

    def load_x_tiles(tiles):
        for (col0, nr, parts) in tiles:
            s = nxt("io", NIO)
            for (r0, n_, src) in parts:
                S.dma("sp", lambda e, s=s, r0=r0, n_=n_, src=src: e.dma_start(out=IO[s][r0:r0 + n_, :], in_=src),
                      writes=[R("IO", s)], lane=("io", s))
            for kh in range(2):
                ps = next_ps()
                for i in range(8):
                    k = kh * 8 + i
                    S.op("pe", lambda e, s=s, k=k, i=i, ps=ps, nr=nr: e.transpose(
                        out=V(PSM[ps], i * P, [1, nr]), in_=IO[s][0:nr, k * P:(k + 1) * P], identity=IDT[0:nr, 0:nr]),
                        reads=[R("IO", s), rIDT], writes=[R("PS", ps)])
                copy_op(evac_eng(), X[:, kh * 8:kh * 8 + 8, col0:col0 + nr], V(PSM[ps], 0, [P, 8], [1, nr]),
                        [R("PS", ps)], [R("X", k) for k in range(kh * 8, kh * 8 + 8)])

    def rmsnorm_fm(ncols, nt, gcol, rounded=True):
        nh = ncols // nt
        ps = next_ps()
        for k in range(KD):
            q = nxt("sq", 2)
            S.op("act", lambda e, k=k, q=q: e.activation(out=SQB[q][:, 0:ncols], in_=X[:, k, 0:ncols], func=AF.Square),
                 reads=[R("X", k)], writes=[R("SQ", q)])
            for h in range(nh):
                S.op("pe", lambda e, k=k, q=q, h=h: e.matmul(PSM[ps][:, h, 0:nt], ONES[:], SQB[q][:, h * nt:(h + 1) * nt],
                                                           start=(k == 0), stop=(k == KD - 1)),
                     reads=[R("SQ", q), rONES], writes=[R("PS", ps)])
        S.op("act", lambda e: e.activation(out=V(RB, 0, [nt, nh], [1, nt]), in_=PSM[ps][:, 0:nh, 0:nt], func=AF.Ln,
                                           bias=dc(DC_EPS), scale=1.0 / D),
             reads=[R("PS", ps), R("DCe")], writes=[R("RB")])
        S.op("act", lambda e: e.activation(out=RB[:, 0:ncols], in_=RB[:, 0:ncols], func=AF.Exp, scale=-0.5),
             reads=[R("RB")], writes=[R("RB")])
        for k in range(KD):
            o = N[:, k, 0:ncols] if rounded else X[:, k, 0:ncols]
            S.op("dve", lambda e, k=k, o=o: e.scalar_tensor_tensor(out=o, in0=X[:, k, 0:ncols], scalar=cp(gcol + k),
                                                                 in1=RB[:, 0:ncols], op0=ALU.mult, op1=ALU.mult),
                 reads=[R("X", k), R("RB"), rCP], writes=[R("N", k) if rounded else R("X", k)])

    def mm_group(ps, wslot, wcol0, wkstride, nk, src_fn, src_res, nt, nhalf=2):
        for k in range(nk):
            for h in range(nhalf):
                S.op("pe", lambda e, k=k, h=h: e.matmul(PSM[ps][:, h, 0:nt],
                                                       Wsl[wslot][:, k * wkstride + wcol0:k * wkstride + wcol0 + P],
                                                       src_fn(k, h), start=(k == 0), stop=(k == nk - 1)),
                     reads=[R("W", wslot), src_res(k)], writes=[R("PS", ps)])

    class Geom:
        pass

    ucnt = [0]

    def bufs():
        u = ucnt[0]
        ucnt[0] += 1
        b = Geom()
        i2, i3, i4 = u % 2, u % 3, u % 4
        b.xp, b.rXP, b.rXPt = XP[i2], R("XP", i2), R("XPt", i2)
        b.xcr, b.rXCR = XCR[i3], R("XCR", i3)
        b.sb, b.rS = SB_[i2], R("SBf", i2)
        b.ab, b.rA = AB[i2], R("AB", i2)
        b.ub, b.rU = UB[i2], R("UB", i2)
        b.xc, b.rXC, b.rXCs = XC[i3], R("XC", i3), R("XCs", i3)
        b.gg, b.rGG = GG[i4], R("GG", i4)
        return b

    def norm_phase(g, yb, yres, b, gcol, out_slot):
        nt, ncols, nh = g.nt, g.ncols, g.nh
        q = nxt("sq", 2)
        sqb = SQB[q]
        S.op("pool", lambda e: e.tensor_tensor(out=sqb[:, 0:ncols], in0=yb[:, 0:ncols], in1=yb[:, 0:ncols], op=ALU.mult),
             reads=yres, writes=[R("SQ", q)])
        ps = next_ps()
        for h in range(nh):
            S.op("pe", lambda e, h=h: e.matmul(PSM[ps][:, h, 0:nt], ONES[:], sqb[:, h * nt:(h + 1) * nt],
                                               start=True, stop=True),
                 reads=[R("SQ", q), rONES], writes=[R("PS", ps)])
        S.op("act", lambda e: e.activation(out=V(b.ab, 0, [nt, nh], [1, nt]), in_=PSM[ps][:, 0:nh, 0:nt], func=AF.Ln,
                                           bias=dc(DC_EPS), scale=1.0 / P),
             reads=[R("PS", ps), R("DCe")], writes=[b.rA])
        S.op("act", lambda e: e.activation(out=b.ab[:, 0:ncols], in_=b.ab[:, 0:ncols], func=AF.Exp, scale=-0.5),
             reads=[b.rA], writes=[b.rA])
        S.op("dve", lambda e: e.scalar_tensor_tensor(out=YH[:, out_slot, 0:ncols], in0=yb[:, 0:ncols], scalar=cp(gcol),
                                                     in1=b.ab[:, 0:ncols], op0=ALU.mult, op1=ALU.mult),
             reads=yres + [b.rA, rCP], writes=[R("YH", out_slot)])

    def lru_unit(n, g, out_slot):
        b = bufs()
        npc, nt, ncols, nh = g.npc, g.nt, g.ncols, g.nh
        xp, xc, xcr, gg, sbuf_, ab, ub = b.xp, b.xc, b.xcr, b.gg, b.sb, b.ab, b.ub
        rXP, rXPt, rXC, rXCs, rXCR, rGG, rS, rA, rU = b.rXP, b.rXPt, b.rXC, b.rXCs, b.rXCR, b.rGG, b.rS, b.rA, b.rU
        nsx = npc + 3
        st = Geom()
        srcN = lambda k, h: N[:, k, h * nt:(h + 1) * nt]
        resN = lambda k: R("N", k)
        cw = lambda k: cp(C_CAW + n * 4 + k)
        vh = lambda t: V(t, 0, [nt, nh], [1, nt])

        def phA():
            ws_xa = load_w(wA[n], 2048)
            ps_xa = next_ps()
            mm_group(ps_xa, ws_xa, 0, P, KD, srcN, resN, nt, nh)
            if g.main:
                ws_ga = load_w(wA[NH + n], 2048)
                ps_ga = next_ps()
                mm_group(ps_ga, ws_ga, 0, P, KD, srcN, resN, nt, nh)
            st.gs = nxt("g", NGS)
            gs = st.gs
            S.dma("sp", lambda e: e.dma_start(out=GWsl[gs][:], in_=wG[n]), writes=[R("GW", gs)], lane=("g", gs))
            S.op("pool", lambda e: e.tensor_copy(out=xp[:, 0:3], in_=PS_rc[:, n, :]), reads=[R("PS_rc", n)], writes=[rXPt])
            if g.main:
                S.op("pool", lambda e: e.tensor_copy(out=V(xp, nsx, [11, SQ], [1, 3]),
                                                     in_=V(ST_rc, n * 48 + g.p * 24, [3, SQ], [1, 3])),
                     reads=[R("ST_rc", n)], writes=[rXPt])
                S.op("act", lambda e: e.activation(out=xp[:, 3:3 + nt], in_=PSM[ps_xa][:, 0, 0:nt], func=AF.Copy),
                     reads=[R("PS", ps_xa)], writes=[rXP])
                S.op("act", lambda e: e.activation(out=xp[:, 3 + nt:3 + npc], in_=PSM[ps_xa][:, 1, 0:npc - nt],
                                                   func=AF.Copy),
                     reads=[R("PS", ps_xa)], writes=[rXP])
                S.op("act", lambda e: e.activation(out=V(xp, nsx + 3, [11, SQ], [1, 8]),
                                                   in_=V(PSM[ps_xa], 512 + npc - nt, [8, SQ], [1, 8]), func=AF.Copy),
                     reads=[R("PS", ps_xa)], writes=[rXP])
            else:
                S.op("act", lambda e: e.activation(out=V(xp, 3, [nt, nh], [1, nt]), in_=PSM[ps_xa][:, 0:nh, 0:nt],
                                                   func=AF.Copy),
                     reads=[R("PS", ps_xa)], writes=[rXP])
            S.op("pool", lambda e: e.tensor_copy(out=PS_rc[:, n, :], in_=xp[:, npc:npc + 3]),
                 reads=[rXP, rXPt], writes=[R("PS_rc", n)])
            if g.main:
                S.op("pool", lambda e: e.tensor_copy(out=V(ST_rc, n * 48 + g.p * 24, [3, SQ], [1, 3]),
                                                     in_=V(xp, nsx + 8, [11, SQ], [1, 3])),
                     reads=[rXP, rXPt], writes=[R("ST_rc", n)])
            S.op("dve", lambda e: e.tensor_scalar(out=xc[:, 0:npc], in0=xp[:, 0:npc], scalar1=cw(0), scalar2=cp(C_CAB + n),
                                                  op0=ALU.mult, op1=ALU.add),
                 reads=[rXP, rXPt, rCP], writes=[rXC])
            for k in range(1, 4):
                o = xcr[:, 0:npc] if k == 3 else xc[:, 0:npc]
                S.op("dve", lambda e, k=k, o=o: e.scalar_tensor_tensor(out=o, in0=xp[:, k:k + npc], scalar=cw(k),
                                                                     in1=xc[:, 0:npc], op0=ALU.mult, op1=ALU.add),
                     reads=[rXP, rXPt, rXC, rCP], writes=[rXCR if k == 3 else rXC])
            if g.main:
                S.op("dve", lambda e: e.tensor_scalar(out=V(xc, npc, [8, SQ], [1, 8]), in0=V(xp, nsx, [11, SQ], [1, 8]),
                                                      scalar1=cw(0), scalar2=cp(C_CAB + n), op0=ALU.mult, op1=ALU.add),
                     reads=[rXP, rXPt, rCP], writes=[rXCs])
                for k in range(1, 4):
                    S.op("dve", lambda e, k=k: e.scalar_tensor_tensor(
                        out=V(xcr if k == 3 else xc, npc, [8, SQ], [1, 8]), in0=V(xp, nsx + k, [11, SQ], [1, 8]),
                        scalar=cw(k), in1=V(xc, npc, [8, SQ], [1, 8]), op0=ALU.mult, op1=ALU.add),
                        reads=[rXP, rXPt, rXCs, rCP], writes=[rXCR if k == 3 else rXCs])
                S.op("act", lambda e: e.activation(out=vh(gg), in_=PSM[ps_ga][:, 0:nh, 0:nt], func=AF.Gelu_apprx_tanh),
                     reads=[R("PS", ps_ga)], writes=[rGG])

        def phB():
            gs = st.gs
            ps_r = next_ps()
            ps_i = next_ps()
            for (psx, c0) in ((ps_r, 0), (ps_i, P)):
                for h in range(nh):
                    S.op("pe", lambda e, psx=psx, c0=c0, h=h: e.matmul(PSM[psx][:, h, 0:nt], GWsl[gs][:, c0:c0 + P],
                                                                     xcr[:, h * nt:(h + 1) * nt], start=True, stop=True),
                         reads=[R("GW", gs), rXCR], writes=[R("PS", psx)])
            S.op("act", lambda e: e.activation(out=vh(sbuf_), in_=PSM[ps_r][:, 0:nh, 0:nt], func=AF.Tanh,
                                               bias=dc(DC_HBA + n), scale=0.5),
                 reads=[R("PS", ps_r), R("DChb")], writes=[rS])
            S.op("act", lambda e: e.activation(out=vh(ub), in_=PSM[ps_i][:, 0:nh, 0:nt], func=AF.Tanh,
                                               bias=dc(DC_HBX + n), scale=0.5),
                 reads=[R("PS", ps_i), R("DChb")], writes=[rU])
            S.op("act", lambda e: e.activation(out=ab[:, 0:ncols], in_=sbuf_[:, 0:ncols], func=AF.Exp,
                                               bias=dc(DC_CH + n), scale=dc(DC_CH + n)),
                 reads=[rS, R("DCch")], writes=[rA])
            S.op("act", lambda e: e.activation(out=sbuf_[:, 0:ncols], in_=sbuf_[:, 0:ncols], func=AF.Exp,
                                               bias=dc(DC_C + n), scale=dc(DC_C + n)),
                 reads=[rS, R("DCc")], writes=[rS])
            S.op("act", lambda e: e.activation(out=sbuf_[:, 0:ncols], in_=sbuf_[:, 0:ncols], func=AF.Ln,
                                               bias=dc(DC_ONE), scale=-1.0),
                 reads=[rS, R("DCo")], writes=[rS])
            S.op("act", lambda e: e.activation(out=sbuf_[:, 0:ncols], in_=sbuf_[:, 0:ncols], func=AF.Exp, scale=0.5),
                 reads=[rS], writes=[rS])
            S.op("dve", lambda e: e.scalar_tensor_tensor(out=ub[:, 0:ncols], in0=ub[:, 0:ncols], scalar=1.0,
                                                         in1=xcr[:, 0:ncols], op0=ALU.add, op1=ALU.mult),
                 reads=[rU, rXCR], writes=[rU])
            S.op("dve", lambda e: e.scalar_tensor_tensor(out=ub[:, 0:ncols], in0=ub[:, 0:ncols], scalar=0.5,
                                                         in1=sbuf_[:, 0:ncols], op0=ALU.mult, op1=ALU.mult),
                 reads=[rU, rS], writes=[rU])
            if g.main and g.p == 0:
                S.op("dve", lambda e: e.tensor_scalar(out=ub[:, 0:HALO], in0=ub[:, 0:HALO], scalar1=cp(C_FLAG),
                                                      scalar2=None, op0=ALU.mult),
                     reads=[rU, rCP], writes=[rU])
            S.op("dve", lambda e: e.tensor_tensor_scan(out=xc[:, 0:npc], data0=ab[:, 0:npc], data1=ub[:, 0:npc],
                                                       initial=PS_h[:, n:n + 1], op0=ALU.mult, op1=ALU.add),
                 reads=[rA, rU, R("PS_h", n)], writes=[rXC])
            S.op("dve", lambda e: e.tensor_copy(out=PS_h[:, n:n + 1], in_=xc[:, npc - 1:npc]),
                 reads=[rXC], writes=[R("PS_h", n)])
            if g.main:
                for j in range(SQ):
                    c0 = npc + 8 * j
                    S.op("dve", lambda e, j=j, c0=c0: e.tensor_tensor_scan(
                        out=xc[:, c0:c0 + 8], data0=ab[:, c0:c0 + 8], data1=ub[:, c0:c0 + 8],
                        initial=ST_h[:, n, g.p * SQ + j:g.p * SQ + j + 1], op0=ALU.mult, op1=ALU.add),
                        reads=[rA, rU, R("ST_h", n)], writes=[rXCs])
                S.op("dve", lambda e: e.tensor_copy(out=ST_h[:, n, g.p * SQ:(g.p + 1) * SQ], in_=V(xc, npc + 7, [8, SQ])),
                     reads=[rXCs], writes=[R("ST_h", n)])
                S.op("dve", lambda e: e.tensor_tensor(out=gg[:, 0:ncols], in0=gg[:, 0:ncols], in1=xc[:, 0:ncols],
                                                      op=ALU.mult),
                     reads=[rGG, rXC, rXCs], writes=[rGG])

        def phC():
            norm_phase(g, gg, [rGG], b, C_GOA + n, out_slot)

        return [phA, phB, phC] if g.main else [phA, phB]

    def tails2(xp, rXPt, PSt, rPSt, STt, rSTt, idx, p, npc):
        nsx = npc + 2
        S.op("pool", lambda e: e.tensor_copy(out=xp[:, 0:2], in_=PSt[:, idx, :]), reads=[rPSt], writes=[rXPt])
        S.op("pool", lambda e: e.tensor_copy(out=V(xp, nsx, [10, SQ], [1, 2]),
                                             in_=V(STt, idx * 32 + p * 16, [2, SQ], [1, 2])),
             reads=[rSTt], writes=[rXPt])

    def tails2_save(xp, rXPb, rXPt, PSt, rPSt, STt, rSTt, idx, p, npc):
        nsx = npc + 2
        S.op("pool", lambda e: e.tensor_copy(out=PSt[:, idx, :], in_=xp[:, npc:npc + 2]),
             reads=[rXPb, rXPt], writes=[rPSt])
        S.op("pool", lambda e: e.tensor_copy(out=V(STt, idx * 32 + p * 16, [2, SQ], [1, 2]),
                                             in_=V(xp, nsx + 8, [10, SQ], [1, 2])),
             reads=[rXPb, rXPt], writes=[rSTt])

    def conv3(xp, xc, rXPb, rXPt, rXCb, npc, cwcol, bias_col):
        nsx = npc + 2
        cw = lambda k: cp(cwcol + k)
        if bias_col is None:
            S.op("act", lambda e: e.activation(out=xc[:, 0:npc], in_=xp[:, 0:npc], func=AF.Copy, scale=cw(0)),
                 reads=[rXPb, rXPt, rCP], writes=[rXCb])
            S.op("act", lambda e: e.activation(out=V(xc, npc, [8, SQ], [1, 8]), in_=V(xp, nsx, [10, SQ], [1, 8]),
                                               func=AF.Copy, scale=cw(0)),
                 reads=[rXPb, rXPt, rCP], writes=[rXCb])
        else:
            S.op("act", lambda e: e.activation(out=xc[:, 0:npc], in_=xp[:, 0:npc], func=AF.Identity, bias=cp(bias_col),
                                               scale=cw(0)),
                 reads=[rXPb, rXPt, rCP], writes=[rXCb])
            S.op("act", lambda e: e.activation(out=V(xc, npc, [8, SQ], [1, 8]), in_=V(xp, nsx, [10, SQ], [1, 8]),
                                               func=AF.Identity, bias=cp(bias_col), scale=cw(0)),
                 reads=[rXPb, rXPt, rCP], writes=[rXCb])
        for k in range(1, 3):
            S.op("dve", lambda e, k=k: e.scalar_tensor_tensor(out=xc[:, 0:npc], in0=xp[:, k:k + npc], scalar=cw(k),
                                                             in1=xc[:, 0:npc], op0=ALU.mult, op1=ALU.add),
                 reads=[rXPb, rXPt, rXCb, rCP], writes=[rXCb])
            S.op("dve", lambda e, k=k: e.scalar_tensor_tensor(out=V(xc, npc, [8, SQ], [1, 8]),
                                                             in0=V(xp, nsx + k, [10, SQ], [1, 8]), scalar=cw(k),
                                                             in1=V(xc, npc, [8, SQ], [1, 8]), op0=ALU.mult, op1=ALU.add),
                 reads=[rXPb, rXPt, rXCb, rCP], writes=[rXCb])

    def sconv_unit(gi, g, out_slot):
        b = bufs()
        npc, nt, ncols, nh = g.npc, g.nt, g.ncols, g.nh
        xp, xc, gg, ub = b.xp, b.xc, b.gg, b.ub
        rXP, rXPt, rXC, rGG, rU = b.rXP, b.rXPt, b.rXC, b.rGG, b.rU
        srcN = lambda k, h: N[:, k, h * nt:(h + 1) * nt]
        resN = lambda k: R("N", k)
        nsx = npc + 2
        vh = lambda t: V(t, 0, [nt, nh], [1, nt])

        def phA():
            ws_vb = load_w(wA[40 + gi], 2048)
            ps_vb = next_ps()
            mm_group(ps_vb, ws_vb, 0, P, KD, srcN, resN, nt)
            S.op("act", lambda e: e.activation(out=vh(gg), in_=PSM[ps_vb][:, :, 0:nt], func=AF.Copy),
                 reads=[R("PS", ps_vb)], writes=[rGG])
            ws_gc = load_w(wA[32 + gi], 2048)
            ps_gc = next_ps()
            mm_group(ps_gc, ws_gc, 0, P, KD, srcN, resN, nt)
            tails2(xp, rXPt, PS_sc, R("PS_sc", gi), ST_sc, R("ST_sc", gi), gi, g.p, npc)
            S.op("dve", lambda e: e.tensor_tensor(out=xp[:, 2:2 + nt], in0=PSM[ps_gc][:, 0, 0:nt], in1=gg[:, 0:nt],
                                                  op=ALU.mult),
                 reads=[R("PS", ps_gc), rGG], writes=[rXP])
            S.op("dve", lambda e: e.tensor_tensor(out=xp[:, 2 + nt:2 + npc], in0=PSM[ps_gc][:, 1, 0:npc - nt],
                                                  in1=gg[:, nt:npc], op=ALU.mult),
                 reads=[R("PS", ps_gc), rGG], writes=[rXP])
            S.op("dve", lambda e: e.tensor_tensor(out=V(xp, nsx + 2, [10, SQ], [1, 8]),
                                                  in0=V(PSM[ps_gc], 512 + npc - nt, [8, SQ], [1, 8]),
                                                  in1=V(gg, npc, [8, SQ], [1, 8]), op=ALU.mult),
                 reads=[R("PS", ps_gc), rGG], writes=[rXP])
            tails2_save(xp, rXP, rXPt, PS_sc, R("PS_sc", gi), ST_sc, R("ST_sc", gi), gi, g.p, npc)
            ws_gb = load_w(wA[24 + gi], 2048)
            ps_gb = next_ps()
            mm_group(ps_gb, ws_gb, 0, P, KD, srcN, resN, nt)
            S.op("act", lambda e: e.activation(out=vh(ub), in_=PSM[ps_gb][:, :, 0:nt], func=AF.Copy),
                 reads=[R("PS", ps_gb)], writes=[rU])
            conv3(xp, xc, rXP, rXPt, rXC, npc, C_CBW + gi * 3, None)
            S.op("dve", lambda e: e.tensor_tensor(out=gg[:, 0:ncols], in0=xc[:, 0:ncols], in1=ub[:, 0:ncols], op=ALU.mult),
                 reads=[rXC, rU, rGG], writes=[rGG])

        def phB():
            pass

        def phC():
            norm_phase(g, gg, [rGG], b, C_GOB + gi, out_slot)

        return [phA, phB, phC]

    def wo_partial(grp, nk, g):
        nt = g.nt
        half = (grp % 2) * YG
        for dd in range(4):
            ws = load_w(wO[grp * 4 + dd][:, 0:nk * 512], nk * 512)
            for d4 in range(4):
                d = dd * 4 + d4
                ps = next_ps()
                mm_group(ps, ws, d4 * P, 512, nk, lambda k, h: YH[:, half + k, h * nt:(h + 1) * nt],
                         lambda k: R("YH", half + k), nt)
                S.op("dve", lambda e, d=d, ps=ps: e.tensor_tensor(out=V(X, d * NC, [nt, 2], [1, nt]),
                                                                in0=PSM[ps][:, :, 0:nt],
                                                                in1=V(X, d * NC, [nt, 2], [1, nt]), op=ALU.add),
                     reads=[R("PS", ps), R("X", d)], writes=[R("X", d)])

    def ffn_unit(j, g, out_slot):
        b = bufs()
        npc, nt, ncols, nh = g.npc, g.nt, g.ncols, g.nh
        xp, xc, gg = b.xp, b.xc, b.gg
        rXP, rXPt, rXC, rGG = b.rXP, b.rXPt, b.rXC, b.rGG
        srcN = lambda k, h: N[:, k, h * nt:(h + 1) * nt]
        resN = lambda k: R("N", k)
        nsx = npc + 2
        vh = lambda t: V(t, 0, [nt, nh], [1, nt])

        def phA():
            ws_g = load_w(wU[j], 2048)
            ps_g = next_ps()
            mm_group(ps_g, ws_g, 0, P, KD, srcN, resN, nt)
            tails2(xp, rXPt, PS_fc, R("PS_fc", j), ST_fc, R("ST_fc", j), j, g.p, npc)
            S.op("act", lambda e: e.activation(out=xp[:, 2:2 + nt], in_=PSM[ps_g][:, 0, 0:nt], func=AF.Copy),
                 reads=[R("PS", ps_g)], writes=[rXP])
            S.op("act", lambda e: e.activation(out=xp[:, 2 + nt:2 + npc], in_=PSM[ps_g][:, 1, 0:npc - nt], func=AF.Copy),
                 reads=[R("PS", ps_g)], writes=[rXP])
            S.op("act", lambda e: e.activation(out=V(xp, nsx + 2, [10, SQ], [1, 8]),
                                               in_=V(PSM[ps_g], 512 + npc - nt, [8, SQ], [1, 8]), func=AF.Copy),
                 reads=[R("PS", ps_g)], writes=[rXP])
            tails2_save(xp, rXP, rXPt, PS_fc, R("PS_fc", j), ST_fc, R("ST_fc", j), j, g.p, npc)
            ws_v = load_w(wU[NF + j], 2048)
            ps_v = next_ps()
            mm_group(ps_v, ws_v, 0, P, KD, srcN, resN, nt)
            S.op("act", lambda e: e.activation(out=vh(gg), in_=PSM[ps_v][:, :, 0:nt], func=AF.Copy),
                 reads=[R("PS", ps_v)], writes=[rGG])
            conv3(xp, xc, rXP, rXPt, rXC, npc, C_CFW + j * 3, C_CFB + j)

        def phB():
            S.op("act", lambda e: e.activation(out=xc[:, 0:ncols], in_=xc[:, 0:ncols], func=AF.Gelu_apprx_tanh),
                 reads=[rXC], writes=[rXC])
            S.op("dve", lambda e: e.tensor_tensor(out=YH[:, out_slot, 0:ncols], in0=gg[:, 0:ncols], in1=xc[:, 0:ncols],
                                                  op=ALU.mult),
                 reads=[rGG, rXC], writes=[R("YH", out_slot)])

        return [phA, phB]

    def down_partial(grp, g):
        nt = g.nt
        half = (grp % 2) * YG
        for dd in range(4):
            ws = load_w(wD[grp * 4 + dd], 2048)
            for d4 in range(4):
                d = dd * 4 + d4
                ps = next_ps()
                mm_group(ps, ws, d4 * P, 512, YG, lambda k, h: YH[:, half + k, h * nt:(h + 1) * nt],
                         lambda k: R("YH", half + k), nt)
                S.op("dve", lambda e, d=d, ps=ps: e.tensor_tensor(out=V(X, d * NC, [nt, 2], [1, nt]),
                                                                in0=PSM[ps][:, :, 0:nt],
                                                                in1=V(X, d * NC, [nt, 2], [1, nt]), op=ALU.add),
                     reads=[R("PS", ps), R("X", d)], writes=[R("X", d)])

    def run_pipelined(units, on_done=None, lags=(0, 2, 3)):
        n = len(units)
        for t in range(n + max(lags)):
            done = []
            for ph in range(len(lags)):
                u = t - lags[ph]
                if 0 <= u < n and ph < len(units[u]):
                    units[u][ph]()
                    if ph == len(units[u]) - 1:
                        done.append(u)
            if on_done is not None:
                for u in done:
                    on_done(u)

    def store_y(p):
        tiles = [(i * P, P) for i in range(4)] + [(512, 72)]
        for ti, (c0, nr) in enumerate(tiles):
            s = nxt("io", NIO)
            for kh in range(2):
                ps = next_ps()
                for i in range(8):
                    k = kh * 8 + i
                    S.op("pe", lambda e, k=k, i=i, ps=ps, c0=c0, nr=nr: e.transpose(
                        out=VP(PSM[ps], i * P, nr, [1, P]), in_=X[:, k, c0:c0 + nr], identity=IDT[:]),
                        reads=[R("X", k), rIDT], writes=[R("PS", ps)])
                copy_op(evac_eng(), IO[s][0:nr, kh * 1024:(kh + 1) * 1024], VP(PSM[ps], 0, nr, [1, 1024]),
                        [R("PS", ps)], [R("IO", s)])
            if ti < 4:
                S.dma(STQ, lambda e, s=s, c0=c0, p=p: e.dma_start(out=y_p[p * PM + c0:p * PM + c0 + P, :], in_=IO[s][:, :]),
                      reads=[R("IO", s)], lane=("io", s))
            else:
                S.dma(STQ, lambda e, s=s, p=p: e.dma_start(out=y_p[p * PM + 512:p * PM + 520, :], in_=IO[s][0:8, :]),
                      reads=[R("IO", s)], lane=("io", s))
                S.dma(STQ, lambda e, s=s, p=p: e.dma_start(out=y_s[p * NS:(p + 1) * NS, :], in_=IO[s][8:72, :]),
                      reads=[R("IO", s)], lane=("io", s))

    def store_states():
        def tr_out(src_fn, nblk, nrows, dst_ap_fn, width):
            s = nxt("io", NIO)
            done = 0
            while done < nblk:
                nb = min(8, nblk - done)
                ps = next_ps()
                for i in range(nb):
                    src, rres = src_fn(done + i)
                    S.op("pe", lambda e, i=i, ps=ps, src=src: e.transpose(out=VP(PSM[ps], i * P, nrows, [1, P]),
                                                                        in_=src, identity=IDT[:]),
                         reads=[rres, rIDT], writes=[R("PS", ps)])
                copy_op(evac_eng(), IO[s][0:nrows, done * P:(done + nb) * P], VP(PSM[ps], 0, nrows, [1, nb * P]),
                        [R("PS", ps)], [R("IO", s)])
                done += nb
            dst_ap_fn(s)

        tr_out(lambda n: (ST_h[:, n, :], R("ST_h", n)), NH, 16,
               lambda s: S.dma(STQ, lambda e: e.dma_start(out=o_sh, in_=IO[s][0:16, 0:DA]), reads=[R("IO", s)],
                               lane=("io", s)), DA)
        tr_out(lambda n: (ST_rc[:, n, :], R("ST_rc", n)), NH, 48,
               lambda s: S.dma(STQ, lambda e: e.dma_start(out=o_src, in_=IO[s][0:48, 0:DA]), reads=[R("IO", s)],
                               lane=("io", s)), DA)
        tr_out(lambda gi: (ST_sc[:, gi, :], R("ST_sc", gi)), NG, 32,
               lambda s: S.dma(STQ, lambda e: e.dma_start(out=o_ssc, in_=IO[s][0:32, 0:DB]), reads=[R("IO", s)],
                               lane=("io", s)), DB)
        for q in range(3):
            tr_out(lambda j, q=q: (ST_fc[:, q * 16 + j, :], R("ST_fc", q * 16 + j)), 16, 32,
                   lambda s, q=q: S.dma(STQ, lambda e: e.dma_start(out=o_sfc[:, q * 2048:(q + 1) * 2048],
                                                                      in_=IO[s][0:32, :]),
                                        reads=[R("IO", s)], lane=("io", s)), 2048)
        for (src, nrows, dst, res) in ((PS_h[:, :], NH, o_ph, [R("PS_h", n) for n in range(NH)]),
                                       (V(PS_rc, 0, [1, NH * 3]), NH * 3, o_prc, [R("PS_rc", n) for n in range(NH)]),
                                       (V(PS_sc, 0, [1, NG * 2]), NG * 2, o_psc, [R("PS_sc", n) for n in range(NG)]),
                                       (V(PS_fc, 0, [1, NF * 2]), NF * 2, o_pfc, [R("PS_fc", n) for n in range(NF)])):
            s = nxt("io", NIO)
            ps = next_ps()
            S.op("pe", lambda e, ps=ps, src=src, nrows=nrows: e.transpose(out=VP(PSM[ps], 0, nrows, [1, P]), in_=src,
                                                                         identity=IDT[:]),
                 reads=res + [rIDT], writes=[R("PS", ps)])
            copy_op(evac_eng(), IO[s][0:nrows, 0:P], VP(PSM[ps], 0, nrows, [1, P]), [R("PS", ps)], [R("IO", s)])
            S.dma(STQ, lambda e, s=s, nrows=nrows, dst=dst: e.dma_start(out=dst, in_=IO[s][0:nrows, 0:P]),
                  reads=[R("IO", s)], lane=("io", s))

    load_states()

    gp = Geom()
    gp.npc, gp.nt, gp.ncols, gp.nh, gp.main, gp.p = 512, 512, 512, 1, False, 0
    for q in range(2):
        load_x_tiles([(i * P, P, [(0, P, xw[q * 512 + i * P:q * 512 + (i + 1) * P, :])]) for i in range(4)])
        rmsnorm_fm(512, 512, C_GMIX)
        run_pipelined([lru_unit(n, gp, None) for n in range(NH)], lags=(0, 2))
    S.op("dve", lambda e: e.tensor_scalar(out=PS_h[:], in0=PS_h[:], scalar1=cp(C_FLAG), scalar2=None, op0=ALU.mult),
         reads=[R("PS_h", n) for n in range(NH)] + [rCP], writes=[R("PS_h", n) for n in range(NH)])

    for p in range(2):
        g = Geom()
        g.npc, g.nt, g.ncols, g.nh, g.main, g.p = PM, NT, NC, 2, True, p
        base = NPRE + p * PM
        tiles = [(i * P, P, [(0, P, xw[base + i * P:base + (i + 1) * P, :])]) for i in range(4)]
        tiles.append((512, 72, [(0, 8, xw[base + 512:base + 520, :]), (8, 64, xs[p * NS:(p + 1) * NS, :])]))
        load_x_tiles(tiles)
        rmsnorm_fm(NC, NT, C_GMIX)
        units = []
        for c in range(NMIX):
            slot = c % (2 * YG)
            units.append(lru_unit(c, g, slot) if c < NH else sconv_unit(c - NH, g, slot))

        def mix_done(u, g=g):
            if u % YG == YG - 1:
                wo_partial(u // YG, YG, g)
        run_pipelined(units, mix_done)
        rmsnorm_fm(NC, NT, C_GFFN)
        funits = [ffn_unit(j, g, j % (2 * YG)) for j in range(NF)]

        def ffn_done(u, g=g):
            if u % YG == YG - 1:
                down_partial(u // YG, g)
        run_pipelined(funits, ffn_done, lags=(0, 1))
        rmsnorm_fm(NC, NT, C_GFIN, rounded=False)
        store_y(p)
    store_states()

    S.emit(nc, es)
    es.close()
    return nc


_PROG = {}


def _tile_cols(w, ncb):
    K = w.shape[0]
    return np.ascontiguousarray(w.reshape(K // P, P, ncb, P).transpose(2, 1, 0, 3)).reshape(ncb, P, (K // P) * P)


def _tile_rows(w, gk):
    K = w.shape[0]
    ng = K // (gk * P)
    a = w.reshape(ng, gk, P, 4, 512).transpose(0, 3, 2, 1, 4)
    return np.ascontiguousarray(a).reshape(ng * 4, P, gk * 512)


def _fm(v, n):
    return np.ascontiguousarray(v.reshape(n, P).T)


def kernel(x_prompt, x_sample, state_lru_h, state_lru_conv, state_sconv, state_ffn_conv, meta_tokens, g_mix, w_in,
           conv_a_w, conv_a_b, w_gate_a, b_gate_a, w_gate_x, b_gate_x, lru_lambda, conv_b_w, g_out_a, g_out_b, w_o,
           g_ffn, w_up, conv_f_w, conv_f_b, w_down, g_final):
    f32 = np.float32
    A = lambda a: np.asarray(a, dtype=f32)
    x_prompt, x_sample = A(x_prompt), A(x_sample)
    import os
    stage = os.environ.get("KSTAGE")
    key = ("nc", stage)
    if key not in _PROG:
        _PROG[key] = build_program(stop_after=None if stage is None else int(stage))
    nc = _PROG[key]
    wA_ = _tile_cols(A(w_in)[0], 48)
    wU_ = _tile_cols(A(w_up)[0], 96)
    wO_ = _tile_rows(A(w_o)[0], YG)
    wD_ = _tile_rows(A(w_down)[0], YG)
    wG_ = np.ascontiguousarray(np.concatenate([A(w_gate_a)[0], A(w_gate_x)[0]], axis=2))
    cpar = np.zeros((P, NCPAR), f32)
    cpar[:, C_GMIX:C_GMIX + 16] = _fm(A(g_mix)[0], 16)
    cpar[:, C_GFFN:C_GFFN + 16] = _fm(A(g_ffn)[0], 16)
    cpar[:, C_GFIN:C_GFIN + 16] = _fm(A(g_final), 16)
    cpar[:, C_CAB:C_CAB + 12] = _fm(A(conv_a_b)[0], 12)
    cpar[:, C_BGA:C_BGA + 12] = _fm(A(b_gate_a)[0], 12)
    cpar[:, C_BGX:C_BGX + 12] = _fm(A(b_gate_x)[0], 12)
    cpar[:, C_LAM:C_LAM + 12] = _fm(A(lru_lambda)[0], 12)
    cpar[:, C_GOA:C_GOA + 12] = _fm(A(g_out_a)[0], 12)
    cpar[:, C_GOB:C_GOB + 8] = _fm(A(g_out_b)[0], 8)
    cpar[:, C_CFB:C_CFB + 48] = _fm(A(conv_f_b)[0], 48)
    caw = A(conv_a_w)[0]
    cpar[:, C_CAW:C_CAW + 48] = caw.reshape(4, NH, P).transpose(2, 1, 0).reshape(P, 48)
    cbw = A(conv_b_w)[0]
    cpar[:, C_CBW:C_CBW + 24] = cbw.reshape(3, NG, P).transpose(2, 1, 0).reshape(P, 24)
    cfw = A(conv_f_w)[0]
    cpar[:, C_CFW:C_CFW + 144] = cfw.reshape(3, NF, P).transpose(2, 1, 0).reshape(P, 144)
    ident = np.eye(P, dtype=f32)
    meta = A(meta_tokens)
    in_maps = []
    for c in range(NCORES):
        b, half = c // 2, c % 2
        seq = np.concatenate([meta, x_prompt[b]], axis=0)
        if half == 0:
            xw_ = np.concatenate([np.zeros((1032, D), f32), seq[0:1032]], axis=0)
        else:
            xw_ = seq
        cp_c = cpar.copy()
        cp_c[:, C_FLAG] = float(half)
        sl = slice(16 * c, 16 * c + 16)
        st_hr = np.concatenate([A(state_lru_h)[0, sl], A(state_lru_conv)[0, sl].reshape(48, DA)], axis=0)
        in_maps.append({
            "xw": np.ascontiguousarray(xw_), "xs": np.ascontiguousarray(x_sample[sl].reshape(128, D)),
            "st_hr": np.ascontiguousarray(st_hr),
            "st_sc": np.ascontiguousarray(A(state_sconv)[0, sl].reshape(32, DB)),
            "st_fc": np.ascontiguousarray(A(state_ffn_conv)[0, sl].reshape(32, DFF)),
            "cpar": cp_c, "ident": ident, "wA": wA_, "wG": wG_, "wO": wO_, "wU": wU_, "wD": wD_,
        })
    res = run_bass_kernel_spmd(nc, in_maps, core_ids=list(range(NCORES)))
    r = res.results
    B = x_prompt.shape[0]
    y_prompt = np.zeros((B, 2048, D), f32)
    y_sample = np.zeros((128, 8, D), f32)
    p_h = np.zeros((1, B, DA), f32)
    p_rc = np.zeros((1, B, 3, DA), f32)
    p_sc = np.zeros((1, B, 2, DB), f32)
    p_fc = np.zeros((1, B, 2, DFF), f32)
    s_h = np.zeros((1, 128, DA), f32)
    s_rc = np.zeros((1, 128, 3, DA), f32)
    s_sc = np.zeros((1, 128, 2, DB), f32)
    s_fc = np.zeros((1, 128, 2, DFF), f32)
    for c in range(NCORES):
        b, half = c // 2, c % 2
        yp = r[c]["y_p"][HALO:]
        if half == 0:
            y_prompt[b, 0:1016] = yp[16:1032]
        else:
            y_prompt[b, 1016:2048] = yp
            p_h[0, b] = r[c]["o_ph"].reshape(DA)
            p_rc[0, b] = r[c]["o_prc"].reshape(NH, 3, P).transpose(1, 0, 2).reshape(3, DA)
            p_sc[0, b] = r[c]["o_psc"].reshape(NG, 2, P).transpose(1, 0, 2).reshape(2, DB)
            p_fc[0, b] = r[c]["o_pfc"].reshape(NF, 2, P).transpose(1, 0, 2).reshape(2, DFF)
        sl = slice(16 * c, 16 * c + 16)
        y_sample[sl] = r[c]["y_s"].reshape(16, 8, D)
        s_h[0, sl] = r[c]["o_sh"]
        s_rc[0, sl] = r[c]["o_src"].reshape(16, 3, DA)
        s_sc[0, sl] = r[c]["o_ssc"].reshape(16, 2, DB)
        s_fc[0, sl] = r[c]["o_sfc"].reshape(16, 2, DFF)
    return (y_prompt, y_sample, p_h, p_rc, p_sc, p_fc, s_h, s_rc, s_sc, s_fc)
```

```python
import numpy as np
from contextlib import ExitStack
import concourse.bass as bass
import concourse.mybir as mybir
from concourse.ap import AP
from concourse.bass_utils import run_bass_kernel_spmd

F32 = mybir.dt.float32
F32R = mybir.dt.float32r
AF = mybir.ActivationFunctionType
ALU = mybir.AluOpType

NCORES = 8
P = 128
D = 2048
KD = 16
DA = 1536
NH = 12
DB = 1024
NG = 8
DFF = 6144
NF = 48
NMIX = 20
DIN = 2 * DA + 3 * DB
HALO = 8
NPRE = 1024
NMAIN = 1040
PM = 520
SQ = 8
NS = 64
NC = 584
NT = 292
WBW = 616
YG = 4
EPS = 1e-6

C_GMIX, C_GFFN, C_GFIN = 0, 16, 32
C_CAB, C_BGA, C_BGX, C_LAM, C_GOA, C_GOB, C_CFB = 48, 60, 72, 84, 96, 108, 116
C_CAW, C_CBW, C_CFW, C_FLAG = 164, 212, 236, 380
NCPAR = 384
DC_HBA, DC_HBX, DC_C, DC_CH, DC_EPS, DC_ONE = 0, 12, 24, 36, 48, 49
NDC = 64


class Res:
    __slots__ = ("name", "w", "r")

    def __init__(self, name):
        self.name = name
        self.w = None
        self.r = []


class Op:
    __slots__ = ("eng", "fn", "deps", "lane", "lane_idx", "sig", "tick", "is_dma")


class Sched:
    ENGS = ("pe", "act", "dve", "pool", "sp")

    def __init__(self):
        self.streams = {e: [] for e in self.ENGS}
        self.lanes = {}
        self.resd = {}

    def R(self, *key):
        r = self.resd.get(key)
        if r is None:
            r = Res(key)
            self.resd[key] = r
        return r

    def _rec(self, op, reads, writes):
        deps = {}
        for r in reads:
            if r.w is not None:
                deps[id(r.w)] = (r.w, True)
        for w in writes:
            if w.w is not None and id(w.w) not in deps:
                deps[id(w.w)] = (w.w, False)
            for rd in w.r:
                if id(rd) not in deps:
                    deps[id(rd)] = (rd, False)
        fin = []
        for p, raw in deps.values():
            if p is op:
                continue
            if (not p.is_dma) and (not op.is_dma) and p.eng == op.eng and not raw:
                continue
            fin.append(p)
            if not p.is_dma:
                p.sig = True
        op.deps = fin
        for r in reads:
            r.r.append(op)
        for w in writes:
            w.w = op
            w.r = []
        self.streams[op.eng].append(op)

    def op(self, eng, fn, reads=(), writes=()):
        if any(r.name[0] == "PS" for r in reads):
            writes = list(writes) + [r for r in reads if r.name[0] == "PS"]
            reads = [r for r in reads if r.name[0] != "PS"]
        o = Op()
        o.eng = eng
        o.fn = fn
        o.is_dma = False
        o.sig = False
        o.tick = None
        o.lane = None
        o.lane_idx = None
        self._rec(o, reads, writes)
        return o

    def dma(self, queue, fn, reads=(), writes=(), lane=None, bulk=False):
        o = Op()
        o.eng = queue
        o.fn = fn
        o.is_dma = True
        o.sig = True
        o.tick = None
        ln = self.lanes.setdefault(lane, [0, bulk])
        o.lane = lane
        o.lane_idx = ln[0]
        ln[0] += 1
        self._rec(o, reads, writes)
        return o

    def emit(self, nc, es):
        for e in self.ENGS:
            t = 0
            for o in self.streams[e]:
                if not o.is_dma and o.sig:
                    t += 1
                    o.tick = t
        esem = {e: es.enter_context(nc.semaphore("sem_" + e)) for e in self.ENGS if e != "sp"}
        lsem = {ln: es.enter_context(nc.semaphore("lane_" + str(ln))) for ln in self.lanes}
        store_lanes = {}
        for e in self.ENGS:
            for o in self.streams[e]:
                if o.is_dma:
                    store_lanes.setdefault(e, set()).add(o.lane)

        def run_stream(ename, eng):
            waited = {}
            for o in self.streams[ename]:
                need = {}
                for p in o.deps:
                    if p.is_dma:
                        cnt, bulk = self.lanes[p.lane]
                        val = 16 * (cnt if bulk else (p.lane_idx + 1))
                        key = ("l", p.lane)
                        sem = lsem[p.lane]
                    else:
                        val = p.tick
                        key = ("e", p.eng)
                        sem = esem[p.eng]
                    if need.get(key, (None, 0))[1] < val:
                        need[key] = (sem, val)
                for key, (sem, val) in need.items():
                    if waited.get(key, 0) >= val:
                        continue
                    eng.wait_ge(sem, val)
                    waited[key] = val
                ins = o.fn(eng)
                if o.is_dma:
                    ins.then_inc(lsem[o.lane], 16)
                elif o.sig:
                    ins.then_inc(esem[ename], 1)
            for ln in sorted(store_lanes.get(ename, ()), key=str):
                val = 16 * self.lanes[ln][0]
                if waited.get(("l", ln), 0) < val:
                    eng.wait_ge(lsem[ln], val)

        block = es.enter_context(nc.Block())

        @block.sync
        def _(e):
            run_stream("sp", e)

        @block.tensor
        def _(e):
            run_stream("pe", e)

        @block.scalar
        def _(e):
            run_stream("act", e)

        @block.vector
        def _(e):
            run_stream("dve", e)

        @block.gpsimd
        def _(e):
            run_stream("pool", e)


def build_program(stop_after=None, dbg=False):
    nc = bass.Bass("TRN2", target_bir_lowering=False)
    nc.dge_precook = False
    S = Sched()
    R = S.R
    es = ExitStack()
    import os as _os
    STQ = _os.environ.get("KSTQ", "pool")

    def din(name, shape, dt=F32):
        return nc.dram_tensor(name, shape, dt, kind="ExternalInput").ap()

    def dout(name, shape, dt=F32):
        return nc.dram_tensor(name, shape, dt, kind="ExternalOutput").ap()

    xw = din("xw", [NPRE + NMAIN, D])
    xs = din("xs", [128, D])
    st_hr = din("st_hr", [64, DA])
    st_sc = din("st_sc", [32, DB])
    st_fc = din("st_fc", [32, DFF])
    cpar = din("cpar", [P, NCPAR])
    identd = din("ident", [P, P])
    wA = din("wA", [48, P, 2048], F32R)
    wG = din("wG", [NH, P, 256], F32R)
    wO = din("wO", [20, P, 2048], F32R)
    wU = din("wU", [96, P, 2048], F32R)
    wD = din("wD", [48, P, 2048], F32R)
    y_p = dout("y_p", [NMAIN, D])
    y_s = dout("y_s", [128, D])
    o_sh = dout("o_sh", [16, DA])
    o_src = dout("o_src", [48, DA])
    o_ssc = dout("o_ssc", [32, DB])
    o_sfc = dout("o_sfc", [32, DFF])
    o_ph = dout("o_ph", [NH, P])
    o_prc = dout("o_prc", [NH * 3, P])
    o_psc = dout("o_psc", [NG * 2, P])
    o_pfc = dout("o_pfc", [NF * 2, P])

    def sb(name, shape, dt=F32):
        return es.enter_context(nc.sbuf_tensor(name, shape, dt))

    X = sb("X", [P, KD, NC])
    N = sb("N", [P, KD, NC], F32R)
    YH = sb("YH", [P, 2 * YG, NC], F32R)
    NWS = 4
    Wsl = [sb(f"W{i}", [P, 2048], F32R) for i in range(NWS)]
    NGS = 4
    GWsl = [sb(f"GW{i}", [P, 256], F32R) for i in range(NGS)]
    NIO = 2
    IO = [sb(f"IO{i}", [P, 2048]) for i in range(NIO)]
    XP = [sb(f"XP{i}", [P, WBW]) for i in range(2)]
    XC = [sb(f"XC{i}", [P, WBW]) for i in range(3)]
    GG = [sb(f"GG{i}", [P, WBW]) for i in range(4)]
    SB_ = [sb(f"SB{i}", [P, WBW]) for i in range(2)]
    AB = [sb(f"AB{i}", [P, WBW]) for i in range(2)]
    UB = [sb(f"UB{i}", [P, WBW]) for i in range(2)]
    SQB = [sb(f"SQB{i}", [P, NC], F32R) for i in range(2)]
    XCR = [sb(f"XCR{i}", [P, NC], F32R) for i in range(3)]
    RB = sb("RB", [P, NC])
    CP = sb("CP", [P, NCPAR])
    DC = sb("DC", [P, NDC])
    IDT = sb("IDT", [P, P])
    ONES = sb("ONES", [P, P], F32R)
    ST_h = sb("ST_h", [P, NH, 16])
    ST_rc = sb("ST_rc", [P, NH, 48])
    ST_sc = sb("ST_sc", [P, NG, 32])
    ST_fc = sb("ST_fc", [P, NF, 32])
    PS_h = sb("PS_h", [P, NH])
    PS_rc = sb("PS_rc", [P, NH, 3])
    PS_sc = sb("PS_sc", [P, NG, 2])
    PS_fc = sb("PS_fc", [P, NF, 2])
    NPS = 4
    PSM = [es.enter_context(nc.psum_tensor(f"PSM{i}", [P, 2, 512], F32)) for i in range(NPS)]

    def pst(t):
        return t[:].ap[0][0]

    def V(t, off, *dims, dt=None):
        a = AP(t, off, [[pst(t), P]] + [list(d) for d in dims])
        if dt is not None:
            a = a.bitcast(dt)
        return a

    def VP(t, off, npart, *dims):
        return AP(t, off, [[pst(t), npart]] + [list(d) for d in dims])

    cnt = {"w": 0, "g": 0, "io": 0, "ps": 0, "sq": 0}

    def nxt(k, n):
        i = cnt[k] % n
        cnt[k] += 1
        return i

    def load_w(src_ap, ncols):
        s = nxt("w", NWS)
        S.dma("sp", lambda e, s=s: e.dma_start(out=Wsl[s][:, 0:ncols], in_=src_ap),
              writes=[R("W", s)], lane=("w", s))
        return s

    def next_ps():
        return nxt("ps", NPS)

    cp = lambda c0, n=1: CP[:, c0:c0 + n]
    dc = lambda c0, n=1: DC[:, c0:c0 + n]
    rCP, rDC, rIDT, rONES = R("CP"), R("DC"), R("IDT"), R("ONES")

    S.dma("sp", lambda e: e.dma_start(out=CP[:], in_=cpar), writes=[rCP], lane="const", bulk=True)
    S.dma("sp", lambda e: e.dma_start(out=IDT[:], in_=identd), writes=[rIDT], lane="const", bulk=True)
    S.op("dve", lambda e: e.memset(RB[:, 0:P], 1.0), writes=[R("RB")])
    S.op("dve", lambda e: e.tensor_copy(out=ONES[:], in_=RB[:, 0:P]), reads=[R("RB")], writes=[rONES])
    S.op("dve", lambda e: e.memset(DC[:, DC_EPS:DC_EPS + 1], EPS), writes=[R("DCe")])
    S.op("dve", lambda e: e.memset(DC[:, DC_ONE:DC_ONE + 1], 1.0), writes=[R("DCo")])
    S.op("dve", lambda e: e.memset(PS_h[:], 0.0), writes=[R("PS_h", n) for n in range(NH)])
    S.op("dve", lambda e: e.memset(PS_rc[:], 0.0), writes=[R("PS_rc", n) for n in range(NH)])
    S.op("dve", lambda e: e.memset(PS_sc[:], 0.0), writes=[R("PS_sc", g) for g in range(NG)])
    S.op("dve", lambda e: e.memset(PS_fc[:], 0.0), writes=[R("PS_fc", j) for j in range(NF)])
    S.op("dve", lambda e: e.tensor_scalar(out=dc(DC_HBA, 24), in0=cp(C_BGA, 24), scalar1=0.5, scalar2=None,
                                          op0=ALU.mult), reads=[rCP], writes=[R("DChb")])
    if "c_act" not in _os.environ.get("KSKIP", "").split(","):
        S.op("act", lambda e: e.activation(out=dc(DC_C, 12), in_=cp(C_LAM, 12), func=AF.Exp, scale=-1.0),
             reads=[rCP], writes=[R("DCc")])
        S.op("act", lambda e: e.activation(out=dc(DC_C, 12), in_=dc(DC_C, 12), func=AF.Ln, bias=dc(DC_ONE), scale=1.0),
             reads=[R("DCc"), R("DCo")], writes=[R("DCc")])
    S.op("dve", lambda e: e.tensor_scalar(out=dc(DC_CH, 12), in0=dc(DC_C, 12), scalar1=-4.0, scalar2=None,
                                          op0=ALU.mult), reads=[R("DCc")], writes=[R("DCch")])
    S.op("dve", lambda e: e.tensor_scalar(out=dc(DC_C, 12), in0=dc(DC_C, 12), scalar1=-8.0, scalar2=None,
                                          op0=ALU.mult), reads=[R("DCc"), R("DCch")], writes=[R("DCc")])
    rCONST = [rCP, R("DChb"), R("DCc"), R("DCch"), R("DCe"), R("DCo")]

    flip = [0]

    def evac_eng():
        flip[0] ^= 1
        return "act" if flip[0] else "dve"

    def copy_op(eng, out, in_, reads, writes):
        if eng == "act":
            S.op("act", lambda e: e.activation(out=out, in_=in_, func=AF.Copy), reads=reads, writes=writes)
        else:
            S.op(eng, lambda e: e.tensor_copy(out=out, in_=in_), reads=reads, writes=writes)

    def load_states():
        KS = _os.environ.get("KSKIP", "").split(",")
        if "hr" in KS:
            return
        s = nxt("io", NIO)
        S.dma("sp", lambda e: e.dma_start(out=IO[s][0:64, 0:DA], in_=st_hr), writes=[R("IO", s)], lane=("io", s))
        ps = next_ps()
        for n in range(NH):
            S.op("pe", lambda e, n=n: e.transpose(out=V(PSM[ps], n * 64, [1, 64]), in_=IO[s][0:64, n * P:(n + 1) * P],
                                                  identity=IDT[0:64, 0:64]),
                 reads=[R("IO", s), rIDT], writes=[R("PS", ps)])
        copy_op("act", ST_h[:], V(PSM[ps], 0, [64, NH], [1, 16]), [R("PS", ps)], [R("ST_h", n) for n in range(NH)])
        if "rc" not in KS:
            copy_op("dve", ST_rc[:], V(PSM[ps], 16, [64, NH], [1, 48]), [R("PS", ps)], [R("ST_rc", n) for n in range(NH)])
        if "sc" in KS:
            return
        s2 = nxt("io", NIO)
        S.dma("sp", lambda e: e.dma_start(out=IO[s2][0:32, 0:DB], in_=st_sc), writes=[R("IO", s2)], lane=("io", s2))
        ps2 = next_ps()
        for g in range(NG):
            S.op("pe", lambda e, g=g: e.transpose(out=V(PSM[ps2], g * 64, [1, 64]), in_=IO[s2][0:64, g * P:(g + 1) * P],
                                                  identity=IDT[0:64, 0:64]),
                 reads=[R("IO", s2), rIDT], writes=[R("PS", ps2)])
        copy_op("act", ST_sc[:], V(PSM[ps2], 0, [64, NG], [1, 32]), [R("PS", ps2)], [R("ST_sc", g) for g in range(NG)])
        if "fc" in KS:
            return
        for q in range(3):
            s3 = nxt("io", NIO)
            S.dma("sp", lambda e, q=q, s3=s3: e.dma_start(out=IO[s3][0:32, :], in_=st_fc[:, q * 2048:(q + 1) * 2048]),
                  writes=[R("IO", s3)], lane=("io", s3))
            for hh in range(2):
                ps3 = next_ps()
                for i in range(8):
                    ii = hh * 8 + i
                    S.op("pe", lambda e, i=i, ii=ii, s3=s3, ps3=ps3: e.transpose(out=V(PSM[ps3], i * 64, [1, 64]),
                                                                                 in_=IO[s3][0:64, ii * P:(ii + 1) * P],
                                                                                 identity=IDT[0:64, 0:64]),
                         reads=[R("IO", s3), rIDT], writes=[R("PS", ps3)])
                j0 = q * 16 + hh * 8
                copy_op(evac_eng(), ST_fc[:, j0:j0 + 8, :], V(PSM[ps3], 0, [64, 8], [1, 32]), [R("PS", ps3)],
                        [R("ST_fc", j) for j in range(j0, j0 + 8)])

    def load_x_tiles(tiles):
        for (col0, nr, parts) in tiles:
            s = nxt("io", NIO)
            for (r0, n_, src) in parts:
                S.dma("sp", lambda e, s=s, r0=r0, n_=n_, src=src: e.dma_start(out=IO[s][r0:r0 + n_, :], in_=src),
                      writes=[R("IO", s)], lane=("io", s))
            for kh in range(2):
                ps = next_ps()
                for i in range(8):
                    k = kh * 8 + i
                    S.op("pe", lambda e, s=s, k=k, i=i, ps=ps, nr=nr: e.transpose(
                        out=V(PSM[ps], i * P, [1, nr]), in_=IO[s][0:nr, k * P:(k + 1) * P], identity=IDT[0:nr, 0:nr]),
                        reads=[R("IO", s), rIDT], writes=[R("PS", ps)])
                copy_op(evac_eng(), X[:, kh * 8:kh * 8 + 8, col0:col0 + nr], V(PSM[ps], 0, [P, 8], [1, nr]),
                        [R("PS", ps)], [R("X", k) for k in range(kh * 8, kh * 8 + 8)])

    def rmsnorm_fm(ncols, nt, gcol, rounded=True):
        nh = ncols // nt
        ps = next_ps()
        for k in range(KD):
            q = nxt("sq", 2)
            S.op("act", lambda e, k=k, q=q: e.activation(out=SQB[q][:, 0:ncols], in_=X[:, k, 0:ncols], func=AF.Square),
                 reads=[R("X", k)], writes=[R("SQ", q)])
            for h in range(nh):
                S.op("pe", lambda e, k=k, q=q, h=h: e.matmul(PSM[ps][:, h, 0:nt], ONES[:], SQB[q][:, h * nt:(h + 1) * nt],
                                                           start=(k == 0), stop=(k == KD - 1)),
                     reads=[R("SQ", q), rONES], writes=[R("PS", ps)])
        S.op("act", lambda e: e.activation(out=V(RB, 0, [nt, nh], [1, nt]), in_=PSM[ps][:, 0:nh, 0:nt], func=AF.Ln,
                                           bias=dc(DC_EPS), scale=1.0 / D),
             reads=[R("PS", ps), R("DCe")], writes=[R("RB")])
        S.op("act", lambda e: e.activation(out=RB[:, 0:ncols], in_=RB[:, 0:ncols], func=AF.Exp, scale=-0.5),
             reads=[R("RB")], writes=[R("RB")])
        for k in range(KD):
            o = N[:, k, 0:ncols] if rounded else X[:, k, 0:ncols]
            S.op("dve", lambda e, k=k, o=o: e.scalar_tensor_tensor(out=o, in0=X[:, k, 0:ncols], scalar=cp(gcol + k),
                                                                 in1=RB[:, 0:ncols], op0=ALU.mult, op1=ALU.mult),
                 reads=[R("X", k), R("RB"), rCP], writes=[R("N", k) if rounded else R("X", k)])

    def mm_group(ps, wslot, wcol0, wkstride, nk, src_fn, src_res, nt, nhalf=2):
        for k in range(nk):
            for h in range(nhalf):
                S.op("pe", lambda e, k=k, h=h: e.matmul(PSM[ps][:, h, 0:nt],
                                                       Wsl[wslot][:, k * wkstride + wcol0:k * wkstride + wcol0 + P],
                                                       src_fn(k, h), start=(k == 0), stop=(k == nk - 1)),
                     reads=[R("W", wslot), src_res(k)], writes=[R("PS", ps)])

    class Geom:
        pass

    ucnt = [0]

    def bufs():
        u = ucnt[0]
        ucnt[0] += 1
        b = Geom()
        i2, i3, i4 = u % 2, u % 3, u % 4
        b.xp, b.rXP, b.rXPt = XP[i2], R("XP", i2), R("XPt", i2)
        b.xcr, b.rXCR = XCR[i3], R("XCR", i3)
        b.sb, b.rS = SB_[i2], R("SBf", i2)
        b.ab, b.rA = AB[i2], R("AB", i2)
        b.ub, b.rU = UB[i2], R("UB", i2)
        b.xc, b.rXC, b.rXCs = XC[i3], R("XC", i3), R("XCs", i3)
        b.gg, b.rGG = GG[i4], R("GG", i4)
        return b

    def norm_phase(g, yb, yres, b, gcol, out_slot):
        nt, ncols, nh = g.nt, g.ncols, g.nh
        q = nxt("sq", 2)
        sqb = SQB[q]
        S.op("pool", lambda e: e.tensor_tensor(out=sqb[:, 0:ncols], in0=yb[:, 0:ncols], in1=yb[:, 0:ncols], op=ALU.mult),
             reads=yres, writes=[R("SQ", q)])
        ps = next_ps()
        for h in range(nh):
            S.op("pe", lambda e, h=h: e.matmul(PSM[ps][:, h, 0:nt], ONES[:], sqb[:, h * nt:(h + 1) * nt],
                                               start=True, stop=True),
                 reads=[R("SQ", q), rONES], writes=[R("PS", ps)])
        S.op("act", lambda e: e.activation(out=V(b.ab, 0, [nt, nh], [1, nt]), in_=PSM[ps][:, 0:nh, 0:nt], func=AF.Ln,
                                           bias=dc(DC_EPS), scale=1.0 / P),
             reads=[R("PS", ps), R("DCe")], writes=[b.rA])
        S.op("act", lambda e: e.activation(out=b.ab[:, 0:ncols], in_=b.ab[:, 0:ncols], func=AF.Exp, scale=-0.5),
             reads=[b.rA], writes=[b.rA])
        S.op("dve", lambda e: e.scalar_tensor_tensor(out=YH[:, out_slot, 0:ncols], in0=yb[:, 0:ncols], scalar=cp(gcol),
                                                     in1=b.ab[:, 0:ncols], op0=ALU.mult, op1=ALU.mult),
             reads=yres + [b.rA, rCP], writes=[R("YH", out_slot)])

    def lru_unit(n, g, out_slot):
        b = bufs()
        npc, nt, ncols, nh = g.npc, g.nt, g.ncols, g.nh
        xp, xc, xcr, gg, sbuf_, ab, ub = b.xp, b.xc, b.xcr, b.gg, b.sb, b.ab, b.ub
        rXP, rXPt, rXC, rXCs, rXCR, rGG, rS, rA, rU = b.rXP, b.rXPt, b.rXC, b.rXCs, b.rXCR, b.rGG, b.rS, b.rA, b.rU
        nsx = npc + 3
        st = Geom()
        srcN = lambda k, h: N[:, k, h * nt:(h + 1) * nt]
        resN = lambda k: R("N", k)
        cw = lambda k: cp(C_CAW + n * 4 + k)
        vh = lambda t: V(t, 0, [nt, nh], [1, nt])

        def phA():
            ws_xa = load_w(wA[n], 2048)
            ps_xa = next_ps()
            mm_group(ps_xa, ws_xa, 0, P, KD, srcN, resN, nt, nh)
            if g.main:
                ws_ga = load_w(wA[NH + n], 2048)
                ps_ga = next_ps()
                mm_group(ps_ga, ws_ga, 0, P, KD, srcN, resN, nt, nh)
            st.gs = nxt("g", NGS)
            gs = st.gs
            S.dma("sp", lambda e: e.dma_start(out=GWsl[gs][:], in_=wG[n]), writes=[R("GW", gs)], lane=("g", gs))
            S.op("pool", lambda e: e.tensor_copy(out=xp[:, 0:3], in_=PS_rc[:, n, :]), reads=[R("PS_rc", n)], writes=[rXPt])
            if g.main:
                S.op("pool", lambda e: e.tensor_copy(out=V(xp, nsx, [11, SQ], [1, 3]),
                                                     in_=V(ST_rc, n * 48 + g.p * 24, [3, SQ], [1, 3])),
                     reads=[R("ST_rc", n)], writes=[rXPt])
                S.op("act", lambda e: e.activation(out=xp[:, 3:3 + nt], in_=PSM[ps_xa][:, 0, 0:nt], func=AF.Copy),
                     reads=[R("PS", ps_xa)], writes=[rXP])
                S.op("act", lambda e: e.activation(out=xp[:, 3 + nt:3 + npc], in_=PSM[ps_xa][:, 1, 0:npc - nt],
                                                   func=AF.Copy),
                     reads=[R("PS", ps_xa)], writes=[rXP])
                S.op("act", lambda e: e.activation(out=V(xp, nsx + 3, [11, SQ], [1, 8]),
                                                   in_=V(PSM[ps_xa], 512 + npc - nt, [8, SQ], [1, 8]), func=AF.Copy),
                     reads=[R("PS", ps_xa)], writes=[rXP])
            else:
                S.op("act", lambda e: e.activation(out=V(xp, 3, [nt, nh], [1, nt]), in_=PSM[ps_xa][:, 0:nh, 0:nt],
                                                   func=AF.Copy),
                     reads=[R("PS", ps_xa)], writes=[rXP])
            S.op("pool", lambda e: e.tensor_copy(out=PS_rc[:, n, :], in_=xp[:, npc:npc + 3]),
                 reads=[rXP, rXPt], writes=[R("PS_rc", n)])
            if g.main:
                S.op("pool", lambda e: e.tensor_copy(out=V(ST_rc, n * 48 + g.p * 24, [3, SQ], [1, 3]),
                                                     in_=V(xp, nsx + 8, [11, SQ], [1, 3])),
                     reads=[rXP, rXPt], writes=[R("ST_rc", n)])
            S.op("dve", lambda e: e.tensor_scalar(out=xc[:, 0:npc], in0=xp[:, 0:npc], scalar1=cw(0), scalar2=cp(C_CAB + n),
                                                  op0=ALU.mult, op1=ALU.add),
                 reads=[rXP, rXPt, rCP], writes=[rXC])
            for k in range(1, 4):
                o = xcr[:, 0:npc] if k == 3 else xc[:, 0:npc]
                S.op("dve", lambda e, k=k, o=o: e.scalar_tensor_tensor(out=o, in0=xp[:, k:k + npc], scalar=cw(k),
                                                                     in1=xc[:, 0:npc], op0=ALU.mult, op1=ALU.add),
                     reads=[rXP, rXPt, rXC, rCP], writes=[rXCR if k == 3 else rXC])
            if g.main:
                S.op("dve", lambda e: e.tensor_scalar(out=V(xc, npc, [8, SQ], [1, 8]), in0=V(xp, nsx, [11, SQ], [1, 8]),
                                                      scalar1=cw(0), scalar2=cp(C_CAB + n), op0=ALU.mult, op1=ALU.add),
                     reads=[rXP, rXPt, rCP], writes=[rXCs])
                for k in range(1, 4):
                    S.op("dve", lambda e, k=k: e.scalar_tensor_tensor(
                        out=V(xcr if k == 3 else xc, npc, [8, SQ], [1, 8]), in0=V(xp, nsx + k, [11, SQ], [1, 8]),
                        scalar=cw(k), in1=V(xc, npc, [8, SQ], [1, 8]), op0=ALU.mult, op1=ALU.add),
                        reads=[rXP, rXPt, rXCs, rCP], writes=[rXCR if k == 3 else rXCs])
                S.op("act", lambda e: e.activation(out=vh(gg), in_=PSM[ps_ga][:, 0:nh, 0:nt], func=AF.Gelu_apprx_tanh),
                     reads=[R("PS", ps_ga)], writes=[rGG])

        def phB():
            gs = st.gs
            ps_r = next_ps()
            ps_i = next_ps()
            for (psx, c0) in ((ps_r, 0), (ps_i, P)):
                for h in range(nh):
                    S.op("pe", lambda e, psx=psx, c0=c0, h=h: e.matmul(PSM[psx][:, h, 0:nt], GWsl[gs][:, c0:c0 + P],
                                                                     xcr[:, h * nt:(h + 1) * nt], start=True, stop=True),
                         reads=[R("GW", gs), rXCR], writes=[R("PS", psx)])
            S.op("act", lambda e: e.activation(out=vh(sbuf_), in_=PSM[ps_r][:, 0:nh, 0:nt], func=AF.Tanh,
                                               bias=dc(DC_HBA + n), scale=0.5),
                 reads=[R("PS", ps_r), R("DChb")], writes=[rS])
            S.op("act", lambda e: e.activation(out=vh(ub), in_=PSM[ps_i][:, 0:nh, 0:nt], func=AF.Tanh,
                                               bias=dc(DC_HBX + n), scale=0.5),
                 reads=[R("PS", ps_i), R("DChb")], writes=[rU])
            S.op("act", lambda e: e.activation(out=ab[:, 0:ncols], in_=sbuf_[:, 0:ncols], func=AF.Exp,
                                               bias=dc(DC_CH + n), scale=dc(DC_CH + n)),
                 reads=[rS, R("DCch")], writes=[rA])
            S.op("act", lambda e: e.activation(out=sbuf_[:, 0:ncols], in_=sbuf_[:, 0:ncols], func=AF.Exp,
                                               bias=dc(DC_C + n), scale=dc(DC_C + n)),
                 reads=[rS, R("DCc")], writes=[rS])
            S.op("act", lambda e: e.activation(out=sbuf_[:, 0:ncols], in_=sbuf_[:, 0:ncols], func=AF.Ln,
                                               bias=dc(DC_ONE), scale=-1.0),
                 reads=[rS, R("DCo")], writes=[rS])
            S.op("act", lambda e: e.activation(out=sbuf_[:, 0:ncols], in_=sbuf_[:, 0:ncols], func=AF.Exp, scale=0.5),
                 reads=[rS], writes=[rS])
            S.op("dve", lambda e: e.scalar_tensor_tensor(out=ub[:, 0:ncols], in0=ub[:, 0:ncols], scalar=1.0,
                                                         in1=xcr[:, 0:ncols], op0=ALU.add, op1=ALU.mult),
                 reads=[rU, rXCR], writes=[rU])
            S.op("dve", lambda e: e.scalar_tensor_tensor(out=ub[:, 0:ncols], in0=ub[:, 0:ncols], scalar=0.5,
                                                         in1=sbuf_[:, 0:ncols], op0=ALU.mult, op1=ALU.mult),
                 reads=[rU, rS], writes=[rU])
            if g.main and g.p == 0:
                S.op("dve", lambda e: e.tensor_scalar(out=ub[:, 0:HALO], in0=ub[:, 0:HALO], scalar1=cp(C_FLAG),
                                                      scalar2=None, op0=ALU.mult),
                     reads=[rU, rCP], writes=[rU])
            S.op("dve", lambda e: e.tensor_tensor_scan(out=xc[:, 0:npc], data0=ab[:, 0:npc], data1=ub[:, 0:npc],
                                                       initial=PS_h[:, n:n + 1], op0=ALU.mult, op1=ALU.add),
                 reads=[rA, rU, R("PS_h", n)], writes=[rXC])
            S.op("dve", lambda e: e.tensor_copy(out=PS_h[:, n:n + 1], in_=xc[:, npc - 1:npc]),
                 reads=[rXC], writes=[R("PS_h", n)])
            if g.main:
                for j in range(SQ):
                    c0 = npc + 8 * j
                    S.op("dve", lambda e, j=j, c0=c0: e.tensor_tensor_scan(
                        out=xc[:, c0:c0 + 8], data0=ab[:, c0:c0 + 8], data1=ub[:, c0:c0 + 8],
                        initial=ST_h[:, n, g.p * SQ + j:g.p * SQ + j + 1], op0=ALU.mult, op1=ALU.add),
                        reads=[rA, rU, R("ST_h", n)], writes=[rXCs])
                S.op("dve", lambda e: e.tensor_copy(out=ST_h[:, n, g.p * SQ:(g.p + 1) * SQ], in_=V(xc, npc + 7, [8, SQ])),
                     reads=[rXCs], writes=[R("ST_h", n)])
                S.op("dve", lambda e: e.tensor_tensor(out=gg[:, 0:ncols], in0=gg[:, 0:ncols], in1=xc[:, 0:ncols],
                                                      op=ALU.mult),
                     reads=[rGG, rXC, rXCs], writes=[rGG])

        def phC():
            norm_phase(g, gg, [rGG], b, C_GOA + n, out_slot)

        return [phA, phB, phC] if g.main else [phA, phB]

    def tails2(xp, rXPt, PSt, rPSt, STt, rSTt, idx, p, npc):
        nsx = npc + 2
        S.op("pool", lambda e: e.tensor_copy(out=xp[:, 0:2], in_=PSt[:, idx, :]), reads=[rPSt], writes=[rXPt])
        S.op("pool", lambda e: e.tensor_copy(out=V(xp, nsx, [10, SQ], [1, 2]),
                                             in_=V(STt, idx * 32 + p * 16, [2, SQ], [1, 2])),
             reads=[rSTt], writes=[rXPt])

    def tails2_save(xp, rXPb, rXPt, PSt, rPSt, STt, rSTt, idx, p, npc):
        nsx = npc + 2
        S.op("pool", lambda e: e.tensor_copy(out=PSt[:, idx, :], in_=xp[:, npc:npc + 2]),
             reads=[rXPb, rXPt], writes=[rPSt])
        S.op("pool", lambda e: e.tensor_copy(out=V(STt, idx * 32 + p * 16, [2, SQ], [1, 2]),
                                             in_=V(xp, nsx + 8, [10, SQ], [1, 2])),
             reads=[rXPb, rXPt], writes=[rSTt])

    def conv3(xp, xc, rXPb, rXPt, rXCb, npc, cwcol, bias_col):
        nsx = npc + 2
        cw = lambda k: cp(cwcol + k)
        if bias_col is None:
            S.op("act", lambda e: e.activation(out=xc[:, 0:npc], in_=xp[:, 0:npc], func=AF.Copy, scale=cw(0)),
                 reads=[rXPb, rXPt, rCP], writes=[rXCb])
            S.op("act", lambda e: e.activation(out=V(xc, npc, [8, SQ], [1, 8]), in_=V(xp, nsx, [10, SQ], [1, 8]),
                                               func=AF.Copy, scale=cw(0)),
                 reads=[rXPb, rXPt, rCP], writes=[rXCb])
        else:
            S.op("act", lambda e: e.activation(out=xc[:, 0:npc], in_=xp[:, 0:npc], func=AF.Identity, bias=cp(bias_col),
                                               scale=cw(0)),
                 reads=[rXPb, rXPt, rCP], writes=[rXCb])
            S.op("act", lambda e: e.activation(out=V(xc, npc, [8, SQ], [1, 8]), in_=V(xp, nsx, [10, SQ], [1, 8]),
                                               func=AF.Identity, bias=cp(bias_col), scale=cw(0)),
                 reads=[rXPb, rXPt, rCP], writes=[rXCb])
        for k in range(1, 3):
            S.op("dve", lambda e, k=k: e.scalar_tensor_tensor(out=xc[:, 0:npc], in0=xp[:, k:k + npc], scalar=cw(k),
                                                             in1=xc[:, 0:npc], op0=ALU.mult, op1=ALU.add),
                 reads=[rXPb, rXPt, rXCb, rCP], writes=[rXCb])
            S.op("dve", lambda e, k=k: e.scalar_tensor_tensor(out=V(xc, npc, [8, SQ], [1, 8]),
                                                             in0=V(xp, nsx + k, [10, SQ], [1, 8]), scalar=cw(k),
                                                             in1=V(xc, npc, [8, SQ], [1, 8]), op0=ALU.mult, op1=ALU.add),
                 reads=[rXPb, rXPt, rXCb, rCP], writes=[rXCb])

    def sconv_unit(gi, g, out_slot):
        b = bufs()
        npc, nt, ncols, nh = g.npc, g.nt, g.ncols, g.nh
        xp, xc, gg, ub = b.xp, b.xc, b.gg, b.ub
        rXP, rXPt, rXC, rGG, rU = b.rXP, b.rXPt, b.rXC, b.rGG, b.rU
        srcN = lambda k, h: N[:, k, h * nt:(h + 1) * nt]
        resN = lambda k: R("N", k)
        nsx = npc + 2
        vh = lambda t: V(t, 0, [nt, nh], [1, nt])

        def phA():
            ws_vb = load_w(wA[40 + gi], 2048)
            ps_vb = next_ps()
            mm_group(ps_vb, ws_vb, 0, P, KD, srcN, resN, nt)
            S.op("act", lambda e: e.activation(out=vh(gg), in_=PSM[ps_vb][:, :, 0:nt], func=AF.Copy),
                 reads=[R("PS", ps_vb)], writes=[rGG])
            ws_gc = load_w(wA[32 + gi], 2048)
            ps_gc = next_ps()
            mm_group(ps_gc, ws_gc, 0, P, KD, srcN, resN, nt)
            tails2(xp, rXPt, PS_sc, R("PS_sc", gi), ST_sc, R("ST_sc", gi), gi, g.p, npc)
            S.op("dve", lambda e: e.tensor_tensor(out=xp[:, 2:2 + nt], in0=PSM[ps_gc][:, 0, 0:nt], in1=gg[:, 0:nt],
                                                  op=ALU.mult),
                 reads=[R("PS", ps_gc), rGG], writes=[rXP])
            S.op("dve", lambda e: e.tensor_tensor(out=xp[:, 2 + nt:2 + npc], in0=PSM[ps_gc][:, 1, 0:npc - nt],
                                                  in1=gg[:, nt:npc], op=ALU.mult),
                 reads=[R("PS", ps_gc), rGG], writes=[rXP])
            S.op("dve", lambda e: e.tensor_tensor(out=V(xp, nsx + 2, [10, SQ], [1, 8]),
                                                  in0=V(PSM[ps_gc], 512 + npc - nt, [8, SQ], [1, 8]),
                                                  in1=V(gg, npc, [8, SQ], [1, 8]), op=ALU.mult),
                 reads=[R("PS", ps_gc), rGG], writes=[rXP])
            tails2_save(xp, rXP, rXPt, PS_sc, R("PS_sc", gi), ST_sc, R("ST_sc", gi), gi, g.p, npc)
            ws_gb = load_w(wA[24 + gi], 2048)
            ps_gb = next_ps()
            mm_group(ps_gb, ws_gb, 0, P, KD, srcN, resN, nt)
            S.op("act", lambda e: e.activation(out=vh(ub), in_=PSM[ps_gb][:, :, 0:nt], func=AF.Copy),
                 reads=[R("PS", ps_gb)], writes=[rU])
            conv3(xp, xc, rXP, rXPt, rXC, npc, C_CBW + gi * 3, None)
            S.op("dve", lambda e: e.tensor_tensor(out=gg[:, 0:ncols], in0=xc[:, 0:ncols], in1=ub[:, 0:ncols], op=ALU.mult),
                 reads=[rXC, rU, rGG], writes=[rGG])

        def phB():
            pass

        def phC():
            norm_phase(g, gg, [rGG], b, C_GOB + gi, out_slot)

        return [phA, phB, phC]

    def wo_partial(grp, nk, g):
        nt = g.nt
        half = (grp % 2) * YG
        for dd in range(4):
            ws = load_w(wO[grp * 4 + dd][:, 0:nk * 512], nk * 512)
            for d4 in range(4):
                d = dd * 4 + d4
                ps = next_ps()
                mm_group(ps, ws, d4 * P, 512, nk, lambda k, h: YH[:, half + k, h * nt:(h + 1) * nt],
                         lambda k: R("YH", half + k), nt)
                S.op("dve", lambda e, d=d, ps=ps: e.tensor_tensor(out=V(X, d * NC, [nt, 2], [1, nt]),
                                                                in0=PSM[ps][:, :, 0:nt],
                                                                in1=V(X, d * NC, [nt, 2], [1, nt]), op=ALU.add),
                     reads=[R("PS", ps), R("X", d)], writes=[R("X", d)])

    def ffn_unit(j, g, out_slot):
        b = bufs()
        npc, nt, ncols, nh = g.npc, g.nt, g.ncols, g.nh
        xp, xc, gg = b.xp, b.xc, b.gg
        rXP, rXPt, rXC, rGG = b.rXP, b.rXPt, b.rXC, b.rGG
        srcN = lambda k, h: N[:, k, h * nt:(h + 1) * nt]
        resN = lambda k: R("N", k)
        nsx = npc + 2
        vh = lambda t: V(t, 0, [nt, nh], [1, nt])

        def phA():
            ws_g = load_w(wU[j], 2048)
            ps_g = next_ps()
            mm_group(ps_g, ws_g, 0, P, KD, srcN, resN, nt)
            tails2(xp, rXPt, PS_fc, R("PS_fc", j), ST_fc, R("ST_fc", j), j, g.p, npc)
            S.op("act", lambda e: e.activation(out=xp[:, 2:2 + nt], in_=PSM[ps_g][:, 0, 0:nt], func=AF.Copy),
                 reads=[R("PS", ps_g)], writes=[rXP])
            S.op("act", lambda e: e.activation(out=xp[:, 2 + nt:2 + npc], in_=PSM[ps_g][:, 1, 0:npc - nt], func=AF.Copy),
                 reads=[R("PS", ps_g)], writes=[rXP])
            S.op("act", lambda e: e.activation(out=V(xp, nsx + 2, [10, SQ], [1, 8]),
                                               in_=V(PSM[ps_g], 512 + npc - nt, [8, SQ], [1, 8]), func=AF.Copy),
                 reads=[R("PS", ps_g)], writes=[rXP])
            tails2_save(xp, rXP, rXPt, PS_fc, R("PS_fc", j), ST_fc, R("ST_fc", j), j, g.p, npc)
            ws_v = load_w(wU[NF + j], 2048)
            ps_v = next_ps()
            mm_group(ps_v, ws_v, 0, P, KD, srcN, resN, nt)
            S.op("act", lambda e: e.activation(out=vh(gg), in_=PSM[ps_v][:, :, 0:nt], func=AF.Copy),
                 reads=[R("PS", ps_v)], writes=[rGG])
            conv3(xp, xc, rXP, rXPt, rXC, npc, C_CFW + j * 3, C_CFB + j)

        def phB():
            S.op("act", lambda e: e.activation(out=xc[:, 0:ncols], in_=xc[:, 0:ncols], func=AF.Gelu_apprx_tanh),
                 reads=[rXC], writes=[rXC])
            S.op("dve", lambda e: e.tensor_tensor(out=YH[:, out_slot, 0:ncols], in0=gg[:, 0:ncols], in1=xc[:, 0:ncols],
                                                  op=ALU.mult),
                 reads=[rGG, rXC], writes=[R("YH", out_slot)])

        return [phA, phB]

    def down_partial(grp, g):
        nt = g.nt
        half = (grp % 2) * YG
        for dd in range(4):
            ws = load_w(wD[grp * 4 + dd], 2048)
            for d4 in range(4):
                d = dd * 4 + d4
                ps = next_ps()
                mm_group(ps, ws, d4 * P, 512, YG, lambda k, h: YH[:, half + k, h * nt:(h + 1) * nt],
                         lambda k: R("YH", half + k), nt)
                S.op("dve", lambda e, d=d, ps=ps: e.tensor_tensor(out=V(X, d * NC, [nt, 2], [1, nt]),
                                                                in0=PSM[ps][:, :, 0:nt],
                                                                in1=V(X, d * NC, [nt, 2], [1, nt]), op=ALU.add),
                     reads=[R("PS", ps), R("X", d)], writes=[R("X", d)])

    def run_pipelined(units, on_done=None, lags=(0, 2, 3), newest_first=True):
        n = len(units)
        for t in range(n + max(lags)):
            done = []
            for ph in (range(len(lags)) if newest_first else reversed(range(len(lags)))):
                u = t - lags[ph]
                if 0 <= u < n and ph < len(units[u]):
                    units[u][ph]()
                    if ph == len(units[u]) - 1:
                        done.append(u)
            if on_done is not None:
                for u in done:
                    on_done(u)

    def store_y(p):
        tiles = [(i * P, P) for i in range(4)] + [(512, 72)]
        for ti, (c0, nr) in enumerate(tiles):
            s = nxt("io", NIO)
            for kh in range(2):
                ps = next_ps()
                for i in range(8):
                    k = kh * 8 + i
                    S.op("pe", lambda e, k=k, i=i, ps=ps, c0=c0, nr=nr: e.transpose(
                        out=VP(PSM[ps], i * P, nr, [1, P]), in_=X[:, k, c0:c0 + nr], identity=IDT[:]),
                        reads=[R("X", k), rIDT], writes=[R("PS", ps)])
                copy_op(evac_eng(), IO[s][0:nr, kh * 1024:(kh + 1) * 1024], VP(PSM[ps], 0, nr, [1, 1024]),
                        [R("PS", ps)], [R("IO", s)])
            if ti < 4:
                S.dma(STQ, lambda e, s=s, c0=c0, p=p: e.dma_start(out=y_p[p * PM + c0:p * PM + c0 + P, :], in_=IO[s][:, :]),
                      reads=[R("IO", s)], lane=("io", s))
            else:
                S.dma(STQ, lambda e, s=s, p=p: e.dma_start(out=y_p[p * PM + 512:p * PM + 520, :], in_=IO[s][0:8, :]),
                      reads=[R("IO", s)], lane=("io", s))
                S.dma(STQ, lambda e, s=s, p=p: e.dma_start(out=y_s[p * NS:(p + 1) * NS, :], in_=IO[s][8:72, :]),
                      reads=[R("IO", s)], lane=("io", s))

    def store_states():
        def tr_out(src_fn, nblk, nrows, dst_ap_fn, width):
            s = nxt("io", NIO)
            done = 0
            while done < nblk:
                nb = min(8, nblk - done)
                ps = next_ps()
                for i in range(nb):
                    src, rres = src_fn(done + i)
                    S.op("pe", lambda e, i=i, ps=ps, src=src: e.transpose(out=VP(PSM[ps], i * P, nrows, [1, P]),
                                                                        in_=src, identity=IDT[:]),
                         reads=[rres, rIDT], writes=[R("PS", ps)])
                copy_op(evac_eng(), IO[s][0:nrows, done * P:(done + nb) * P], VP(PSM[ps], 0, nrows, [1, nb * P]),
                        [R("PS", ps)], [R("IO", s)])
                done += nb
            dst_ap_fn(s)

        tr_out(lambda n: (ST_h[:, n, :], R("ST_h", n)), NH, 16,
               lambda s: S.dma(STQ, lambda e: e.dma_start(out=o_sh, in_=IO[s][0:16, 0:DA]), reads=[R("IO", s)],
                               lane=("io", s)), DA)
        tr_out(lambda n: (ST_rc[:, n, :], R("ST_rc", n)), NH, 48,
               lambda s: S.dma(STQ, lambda e: e.dma_start(out=o_src, in_=IO[s][0:48, 0:DA]), reads=[R("IO", s)],
                               lane=("io", s)), DA)
        tr_out(lambda gi: (ST_sc[:, gi, :], R("ST_sc", gi)), NG, 32,
               lambda s: S.dma(STQ, lambda e: e.dma_start(out=o_ssc, in_=IO[s][0:32, 0:DB]), reads=[R("IO", s)],
                               lane=("io", s)), DB)
        for q in range(3):
            tr_out(lambda j, q=q: (ST_fc[:, q * 16 + j, :], R("ST_fc", q * 16 + j)), 16, 32,
                   lambda s, q=q: S.dma(STQ, lambda e: e.dma_start(out=o_sfc[:, q * 2048:(q + 1) * 2048],
                                                                      in_=IO[s][0:32, :]),
                                        reads=[R("IO", s)], lane=("io", s)), 2048)
        for (src, nrows, dst, res) in ((PS_h[:, :], NH, o_ph, [R("PS_h", n) for n in range(NH)]),
                                       (V(PS_rc, 0, [1, NH * 3]), NH * 3, o_prc, [R("PS_rc", n) for n in range(NH)]),
                                       (V(PS_sc, 0, [1, NG * 2]), NG * 2, o_psc, [R("PS_sc", n) for n in range(NG)]),
                                       (V(PS_fc, 0, [1, NF * 2]), NF * 2, o_pfc, [R("PS_fc", n) for n in range(NF)])):
            s = nxt("io", NIO)
            ps = next_ps()
            S.op("pe", lambda e, ps=ps, src=src, nrows=nrows: e.transpose(out=VP(PSM[ps], 0, nrows, [1, P]), in_=src,
                                                                         identity=IDT[:]),
                 reads=res + [rIDT], writes=[R("PS", ps)])
            copy_op(evac_eng(), IO[s][0:nrows, 0:P], VP(PSM[ps], 0, nrows, [1, P]), [R("PS", ps)], [R("IO", s)])
            S.dma(STQ, lambda e, s=s, nrows=nrows, dst=dst: e.dma_start(out=dst, in_=IO[s][0:nrows, 0:P]),
                  reads=[R("IO", s)], lane=("io", s))


    gp = Geom()
    gp.npc, gp.nt, gp.ncols, gp.nh, gp.main, gp.p = 512, 512, 512, 1, False, 0
    for q in range(2):
        load_x_tiles([(i * P, P, [(0, P, xw[q * 512 + i * P:q * 512 + (i + 1) * P, :])]) for i in range(4)])
        rmsnorm_fm(512, 512, C_GMIX)
        run_pipelined([lru_unit(n, gp, None) for n in range(NH)], lags=(0, 2))
    S.op("dve", lambda e: e.tensor_scalar(out=PS_h[:], in0=PS_h[:], scalar1=cp(C_FLAG), scalar2=None, op0=ALU.mult),
         reads=[R("PS_h", n) for n in range(NH)] + [rCP], writes=[R("PS_h", n) for n in range(NH)])

    load_states()

    for p in range(2):
        g = Geom()
        g.npc, g.nt, g.ncols, g.nh, g.main, g.p = PM, NT, NC, 2, True, p
        base = NPRE + p * PM
        tiles = [(i * P, P, [(0, P, xw[base + i * P:base + (i + 1) * P, :])]) for i in range(4)]
        tiles.append((512, 72, [(0, 8, xw[base + 512:base + 520, :]), (8, 64, xs[p * NS:(p + 1) * NS, :])]))
        load_x_tiles(tiles)
        rmsnorm_fm(NC, NT, C_GMIX)
        units = []
        for c in range(NMIX):
            slot = c % (2 * YG)
            units.append(lru_unit(c, g, slot) if c < NH else sconv_unit(c - NH, g, slot))

        def mix_done(u, g=g):
            if u % YG == YG - 1:
                wo_partial(u // YG, YG, g)
        run_pipelined(units, mix_done)
        rmsnorm_fm(NC, NT, C_GFFN)
        funits = [ffn_unit(j, g, j % (2 * YG)) for j in range(NF)]

        def ffn_done(u, g=g):
            if u % YG == YG - 1:
                down_partial(u // YG, g)
        run_pipelined(funits, ffn_done, lags=(0, 1), newest_first=False)
        rmsnorm_fm(NC, NT, C_GFIN, rounded=False)
        store_y(p)
    store_states()

    S.emit(nc, es)
    es.close()
    return nc


_PROG = {}


def _tile_cols(w, ncb):
    K = w.shape[0]
    return np.ascontiguousarray(w.reshape(K // P, P, ncb, P).transpose(2, 1, 0, 3)).reshape(ncb, P, (K // P) * P)


def _tile_rows(w, gk):
    K = w.shape[0]
    ng = K // (gk * P)
    a = w.reshape(ng, gk, P, 4, 512).transpose(0, 3, 2, 1, 4)
    return np.ascontiguousarray(a).reshape(ng * 4, P, gk * 512)


def _fm(v, n):
    return np.ascontiguousarray(v.reshape(n, P).T)


def kernel(x_prompt, x_sample, state_lru_h, state_lru_conv, state_sconv, state_ffn_conv, meta_tokens, g_mix, w_in,
           conv_a_w, conv_a_b, w_gate_a, b_gate_a, w_gate_x, b_gate_x, lru_lambda, conv_b_w, g_out_a, g_out_b, w_o,
           g_ffn, w_up, conv_f_w, conv_f_b, w_down, g_final):
    f32 = np.float32
    A = lambda a: np.asarray(a, dtype=f32)
    x_prompt, x_sample = A(x_prompt), A(x_sample)
    import os
    stage = os.environ.get("KSTAGE")
    key = ("nc", stage)
    if key not in _PROG:
        _PROG[key] = build_program(stop_after=None if stage is None else int(stage))
    nc = _PROG[key]
    wA_ = _tile_cols(A(w_in)[0], 48)
    wU_ = _tile_cols(A(w_up)[0], 96)
    wO_ = _tile_rows(A(w_o)[0], YG)
    wD_ = _tile_rows(A(w_down)[0], YG)
    wG_ = np.ascontiguousarray(np.concatenate([A(w_gate_a)[0], A(w_gate_x)[0]], axis=2))
    cpar = np.zeros((P, NCPAR), f32)
    cpar[:, C_GMIX:C_GMIX + 16] = _fm(A(g_mix)[0], 16)
    cpar[:, C_GFFN:C_GFFN + 16] = _fm(A(g_ffn)[0], 16)
    cpar[:, C_GFIN:C_GFIN + 16] = _fm(A(g_final), 16)
    cpar[:, C_CAB:C_CAB + 12] = _fm(A(conv_a_b)[0], 12)
    cpar[:, C_BGA:C_BGA + 12] = _fm(A(b_gate_a)[0], 12)
    cpar[:, C_BGX:C_BGX + 12] = _fm(A(b_gate_x)[0], 12)
    cpar[:, C_LAM:C_LAM + 12] = _fm(A(lru_lambda)[0], 12)
    cpar[:, C_GOA:C_GOA + 12] = _fm(A(g_out_a)[0], 12)
    cpar[:, C_GOB:C_GOB + 8] = _fm(A(g_out_b)[0], 8)
    cpar[:, C_CFB:C_CFB + 48] = _fm(A(conv_f_b)[0], 48)
    caw = A(conv_a_w)[0]
    cpar[:, C_CAW:C_CAW + 48] = caw.reshape(4, NH, P).transpose(2, 1, 0).reshape(P, 48)
    cbw = A(conv_b_w)[0]
    cpar[:, C_CBW:C_CBW + 24] = cbw.reshape(3, NG, P).transpose(2, 1, 0).reshape(P, 24)
    cfw = A(conv_f_w)[0]
    cpar[:, C_CFW:C_CFW + 144] = cfw.reshape(3, NF, P).transpose(2, 1, 0).reshape(P, 144)
    ident = np.eye(P, dtype=f32)
    meta = A(meta_tokens)
    in_maps = []
    for c in range(NCORES):
        b, half = c // 2, c % 2
        seq = np.concatenate([meta, x_prompt[b]], axis=0)
        if half == 0:
            xw_ = np.concatenate([np.zeros((1032, D), f32), seq[0:1032]], axis=0)
        else:
            xw_ = seq
        cp_c = cpar.copy()
        cp_c[:, C_FLAG] = float(half)
        sl = slice(16 * c, 16 * c + 16)
        st_hr = np.concatenate([A(state_lru_h)[0, sl], A(state_lru_conv)[0, sl].reshape(48, DA)], axis=0)
        in_maps.append({
            "xw": np.ascontiguousarray(xw_), "xs": np.ascontiguousarray(x_sample[sl].reshape(128, D)),
            "st_hr": np.ascontiguousarray(st_hr),
            "st_sc": np.ascontiguousarray(A(state_sconv)[0, sl].reshape(32, DB)),
            "st_fc": np.ascontiguousarray(A(state_ffn_conv)[0, sl].reshape(32, DFF)),
            "cpar": cp_c, "ident": ident, "wA": wA_, "wG": wG_, "wO": wO_, "wU": wU_, "wD": wD_,
        })
    res = run_bass_kernel_spmd(nc, in_maps, core_ids=list(range(NCORES)))
    r = res.results
    B = x_prompt.shape[0]
    y_prompt = np.zeros((B, 2048, D), f32)
    y_sample = np.zeros((128, 8, D), f32)
    p_h = np.zeros((1, B, DA), f32)
    p_rc = np.zeros((1, B, 3, DA), f32)
    p_sc = np.zeros((1, B, 2, DB), f32)
    p_fc = np.zeros((1, B, 2, DFF), f32)
    s_h = np.zeros((1, 128, DA), f32)
    s_rc = np.zeros((1, 128, 3, DA), f32)
    s_sc = np.zeros((1, 128, 2, DB), f32)
    s_fc = np.zeros((1, 128, 2, DFF), f32)
    for c in range(NCORES):
        b, half = c // 2, c % 2
        yp = r[c]["y_p"][HALO:]
        if half == 0:
            y_prompt[b, 0:1016] = yp[16:1032]
        else:
            y_prompt[b, 1016:2048] = yp
            p_h[0, b] = r[c]["o_ph"].reshape(DA)
            p_rc[0, b] = r[c]["o_prc"].reshape(NH, 3, P).transpose(1, 0, 2).reshape(3, DA)
            p_sc[0, b] = r[c]["o_psc"].reshape(NG, 2, P).transpose(1, 0, 2).reshape(2, DB)
            p_fc[0, b] = r[c]["o_pfc"].reshape(NF, 2, P).transpose(1, 0, 2).reshape(2, DFF)
        sl = slice(16 * c, 16 * c + 16)
        y_sample[sl] = r[c]["y_s"].reshape(16, 8, D)
        s_h[0, sl] = r[c]["o_sh"]
        s_rc[0, sl] = r[c]["o_src"].reshape(16, 3, DA)
        s_sc[0, sl] = r[c]["o_ssc"].reshape(16, 2, DB)
        s_fc[0, sl] = r[c]["o_sfc"].reshape(16, 2, DFF)
    return (y_prompt, y_sample, p_h, p_rc, p_sc, p_fc, s_h, s_rc, s_sc, s_fc)
```

```python
import numpy as np
from contextlib import ExitStack
import concourse.bass as bass
import concourse.mybir as mybir
from concourse.ap import AP
from concourse.bass_utils import run_bass_kernel_spmd

F32 = mybir.dt.float32
F32R = mybir.dt.float32r
AF = mybir.ActivationFunctionType
ALU = mybir.AluOpType

NCORES = 8
P = 128
D = 2048
KD = 16
DA = 1536
NH = 12
DB = 1024
NG = 8
DFF = 6144
NF = 48
NMIX = 20
DIN = 2 * DA + 3 * DB
HALO = 8
NPRE = 1024
NMAIN = 1040
PM = 520
SQ = 8
NS = 64
NC = 584
NT = 292
WBW = 616
YG = 4
EPS = 1e-6

C_GMIX, C_GFFN, C_GFIN = 0, 16, 32
C_CAB, C_BGA, C_BGX, C_LAM, C_GOA, C_GOB, C_CFB = 48, 60, 72, 84, 96, 108, 116
C_CAW, C_CBW, C_CFW, C_FLAG = 164, 212, 236, 380
NCPAR = 384
DC_HBA, DC_HBX, DC_C, DC_CH, DC_EPS, DC_ONE = 0, 12, 24, 36, 48, 49
NDC = 64


class Res:
    __slots__ = ("name", "w", "r")

    def __init__(self, name):
        self.name = name
        self.w = None
        self.r = []


class Op:
    __slots__ = ("eng", "fn", "deps", "lane", "lane_idx", "sig", "tick", "is_dma")


class Sched:
    ENGS = ("pe", "act", "dve", "pool", "sp")

    def __init__(self):
        self.streams = {e: [] for e in self.ENGS}
        self.lanes = {}
        self.resd = {}

    def R(self, *key):
        r = self.resd.get(key)
        if r is None:
            r = Res(key)
            self.resd[key] = r
        return r

    def _rec(self, op, reads, writes):
        deps = {}
        for r in reads:
            if r.w is not None:
                deps[id(r.w)] = (r.w, True)
        for w in writes:
            if w.w is not None and id(w.w) not in deps:
                deps[id(w.w)] = (w.w, False)
            for rd in w.r:
                if id(rd) not in deps:
                    deps[id(rd)] = (rd, False)
        fin = []
        for p, raw in deps.values():
            if p is op:
                continue
            if (not p.is_dma) and (not op.is_dma) and p.eng == op.eng and not raw:
                continue
            fin.append(p)
            if not p.is_dma:
                p.sig = True
        op.deps = fin
        for r in reads:
            r.r.append(op)
        for w in writes:
            w.w = op
            w.r = []
        self.streams[op.eng].append(op)

    def op(self, eng, fn, reads=(), writes=()):
        if any(r.name[0] == "PS" for r in reads):
            writes = list(writes) + [r for r in reads if r.name[0] == "PS"]
            reads = [r for r in reads if r.name[0] != "PS"]
        o = Op()
        o.eng = eng
        o.fn = fn
        o.is_dma = False
        o.sig = False
        o.tick = None
        o.lane = None
        o.lane_idx = None
        self._rec(o, reads, writes)
        return o

    def dma(self, queue, fn, reads=(), writes=(), lane=None, bulk=False):
        o = Op()
        o.eng = queue
        o.fn = fn
        o.is_dma = True
        o.sig = True
        o.tick = None
        ln = self.lanes.setdefault(lane, [0, bulk])
        o.lane = lane
        o.lane_idx = ln[0]
        ln[0] += 1
        self._rec(o, reads, writes)
        return o

    def emit(self, nc, es):
        for e in self.ENGS:
            t = 0
            for o in self.streams[e]:
                if not o.is_dma and o.sig:
                    t += 1
                    o.tick = t
        esem = {e: es.enter_context(nc.semaphore("sem_" + e)) for e in self.ENGS if e != "sp"}
        lsem = {ln: es.enter_context(nc.semaphore("lane_" + str(ln))) for ln in self.lanes}
        store_lanes = {}
        for e in self.ENGS:
            for o in self.streams[e]:
                if o.is_dma:
                    store_lanes.setdefault(e, set()).add(o.lane)

        def run_stream(ename, eng):
            waited = {}
            for o in self.streams[ename]:
                need = {}
                for p in o.deps:
                    if p.is_dma:
                        cnt, bulk = self.lanes[p.lane]
                        val = 16 * (cnt if bulk else (p.lane_idx + 1))
                        key = ("l", p.lane)
                        sem = lsem[p.lane]
                    else:
                        val = p.tick
                        key = ("e", p.eng)
                        sem = esem[p.eng]
                    if need.get(key, (None, 0))[1] < val:
                        need[key] = (sem, val)
                for key, (sem, val) in need.items():
                    if waited.get(key, 0) >= val:
                        continue
                    eng.wait_ge(sem, val)
                    waited[key] = val
                ins = o.fn(eng)
                if o.is_dma:
                    ins.then_inc(lsem[o.lane], 16)
                elif o.sig:
                    ins.then_inc(esem[ename], 1)
            for ln in sorted(store_lanes.get(ename, ()), key=str):
                val = 16 * self.lanes[ln][0]
                if waited.get(("l", ln), 0) < val:
                    eng.wait_ge(lsem[ln], val)

        block = es.enter_context(nc.Block())

        @block.sync
        def _(e):
            run_stream("sp", e)

        @block.tensor
        def _(e):
            run_stream("pe", e)

        @block.scalar
        def _(e):
            run_stream("act", e)

        @block.vector
        def _(e):
            run_stream("dve", e)

        @block.gpsimd
        def _(e):
            run_stream("pool", e)


def build_program(stop_after=None, dbg=False):
    nc = bass.Bass("TRN2", target_bir_lowering=False)
    nc.dge_precook = False
    S = Sched()
    R = S.R
    es = ExitStack()
    import os as _os
    STQ = _os.environ.get("KSTQ", "pool")

    def din(name, shape, dt=F32):
        return nc.dram_tensor(name, shape, dt, kind="ExternalInput").ap()

    def dout(name, shape, dt=F32):
        return nc.dram_tensor(name, shape, dt, kind="ExternalOutput").ap()

    xw = din("xw", [NPRE + NMAIN, D])
    xs = din("xs", [128, D])
    st_hr = din("st_hr", [64, DA])
    st_sc = din("st_sc", [32, DB])
    st_fc = din("st_fc", [32, DFF])
    cpar = din("cpar", [P, NCPAR])
    identd = din("ident", [P, P])
    wA = din("wA", [48, P, 2048], F32R)
    wG = din("wG", [NH, P, 256], F32R)
    wO = din("wO", [20, P, 2048], F32R)
    wU = din("wU", [96, P, 2048], F32R)
    wD = din("wD", [48, P, 2048], F32R)
    y_p = dout("y_p", [NMAIN, D])
    y_s = dout("y_s", [128, D])
    o_sh = dout("o_sh", [16, DA])
    o_src = dout("o_src", [48, DA])
    o_ssc = dout("o_ssc", [32, DB])
    o_sfc = dout("o_sfc", [32, DFF])
    o_ph = dout("o_ph", [NH, P])
    o_prc = dout("o_prc", [NH * 3, P])
    o_psc = dout("o_psc", [NG * 2, P])
    o_pfc = dout("o_pfc", [NF * 2, P])

    def sb(name, shape, dt=F32):
        return es.enter_context(nc.sbuf_tensor(name, shape, dt))

    X = sb("X", [P, KD, NC])
    N = sb("N", [P, KD, NC], F32R)
    YH = sb("YH", [P, 2 * YG, NC], F32R)
    NWS = 4
    Wsl = [sb(f"W{i}", [P, 2048], F32R) for i in range(NWS)]
    NGS = 4
    GWsl = [sb(f"GW{i}", [P, 256], F32R) for i in range(NGS)]
    NIO = 2
    IO = [sb(f"IO{i}", [P, 2048]) for i in range(NIO)]
    XP = [sb(f"XP{i}", [P, WBW]) for i in range(2)]
    XC = [sb(f"XC{i}", [P, WBW]) for i in range(3)]
    GG = [sb(f"GG{i}", [P, WBW]) for i in range(4)]
    SB_ = [sb(f"SB{i}", [P, WBW]) for i in range(2)]
    AB = [sb(f"AB{i}", [P, WBW]) for i in range(2)]
    UB = [sb(f"UB{i}", [P, WBW]) for i in range(2)]
    SQB = [sb(f"SQB{i}", [P, NC], F32R) for i in range(2)]
    XCR = [sb(f"XCR{i}", [P, NC], F32R) for i in range(3)]
    RB = sb("RB", [P, NC])
    CP = sb("CP", [P, NCPAR])
    DC = sb("DC", [P, NDC])
    IDT = sb("IDT", [P, P])
    ONES = sb("ONES", [P, P], F32R)
    ST_h = sb("ST_h", [P, NH, 16])
    ST_rc = sb("ST_rc", [P, NH, 48])
    ST_sc = sb("ST_sc", [P, NG, 32])
    ST_fc = sb("ST_fc", [P, NF, 32])
    PS_h = sb("PS_h", [P, NH])
    PS_rc = sb("PS_rc", [P, NH, 3])
    PS_sc = sb("PS_sc", [P, NG, 2])
    PS_fc = sb("PS_fc", [P, NF, 2])
    NPS = 4
    PSM = [es.enter_context(nc.psum_tensor(f"PSM{i}", [P, 2, 512], F32)) for i in range(NPS)]

    def pst(t):
        return t[:].ap[0][0]

    def V(t, off, *dims, dt=None):
        a = AP(t, off, [[pst(t), P]] + [list(d) for d in dims])
        if dt is not None:
            a = a.bitcast(dt)
        return a

    def VP(t, off, npart, *dims):
        return AP(t, off, [[pst(t), npart]] + [list(d) for d in dims])

    cnt = {"w": 0, "g": 0, "io": 0, "ps": 0, "sq": 0}

    def nxt(k, n):
        i = cnt[k] % n
        cnt[k] += 1
        return i

    def load_w(src_ap, ncols):
        s = nxt("w", NWS)
        S.dma("sp", lambda e, s=s: e.dma_start(out=Wsl[s][:, 0:ncols], in_=src_ap),
              writes=[R("W", s)], lane=("w", s))
        return s

    def next_ps():
        return nxt("ps", NPS)

    cp = lambda c0, n=1: CP[:, c0:c0 + n]
    dc = lambda c0, n=1: DC[:, c0:c0 + n]
    rCP, rDC, rIDT, rONES = R("CP"), R("DC"), R("IDT"), R("ONES")

    S.dma("sp", lambda e: e.dma_start(out=CP[:], in_=cpar), writes=[rCP], lane="const", bulk=True)
    S.dma("sp", lambda e: e.dma_start(out=IDT[:], in_=identd), writes=[rIDT], lane="const", bulk=True)
    S.op("dve", lambda e: e.memset(RB[:, 0:P], 1.0), writes=[R("RB")])
    S.op("dve", lambda e: e.tensor_copy(out=ONES[:], in_=RB[:, 0:P]), reads=[R("RB")], writes=[rONES])
    S.op("dve", lambda e: e.memset(DC[:, DC_EPS:DC_EPS + 1], EPS), writes=[R("DCe")])
    S.op("dve", lambda e: e.memset(DC[:, DC_ONE:DC_ONE + 1], 1.0), writes=[R("DCo")])
    S.op("dve", lambda e: e.memset(PS_h[:], 0.0), writes=[R("PS_h", n) for n in range(NH)])
    S.op("dve", lambda e: e.memset(PS_rc[:], 0.0), writes=[R("PS_rc", n) for n in range(NH)])
    S.op("dve", lambda e: e.memset(PS_sc[:], 0.0), writes=[R("PS_sc", g) for g in range(NG)])
    S.op("dve", lambda e: e.memset(PS_fc[:], 0.0), writes=[R("PS_fc", j) for j in range(NF)])
    S.op("dve", lambda e: e.tensor_scalar(out=dc(DC_HBA, 24), in0=cp(C_BGA, 24), scalar1=0.5, scalar2=None,
                                          op0=ALU.mult), reads=[rCP], writes=[R("DChb")])
    if "c_act" not in _os.environ.get("KSKIP", "").split(","):
        S.op("act", lambda e: e.activation(out=dc(DC_C, 12), in_=cp(C_LAM, 12), func=AF.Exp, scale=-1.0),
             reads=[rCP], writes=[R("DCc")])
        S.op("act", lambda e: e.activation(out=dc(DC_C, 12), in_=dc(DC_C, 12), func=AF.Ln, bias=dc(DC_ONE), scale=1.0),
             reads=[R("DCc"), R("DCo")], writes=[R("DCc")])
    S.op("dve", lambda e: e.tensor_scalar(out=dc(DC_CH, 12), in0=dc(DC_C, 12), scalar1=-4.0, scalar2=None,
                                          op0=ALU.mult), reads=[R("DCc")], writes=[R("DCch")])
    S.op("dve", lambda e: e.tensor_scalar(out=dc(DC_C, 12), in0=dc(DC_C, 12), scalar1=-8.0, scalar2=None,
                                          op0=ALU.mult), reads=[R("DCc"), R("DCch")], writes=[R("DCc")])
    rCONST = [rCP, R("DChb"), R("DCc"), R("DCch"), R("DCe"), R("DCo")]

    flip = [0]

    def evac_eng():
        flip[0] ^= 1
        return "act" if flip[0] else "dve"

    def copy_op(eng, out, in_, reads, writes):
        if eng == "act":
            S.op("act", lambda e: e.activation(out=out, in_=in_, func=AF.Copy), reads=reads, writes=writes)
        else:
            S.op(eng, lambda e: e.tensor_copy(out=out, in_=in_), reads=reads, writes=writes)

    def load_states():
        KS = _os.environ.get("KSKIP", "").split(",")
        if "hr" in KS:
            return
        s = nxt("io", NIO)
        S.dma("sp", lambda e: e.dma_start(out=IO[s][0:64, 0:DA], in_=st_hr), writes=[R("IO", s)], lane=("io", s))
        ps = next_ps()
        for n in range(NH):
            S.op("pe", lambda e, n=n: e.transpose(out=V(PSM[ps], n * 64, [1, 64]), in_=IO[s][0:64, n * P:(n + 1) * P],
                                                  identity=IDT[0:64, 0:64]),
                 reads=[R("IO", s), rIDT], writes=[R("PS", ps)])
        copy_op("act", ST_h[:], V(PSM[ps], 0, [64, NH], [1, 16]), [R("PS", ps)], [R("ST_h", n) for n in range(NH)])
        if "rc" not in KS:
            copy_op("dve", ST_rc[:], V(PSM[ps], 16, [64, NH], [1, 48]), [R("PS", ps)], [R("ST_rc", n) for n in range(NH)])
        if "sc" in KS:
            return
        s2 = nxt("io", NIO)
        S.dma("sp", lambda e: e.dma_start(out=IO[s2][0:32, 0:DB], in_=st_sc), writes=[R("IO", s2)], lane=("io", s2))
        ps2 = next_ps()
        for g in range(NG):
            S.op("pe", lambda e, g=g: e.transpose(out=V(PSM[ps2], g * 64, [1, 64]), in_=IO[s2][0:64, g * P:(g + 1) * P],
                                                  identity=IDT[0:64, 0:64]),
                 reads=[R("IO", s2), rIDT], writes=[R("PS", ps2)])
        copy_op("act", ST_sc[:], V(PSM[ps2], 0, [64, NG], [1, 32]), [R("PS", ps2)], [R("ST_sc", g) for g in range(NG)])
        if "fc" in KS:
            return
        for q in range(3):
            s3 = nxt("io", NIO)
            S.dma("sp", lambda e, q=q, s3=s3: e.dma_start(out=IO[s3][0:32, :], in_=st_fc[:, q * 2048:(q + 1) * 2048]),
                  writes=[R("IO", s3)], lane=("io", s3))
            for hh in range(2):
                ps3 = next_ps()
                for i in range(8):
                    ii = hh * 8 + i
                    S.op("pe", lambda e, i=i, ii=ii, s3=s3, ps3=ps3: e.transpose(out=V(PSM[ps3], i * 64, [1, 64]),
                                                                                 in_=IO[s3][0:64, ii * P:(ii + 1) * P],
                                                                                 identity=IDT[0:64, 0:64]),
                         reads=[R("IO", s3), rIDT], writes=[R("PS", ps3)])
                j0 = q * 16 + hh * 8
                copy_op(evac_eng(), ST_fc[:, j0:j0 + 8, :], V(PSM[ps3], 0, [64, 8], [1, 32]), [R("PS", ps3)],
                        [R("ST_fc", j) for j in range(j0, j0 + 8)])

    def load_x_tiles(tiles):
        for (col0, nr, parts) in tiles:
            s = nxt("io", NIO)
            for (r0, n_, src) in parts:
                S.dma("sp", lambda e, s=s, r0=r0, n_=n_, src=src: e.dma_start(out=IO[s][r0:r0 + n_, :], in_=src),
                      writes=[R("IO", s)], lane=("io", s))
            for kh in range(2):
                ps = next_ps()
                for i in range(8):
                    k = kh * 8 + i
                    S.op("pe", lambda e, s=s, k=k, i=i, ps=ps, nr=nr: e.transpose(
                        out=V(PSM[ps], i * P, [1, nr]), in_=IO[s][0:nr, k * P:(k + 1) * P], identity=IDT[0:nr, 0:nr]),
                        reads=[R("IO", s), rIDT], writes=[R("PS", ps)])
                copy_op(evac_eng(), X[:, kh * 8:kh * 8 + 8, col0:col0 + nr], V(PSM[ps], 0, [P, 8], [1, nr]),
                        [R("PS", ps)], [R("X", k) for k in range(kh * 8, kh * 8 + 8)])

    def rmsnorm_fm(ncols, nt, gcol, rounded=True):
        nh = ncols // nt
        ps = next_ps()
        scr = [(SQB[0], R("SQ", 0)), (SQB[1], R("SQ", 1)), (XCR[0], R("XCR", 0)), (XCR[1], R("XCR", 1)),
               (XCR[2], R("XCR", 2))]
        engs = ["act", "dve", "act", "pool"]
        for k in range(KD):
            sq, rsq = scr[k % len(scr)]
            eng = engs[k % len(engs)]
            if eng == "act":
                S.op("act", lambda e, k=k, sq=sq: e.activation(out=sq[:, 0:ncols], in_=X[:, k, 0:ncols], func=AF.Square),
                     reads=[R("X", k)], writes=[rsq])
            else:
                S.op(eng, lambda e, k=k, sq=sq: e.tensor_tensor(out=sq[:, 0:ncols], in0=X[:, k, 0:ncols],
                                                               in1=X[:, k, 0:ncols], op=ALU.mult),
                     reads=[R("X", k)], writes=[rsq])
            for h in range(nh):
                S.op("pe", lambda e, k=k, sq=sq, h=h: e.matmul(PSM[ps][:, h, 0:nt], ONES[:], sq[:, h * nt:(h + 1) * nt],
                                                            start=(k == 0), stop=(k == KD - 1)),
                     reads=[rsq, rONES], writes=[R("PS", ps)])
        S.op("act", lambda e: e.activation(out=V(RB, 0, [nt, nh], [1, nt]), in_=PSM[ps][:, 0:nh, 0:nt], func=AF.Ln,
                                           bias=dc(DC_EPS), scale=1.0 / D),
             reads=[R("PS", ps), R("DCe")], writes=[R("RB")])
        S.op("act", lambda e: e.activation(out=RB[:, 0:ncols], in_=RB[:, 0:ncols], func=AF.Exp, scale=-0.5),
             reads=[R("RB")], writes=[R("RB")])
        for k in range(KD):
            o = N[:, k, 0:ncols] if rounded else X[:, k, 0:ncols]
            S.op("dve", lambda e, k=k, o=o: e.scalar_tensor_tensor(out=o, in0=X[:, k, 0:ncols], scalar=cp(gcol + k),
                                                                 in1=RB[:, 0:ncols], op0=ALU.mult, op1=ALU.mult),
                 reads=[R("X", k), R("RB"), rCP], writes=[R("N", k) if rounded else R("X", k)])

    def mm_group(ps, wslot, wcol0, wkstride, nk, src_fn, src_res, nt, nhalf=2):
        for k in range(nk):
            for h in range(nhalf):
                S.op("pe", lambda e, k=k, h=h: e.matmul(PSM[ps][:, h, 0:nt],
                                                       Wsl[wslot][:, k * wkstride + wcol0:k * wkstride + wcol0 + P],
                                                       src_fn(k, h), start=(k == 0), stop=(k == nk - 1)),
                     reads=[R("W", wslot), src_res(k)], writes=[R("PS", ps)])

    class Geom:
        pass

    ucnt = [0]

    def bufs():
        u = ucnt[0]
        ucnt[0] += 1
        b = Geom()
        i2, i3, i4 = u % 2, u % 3, u % 4
        b.xp, b.rXP, b.rXPt = XP[i2], R("XP", i2), R("XPt", i2)
        b.xcr, b.rXCR = XCR[i3], R("XCR", i3)
        b.sb, b.rS = SB_[i2], R("SBf", i2)
        b.ab, b.rA = AB[i2], R("AB", i2)
        b.ub, b.rU = UB[i2], R("UB", i2)
        b.xc, b.rXC, b.rXCs = XC[i3], R("XC", i3), R("XCs", i3)
        b.gg, b.rGG = GG[i4], R("GG", i4)
        return b

    def norm_phase(g, yb, yres, b, gcol, out_slot):
        nt, ncols, nh = g.nt, g.ncols, g.nh
        q = nxt("sq", 2)
        sqb = SQB[q]
        S.op("pool", lambda e: e.tensor_tensor(out=sqb[:, 0:ncols], in0=yb[:, 0:ncols], in1=yb[:, 0:ncols], op=ALU.mult),
             reads=yres, writes=[R("SQ", q)])
        ps = next_ps()
        for h in range(nh):
            S.op("pe", lambda e, h=h: e.matmul(PSM[ps][:, h, 0:nt], ONES[:], sqb[:, h * nt:(h + 1) * nt],
                                               start=True, stop=True),
                 reads=[R("SQ", q), rONES], writes=[R("PS", ps)])
        S.op("act", lambda e: e.activation(out=V(b.ab, 0, [nt, nh], [1, nt]), in_=PSM[ps][:, 0:nh, 0:nt], func=AF.Ln,
                                           bias=dc(DC_EPS), scale=1.0 / P),
             reads=[R("PS", ps), R("DCe")], writes=[b.rA])
        S.op("act", lambda e: e.activation(out=b.ab[:, 0:ncols], in_=b.ab[:, 0:ncols], func=AF.Exp, scale=-0.5),
             reads=[b.rA], writes=[b.rA])
        S.op("dve", lambda e: e.scalar_tensor_tensor(out=YH[:, out_slot, 0:ncols], in0=yb[:, 0:ncols], scalar=cp(gcol),
                                                     in1=b.ab[:, 0:ncols], op0=ALU.mult, op1=ALU.mult),
             reads=yres + [b.rA, rCP], writes=[R("YH", out_slot)])

    def lru_unit(n, g, out_slot):
        b = bufs()
        npc, nt, ncols, nh = g.npc, g.nt, g.ncols, g.nh
        xp, xc, xcr, gg, sbuf_, ab, ub = b.xp, b.xc, b.xcr, b.gg, b.sb, b.ab, b.ub
        rXP, rXPt, rXC, rXCs, rXCR, rGG, rS, rA, rU = b.rXP, b.rXPt, b.rXC, b.rXCs, b.rXCR, b.rGG, b.rS, b.rA, b.rU
        nsx = npc + 3
        st = Geom()
        srcN = lambda k, h: N[:, k, h * nt:(h + 1) * nt]
        resN = lambda k: R("N", k)
        cw = lambda k: cp(C_CAW + n * 4 + k)
        vh = lambda t: V(t, 0, [nt, nh], [1, nt])

        def phA():
            ws_xa = load_w(wA[n], 2048)
            ps_xa = next_ps()
            mm_group(ps_xa, ws_xa, 0, P, KD, srcN, resN, nt, nh)
            if g.main:
                ws_ga = load_w(wA[NH + n], 2048)
                ps_ga = next_ps()
                mm_group(ps_ga, ws_ga, 0, P, KD, srcN, resN, nt, nh)
            st.gs = nxt("g", NGS)
            gs = st.gs
            S.dma("sp", lambda e: e.dma_start(out=GWsl[gs][:], in_=wG[n]), writes=[R("GW", gs)], lane=("g", gs))
            S.op("pool", lambda e: e.tensor_copy(out=xp[:, 0:3], in_=PS_rc[:, n, :]), reads=[R("PS_rc", n)], writes=[rXPt])
            if g.main:
                S.op("pool", lambda e: e.tensor_copy(out=V(xp, nsx, [11, SQ], [1, 3]),
                                                     in_=V(ST_rc, n * 48 + g.p * 24, [3, SQ], [1, 3])),
                     reads=[R("ST_rc", n)], writes=[rXPt])
                S.op("act", lambda e: e.activation(out=xp[:, 3:3 + nt], in_=PSM[ps_xa][:, 0, 0:nt], func=AF.Copy),
                     reads=[R("PS", ps_xa)], writes=[rXP])
                S.op("act", lambda e: e.activation(out=xp[:, 3 + nt:3 + npc], in_=PSM[ps_xa][:, 1, 0:npc - nt],
                                                   func=AF.Copy),
                     reads=[R("PS", ps_xa)], writes=[rXP])
                S.op("act", lambda e: e.activation(out=V(xp, nsx + 3, [11, SQ], [1, 8]),
                                                   in_=V(PSM[ps_xa], 512 + npc - nt, [8, SQ], [1, 8]), func=AF.Copy),
                     reads=[R("PS", ps_xa)], writes=[rXP])
            else:
                S.op("act", lambda e: e.activation(out=V(xp, 3, [nt, nh], [1, nt]), in_=PSM[ps_xa][:, 0:nh, 0:nt],
                                                   func=AF.Copy),
                     reads=[R("PS", ps_xa)], writes=[rXP])
            S.op("pool", lambda e: e.tensor_copy(out=PS_rc[:, n, :], in_=xp[:, npc:npc + 3]),
                 reads=[rXP, rXPt], writes=[R("PS_rc", n)])
            if g.main:
                S.op("pool", lambda e: e.tensor_copy(out=V(ST_rc, n * 48 + g.p * 24, [3, SQ], [1, 3]),
                                                     in_=V(xp, nsx + 8, [11, SQ], [1, 3])),
                     reads=[rXP, rXPt], writes=[R("ST_rc", n)])
            S.op("dve", lambda e: e.tensor_scalar(out=xc[:, 0:npc], in0=xp[:, 0:npc], scalar1=cw(0), scalar2=cp(C_CAB + n),
                                                  op0=ALU.mult, op1=ALU.add),
                 reads=[rXP, rXPt, rCP], writes=[rXC])
            for k in range(1, 4):
                o = xcr[:, 0:npc] if k == 3 else xc[:, 0:npc]
                S.op("dve", lambda e, k=k, o=o: e.scalar_tensor_tensor(out=o, in0=xp[:, k:k + npc], scalar=cw(k),
                                                                     in1=xc[:, 0:npc], op0=ALU.mult, op1=ALU.add),
                     reads=[rXP, rXPt, rXC, rCP], writes=[rXCR if k == 3 else rXC])
            if g.main:
                S.op("dve", lambda e: e.tensor_scalar(out=V(xc, npc, [8, SQ], [1, 8]), in0=V(xp, nsx, [11, SQ], [1, 8]),
                                                      scalar1=cw(0), scalar2=cp(C_CAB + n), op0=ALU.mult, op1=ALU.add),
                     reads=[rXP, rXPt, rCP], writes=[rXCs])
                for k in range(1, 4):
                    S.op("dve", lambda e, k=k: e.scalar_tensor_tensor(
                        out=V(xcr if k == 3 else xc, npc, [8, SQ], [1, 8]), in0=V(xp, nsx + k, [11, SQ], [1, 8]),
                        scalar=cw(k), in1=V(xc, npc, [8, SQ], [1, 8]), op0=ALU.mult, op1=ALU.add),
                        reads=[rXP, rXPt, rXCs, rCP], writes=[rXCR if k == 3 else rXCs])
                S.op("act", lambda e: e.activation(out=vh(gg), in_=PSM[ps_ga][:, 0:nh, 0:nt], func=AF.Gelu_apprx_tanh),
                     reads=[R("PS", ps_ga)], writes=[rGG])

        def phB():
            gs = st.gs
            ps_r = next_ps()
            ps_i = next_ps()
            for (psx, c0) in ((ps_r, 0), (ps_i, P)):
                for h in range(nh):
                    S.op("pe", lambda e, psx=psx, c0=c0, h=h: e.matmul(PSM[psx][:, h, 0:nt], GWsl[gs][:, c0:c0 + P],
                                                                     xcr[:, h * nt:(h + 1) * nt], start=True, stop=True),
                         reads=[R("GW", gs), rXCR], writes=[R("PS", psx)])
            S.op("act", lambda e: e.activation(out=vh(sbuf_), in_=PSM[ps_r][:, 0:nh, 0:nt], func=AF.Tanh,
                                               bias=dc(DC_HBA + n), scale=0.5),
                 reads=[R("PS", ps_r), R("DChb")], writes=[rS])
            S.op("act", lambda e: e.activation(out=vh(ub), in_=PSM[ps_i][:, 0:nh, 0:nt], func=AF.Tanh,
                                               bias=dc(DC_HBX + n), scale=0.5),
                 reads=[R("PS", ps_i), R("DChb")], writes=[rU])
            S.op("act", lambda e: e.activation(out=ab[:, 0:ncols], in_=sbuf_[:, 0:ncols], func=AF.Exp,
                                               bias=dc(DC_CH + n), scale=dc(DC_CH + n)),
                 reads=[rS, R("DCch")], writes=[rA])
            S.op("act", lambda e: e.activation(out=sbuf_[:, 0:ncols], in_=sbuf_[:, 0:ncols], func=AF.Exp,
                                               bias=dc(DC_C + n), scale=dc(DC_C + n)),
                 reads=[rS, R("DCc")], writes=[rS])
            S.op("act", lambda e: e.activation(out=sbuf_[:, 0:ncols], in_=sbuf_[:, 0:ncols], func=AF.Ln,
                                               bias=dc(DC_ONE), scale=-1.0),
                 reads=[rS, R("DCo")], writes=[rS])
            S.op("act", lambda e: e.activation(out=sbuf_[:, 0:ncols], in_=sbuf_[:, 0:ncols], func=AF.Exp, scale=0.5),
                 reads=[rS], writes=[rS])
            S.op("dve", lambda e: e.scalar_tensor_tensor(out=ub[:, 0:ncols], in0=ub[:, 0:ncols], scalar=1.0,
                                                         in1=xcr[:, 0:ncols], op0=ALU.add, op1=ALU.mult),
                 reads=[rU, rXCR], writes=[rU])
            S.op("dve", lambda e: e.scalar_tensor_tensor(out=ub[:, 0:ncols], in0=ub[:, 0:ncols], scalar=0.5,
                                                         in1=sbuf_[:, 0:ncols], op0=ALU.mult, op1=ALU.mult),
                 reads=[rU, rS], writes=[rU])
            if g.main and g.p == 0:
                S.op("dve", lambda e: e.tensor_scalar(out=ub[:, 0:HALO], in0=ub[:, 0:HALO], scalar1=cp(C_FLAG),
                                                      scalar2=None, op0=ALU.mult),
                     reads=[rU, rCP], writes=[rU])
            S.op("dve", lambda e: e.tensor_tensor_scan(out=xc[:, 0:npc], data0=ab[:, 0:npc], data1=ub[:, 0:npc],
                                                       initial=PS_h[:, n:n + 1], op0=ALU.mult, op1=ALU.add),
                 reads=[rA, rU, R("PS_h", n)], writes=[rXC])
            S.op("dve", lambda e: e.tensor_copy(out=PS_h[:, n:n + 1], in_=xc[:, npc - 1:npc]),
                 reads=[rXC], writes=[R("PS_h", n)])
            if g.main:
                for j in range(SQ):
                    c0 = npc + 8 * j
                    S.op("dve", lambda e, j=j, c0=c0: e.tensor_tensor_scan(
                        out=xc[:, c0:c0 + 8], data0=ab[:, c0:c0 + 8], data1=ub[:, c0:c0 + 8],
                        initial=ST_h[:, n, g.p * SQ + j:g.p * SQ + j + 1], op0=ALU.mult, op1=ALU.add),
                        reads=[rA, rU, R("ST_h", n)], writes=[rXCs])
                S.op("dve", lambda e: e.tensor_copy(out=ST_h[:, n, g.p * SQ:(g.p + 1) * SQ], in_=V(xc, npc + 7, [8, SQ])),
                     reads=[rXCs], writes=[R("ST_h", n)])
                S.op("dve", lambda e: e.tensor_tensor(out=gg[:, 0:ncols], in0=gg[:, 0:ncols], in1=xc[:, 0:ncols],
                                                      op=ALU.mult),
                     reads=[rGG, rXC, rXCs], writes=[rGG])

        def phC():
            norm_phase(g, gg, [rGG], b, C_GOA + n, out_slot)

        return [phA, phB, phC] if g.main else [phA, phB]

    def tails2(xp, rXPt, PSt, rPSt, STt, rSTt, idx, p, npc):
        nsx = npc + 2
        S.op("pool", lambda e: e.tensor_copy(out=xp[:, 0:2], in_=PSt[:, idx, :]), reads=[rPSt], writes=[rXPt])
        S.op("pool", lambda e: e.tensor_copy(out=V(xp, nsx, [10, SQ], [1, 2]),
                                             in_=V(STt, idx * 32 + p * 16, [2, SQ], [1, 2])),
             reads=[rSTt], writes=[rXPt])

    def tails2_save(xp, rXPb, rXPt, PSt, rPSt, STt, rSTt, idx, p, npc):
        nsx = npc + 2
        S.op("pool", lambda e: e.tensor_copy(out=PSt[:, idx, :], in_=xp[:, npc:npc + 2]),
             reads=[rXPb, rXPt], writes=[rPSt])
        S.op("pool", lambda e: e.tensor_copy(out=V(STt, idx * 32 + p * 16, [2, SQ], [1, 2]),
                                             in_=V(xp, nsx + 8, [10, SQ], [1, 2])),
             reads=[rXPb, rXPt], writes=[rSTt])

    def conv3(xp, xc, rXPb, rXPt, rXCb, npc, cwcol, bias_col):
        nsx = npc + 2
        cw = lambda k: cp(cwcol + k)
        if bias_col is None:
            S.op("act", lambda e: e.activation(out=xc[:, 0:npc], in_=xp[:, 0:npc], func=AF.Copy, scale=cw(0)),
                 reads=[rXPb, rXPt, rCP], writes=[rXCb])
            S.op("act", lambda e: e.activation(out=V(xc, npc, [8, SQ], [1, 8]), in_=V(xp, nsx, [10, SQ], [1, 8]),
                                               func=AF.Copy, scale=cw(0)),
                 reads=[rXPb, rXPt, rCP], writes=[rXCb])
        else:
            S.op("act", lambda e: e.activation(out=xc[:, 0:npc], in_=xp[:, 0:npc], func=AF.Identity, bias=cp(bias_col),
                                               scale=cw(0)),
                 reads=[rXPb, rXPt, rCP], writes=[rXCb])
            S.op("act", lambda e: e.activation(out=V(xc, npc, [8, SQ], [1, 8]), in_=V(xp, nsx, [10, SQ], [1, 8]),
                                               func=AF.Identity, bias=cp(bias_col), scale=cw(0)),
                 reads=[rXPb, rXPt, rCP], writes=[rXCb])
        for k in range(1, 3):
            S.op("dve", lambda e, k=k: e.scalar_tensor_tensor(out=xc[:, 0:npc], in0=xp[:, k:k + npc], scalar=cw(k),
                                                             in1=xc[:, 0:npc], op0=ALU.mult, op1=ALU.add),
                 reads=[rXPb, rXPt, rXCb, rCP], writes=[rXCb])
            S.op("dve", lambda e, k=k: e.scalar_tensor_tensor(out=V(xc, npc, [8, SQ], [1, 8]),
                                                             in0=V(xp, nsx + k, [10, SQ], [1, 8]), scalar=cw(k),
                                                             in1=V(xc, npc, [8, SQ], [1, 8]), op0=ALU.mult, op1=ALU.add),
                 reads=[rXPb, rXPt, rXCb, rCP], writes=[rXCb])

    def sconv_unit(gi, g, out_slot):
        b = bufs()
        npc, nt, ncols, nh = g.npc, g.nt, g.ncols, g.nh
        xp, xc, gg, ub = b.xp, b.xc, b.gg, b.ub
        rXP, rXPt, rXC, rGG, rU = b.rXP, b.rXPt, b.rXC, b.rGG, b.rU
        srcN = lambda k, h: N[:, k, h * nt:(h + 1) * nt]
        resN = lambda k: R("N", k)
        nsx = npc + 2
        vh = lambda t: V(t, 0, [nt, nh], [1, nt])

        def phA():
            ws_vb = load_w(wA[40 + gi], 2048)
            ps_vb = next_ps()
            mm_group(ps_vb, ws_vb, 0, P, KD, srcN, resN, nt)
            S.op("act", lambda e: e.activation(out=vh(gg), in_=PSM[ps_vb][:, :, 0:nt], func=AF.Copy),
                 reads=[R("PS", ps_vb)], writes=[rGG])
            ws_gc = load_w(wA[32 + gi], 2048)
            ps_gc = next_ps()
            mm_group(ps_gc, ws_gc, 0, P, KD, srcN, resN, nt)
            tails2(xp, rXPt, PS_sc, R("PS_sc", gi), ST_sc, R("ST_sc", gi), gi, g.p, npc)
            S.op("dve", lambda e: e.tensor_tensor(out=xp[:, 2:2 + nt], in0=PSM[ps_gc][:, 0, 0:nt], in1=gg[:, 0:nt],
                                                  op=ALU.mult),
                 reads=[R("PS", ps_gc), rGG], writes=[rXP])
            S.op("dve", lambda e: e.tensor_tensor(out=xp[:, 2 + nt:2 + npc], in0=PSM[ps_gc][:, 1, 0:npc - nt],
                                                  in1=gg[:, nt:npc], op=ALU.mult),
                 reads=[R("PS", ps_gc), rGG], writes=[rXP])
            S.op("dve", lambda e: e.tensor_tensor(out=V(xp, nsx + 2, [10, SQ], [1, 8]),
                                                  in0=V(PSM[ps_gc], 512 + npc - nt, [8, SQ], [1, 8]),
                                                  in1=V(gg, npc, [8, SQ], [1, 8]), op=ALU.mult),
                 reads=[R("PS", ps_gc), rGG], writes=[rXP])
            tails2_save(xp, rXP, rXPt, PS_sc, R("PS_sc", gi), ST_sc, R("ST_sc", gi), gi, g.p, npc)
            ws_gb = load_w(wA[24 + gi], 2048)
            ps_gb = next_ps()
            mm_group(ps_gb, ws_gb, 0, P, KD, srcN, resN, nt)
            S.op("act", lambda e: e.activation(out=vh(ub), in_=PSM[ps_gb][:, :, 0:nt], func=AF.Copy),
                 reads=[R("PS", ps_gb)], writes=[rU])
            conv3(xp, xc, rXP, rXPt, rXC, npc, C_CBW + gi * 3, None)
            S.op("dve", lambda e: e.tensor_tensor(out=gg[:, 0:ncols], in0=xc[:, 0:ncols], in1=ub[:, 0:ncols], op=ALU.mult),
                 reads=[rXC, rU, rGG], writes=[rGG])

        def phB():
            pass

        def phC():
            norm_phase(g, gg, [rGG], b, C_GOB + gi, out_slot)

        return [phA, phB, phC]

    def wo_partial(grp, nk, g):
        nt = g.nt
        half = (grp % 2) * YG
        for dd in range(4):
            ws = load_w(wO[grp * 4 + dd][:, 0:nk * 512], nk * 512)
            for d4 in range(4):
                d = dd * 4 + d4
                ps = next_ps()
                mm_group(ps, ws, d4 * P, 512, nk, lambda k, h: YH[:, half + k, h * nt:(h + 1) * nt],
                         lambda k: R("YH", half + k), nt)
                S.op("dve", lambda e, d=d, ps=ps: e.tensor_tensor(out=V(X, d * NC, [nt, 2], [1, nt]),
                                                                in0=PSM[ps][:, :, 0:nt],
                                                                in1=V(X, d * NC, [nt, 2], [1, nt]), op=ALU.add),
                     reads=[R("PS", ps), R("X", d)], writes=[R("X", d)])

    def ffn_unit(j, g, out_slot):
        b = bufs()
        npc, nt, ncols, nh = g.npc, g.nt, g.ncols, g.nh
        xp, xc, gg = b.xp, b.xc, b.gg
        rXP, rXPt, rXC, rGG = b.rXP, b.rXPt, b.rXC, b.rGG
        srcN = lambda k, h: N[:, k, h * nt:(h + 1) * nt]
        resN = lambda k: R("N", k)
        nsx = npc + 2
        vh = lambda t: V(t, 0, [nt, nh], [1, nt])

        def phA():
            ws_g = load_w(wU[j], 2048)
            ps_g = next_ps()
            mm_group(ps_g, ws_g, 0, P, KD, srcN, resN, nt)
            tails2(xp, rXPt, PS_fc, R("PS_fc", j), ST_fc, R("ST_fc", j), j, g.p, npc)
            S.op("act", lambda e: e.activation(out=xp[:, 2:2 + nt], in_=PSM[ps_g][:, 0, 0:nt], func=AF.Copy),
                 reads=[R("PS", ps_g)], writes=[rXP])
            S.op("act", lambda e: e.activation(out=xp[:, 2 + nt:2 + npc], in_=PSM[ps_g][:, 1, 0:npc - nt], func=AF.Copy),
                 reads=[R("PS", ps_g)], writes=[rXP])
            S.op("act", lambda e: e.activation(out=V(xp, nsx + 2, [10, SQ], [1, 8]),
                                               in_=V(PSM[ps_g], 512 + npc - nt, [8, SQ], [1, 8]), func=AF.Copy),
                 reads=[R("PS", ps_g)], writes=[rXP])
            tails2_save(xp, rXP, rXPt, PS_fc, R("PS_fc", j), ST_fc, R("ST_fc", j), j, g.p, npc)
            ws_v = load_w(wU[NF + j], 2048)
            ps_v = next_ps()
            mm_group(ps_v, ws_v, 0, P, KD, srcN, resN, nt)
            S.op("act", lambda e: e.activation(out=vh(gg), in_=PSM[ps_v][:, :, 0:nt], func=AF.Copy),
                 reads=[R("PS", ps_v)], writes=[rGG])
            conv3(xp, xc, rXP, rXPt, rXC, npc, C_CFW + j * 3, C_CFB + j)

        def phB():
            S.op("act", lambda e: e.activation(out=xc[:, 0:ncols], in_=xc[:, 0:ncols], func=AF.Gelu_apprx_tanh),
                 reads=[rXC], writes=[rXC])
            S.op("dve", lambda e: e.tensor_tensor(out=YH[:, out_slot, 0:ncols], in0=gg[:, 0:ncols], in1=xc[:, 0:ncols],
                                                  op=ALU.mult),
                 reads=[rGG, rXC], writes=[R("YH", out_slot)])

        return [phA, phB]

    def down_partial(grp, g):
        nt = g.nt
        half = (grp % 2) * YG
        for dd in range(4):
            ws = load_w(wD[grp * 4 + dd], 2048)
            for d4 in range(4):
                d = dd * 4 + d4
                ps = next_ps()
                mm_group(ps, ws, d4 * P, 512, YG, lambda k, h: YH[:, half + k, h * nt:(h + 1) * nt],
                         lambda k: R("YH", half + k), nt)
                S.op("dve", lambda e, d=d, ps=ps: e.tensor_tensor(out=V(X, d * NC, [nt, 2], [1, nt]),
                                                                in0=PSM[ps][:, :, 0:nt],
                                                                in1=V(X, d * NC, [nt, 2], [1, nt]), op=ALU.add),
                     reads=[R("PS", ps), R("X", d)], writes=[R("X", d)])

    def run_pipelined(units, on_done=None, lags=(0, 2, 3), newest_first=True, done_delay=1):
        n = len(units)
        pending = []
        for t in range(n + max(lags) + done_delay + 1):
            for ph in (range(len(lags)) if newest_first else reversed(range(len(lags)))):
                u = t - lags[ph]
                if 0 <= u < n and ph < len(units[u]):
                    units[u][ph]()
                    if ph == len(units[u]) - 1:
                        pending.append((t + done_delay, u))
            if on_done is not None:
                for (td, u) in [x for x in pending if x[0] <= t]:
                    on_done(u)
            pending = [x for x in pending if x[0] > t]

    def store_y(p):
        tiles = [(i * P, P) for i in range(4)] + [(512, 72)]
        for ti, (c0, nr) in enumerate(tiles):
            s = nxt("io", NIO)
            for kh in range(2):
                ps = next_ps()
                for i in range(8):
                    k = kh * 8 + i
                    S.op("pe", lambda e, k=k, i=i, ps=ps, c0=c0, nr=nr: e.transpose(
                        out=VP(PSM[ps], i * P, nr, [1, P]), in_=X[:, k, c0:c0 + nr], identity=IDT[:]),
                        reads=[R("X", k), rIDT], writes=[R("PS", ps)])
                copy_op(evac_eng(), IO[s][0:nr, kh * 1024:(kh + 1) * 1024], VP(PSM[ps], 0, nr, [1, 1024]),
                        [R("PS", ps)], [R("IO", s)])
            if ti < 4:
                S.dma(STQ, lambda e, s=s, c0=c0, p=p: e.dma_start(out=y_p[p * PM + c0:p * PM + c0 + P, :], in_=IO[s][:, :]),
                      reads=[R("IO", s)], lane=("io", s))
            else:
                S.dma(STQ, lambda e, s=s, p=p: e.dma_start(out=y_p[p * PM + 512:p * PM + 520, :], in_=IO[s][0:8, :]),
                      reads=[R("IO", s)], lane=("io", s))
                S.dma(STQ, lambda e, s=s, p=p: e.dma_start(out=y_s[p * NS:(p + 1) * NS, :], in_=IO[s][8:72, :]),
                      reads=[R("IO", s)], lane=("io", s))

    def store_states():
        def tr_out(src_fn, nblk, nrows, dst_ap_fn, width):
            s = nxt("io", NIO)
            done = 0
            while done < nblk:
                nb = min(8, nblk - done)
                ps = next_ps()
                for i in range(nb):
                    src, rres = src_fn(done + i)
                    S.op("pe", lambda e, i=i, ps=ps, src=src: e.transpose(out=VP(PSM[ps], i * P, nrows, [1, P]),
                                                                        in_=src, identity=IDT[:]),
                         reads=[rres, rIDT], writes=[R("PS", ps)])
                copy_op(evac_eng(), IO[s][0:nrows, done * P:(done + nb) * P], VP(PSM[ps], 0, nrows, [1, nb * P]),
                        [R("PS", ps)], [R("IO", s)])
                done += nb
            dst_ap_fn(s)

        tr_out(lambda n: (ST_h[:, n, :], R("ST_h", n)), NH, 16,
               lambda s: S.dma(STQ, lambda e: e.dma_start(out=o_sh, in_=IO[s][0:16, 0:DA]), reads=[R("IO", s)],
                               lane=("io", s)), DA)
        tr_out(lambda n: (ST_rc[:, n, :], R("ST_rc", n)), NH, 48,
               lambda s: S.dma(STQ, lambda e: e.dma_start(out=o_src, in_=IO[s][0:48, 0:DA]), reads=[R("IO", s)],
                               lane=("io", s)), DA)
        tr_out(lambda gi: (ST_sc[:, gi, :], R("ST_sc", gi)), NG, 32,
               lambda s: S.dma(STQ, lambda e: e.dma_start(out=o_ssc, in_=IO[s][0:32, 0:DB]), reads=[R("IO", s)],
                               lane=("io", s)), DB)
        for q in range(3):
            tr_out(lambda j, q=q: (ST_fc[:, q * 16 + j, :], R("ST_fc", q * 16 + j)), 16, 32,
                   lambda s, q=q: S.dma(STQ, lambda e: e.dma_start(out=o_sfc[:, q * 2048:(q + 1) * 2048],
                                                                      in_=IO[s][0:32, :]),
                                        reads=[R("IO", s)], lane=("io", s)), 2048)
        for (src, nrows, dst, res) in ((PS_h[:, :], NH, o_ph, [R("PS_h", n) for n in range(NH)]),
                                       (V(PS_rc, 0, [1, NH * 3]), NH * 3, o_prc, [R("PS_rc", n) for n in range(NH)]),
                                       (V(PS_sc, 0, [1, NG * 2]), NG * 2, o_psc, [R("PS_sc", n) for n in range(NG)]),
                                       (V(PS_fc, 0, [1, NF * 2]), NF * 2, o_pfc, [R("PS_fc", n) for n in range(NF)])):
            s = nxt("io", NIO)
            ps = next_ps()
            S.op("pe", lambda e, ps=ps, src=src, nrows=nrows: e.transpose(out=VP(PSM[ps], 0, nrows, [1, P]), in_=src,
                                                                         identity=IDT[:]),
                 reads=res + [rIDT], writes=[R("PS", ps)])
            copy_op(evac_eng(), IO[s][0:nrows, 0:P], VP(PSM[ps], 0, nrows, [1, P]), [R("PS", ps)], [R("IO", s)])
            S.dma(STQ, lambda e, s=s, nrows=nrows, dst=dst: e.dma_start(out=dst, in_=IO[s][0:nrows, 0:P]),
                  reads=[R("IO", s)], lane=("io", s))


    gp = Geom()
    gp.npc, gp.nt, gp.ncols, gp.nh, gp.main, gp.p = 512, 512, 512, 1, False, 0
    for q in range(2):
        load_x_tiles([(i * P, P, [(0, P, xw[q * 512 + i * P:q * 512 + (i + 1) * P, :])]) for i in range(4)])
        rmsnorm_fm(512, 512, C_GMIX)
        run_pipelined([lru_unit(n, gp, None) for n in range(NH)], lags=(0, 2))
    S.op("dve", lambda e: e.tensor_scalar(out=PS_h[:], in0=PS_h[:], scalar1=cp(C_FLAG), scalar2=None, op0=ALU.mult),
         reads=[R("PS_h", n) for n in range(NH)] + [rCP], writes=[R("PS_h", n) for n in range(NH)])

    load_states()

    for p in range(2):
        g = Geom()
        g.npc, g.nt, g.ncols, g.nh, g.main, g.p = PM, NT, NC, 2, True, p
        base = NPRE + p * PM
        tiles = [(i * P, P, [(0, P, xw[base + i * P:base + (i + 1) * P, :])]) for i in range(4)]
        tiles.append((512, 72, [(0, 8, xw[base + 512:base + 520, :]), (8, 64, xs[p * NS:(p + 1) * NS, :])]))
        load_x_tiles(tiles)
        rmsnorm_fm(NC, NT, C_GMIX)
        units = []
        for c in range(NMIX):
            slot = c % (2 * YG)
            units.append(lru_unit(c, g, slot) if c < NH else sconv_unit(c - NH, g, slot))

        def mix_done(u, g=g):
            if u % YG == YG - 1:
                wo_partial(u // YG, YG, g)
        run_pipelined(units, mix_done)
        rmsnorm_fm(NC, NT, C_GFFN)
        funits = [ffn_unit(j, g, j % (2 * YG)) for j in range(NF)]

        def ffn_done(u, g=g):
            if u % YG == YG - 1:
                down_partial(u // YG, g)
        run_pipelined(funits, ffn_done, lags=(0, 1), newest_first=False)
        rmsnorm_fm(NC, NT, C_GFIN, rounded=False)
        store_y(p)
    store_states()

    S.emit(nc, es)
    es.close()
    return nc


_PROG = {}


def _tile_cols(w, ncb):
    K = w.shape[0]
    return np.ascontiguousarray(w.reshape(K // P, P, ncb, P).transpose(2, 1, 0, 3)).reshape(ncb, P, (K // P) * P)


def _tile_rows(w, gk):
    K = w.shape[0]
    ng = K // (gk * P)
    a = w.reshape(ng, gk, P, 4, 512).transpose(0, 3, 2, 1, 4)
    return np.ascontiguousarray(a).reshape(ng * 4, P, gk * 512)


def _fm(v, n):
    return np.ascontiguousarray(v.reshape(n, P).T)


def kernel(x_prompt, x_sample, state_lru_h, state_lru_conv, state_sconv, state_ffn_conv, meta_tokens, g_mix, w_in,
           conv_a_w, conv_a_b, w_gate_a, b_gate_a, w_gate_x, b_gate_x, lru_lambda, conv_b_w, g_out_a, g_out_b, w_o,
           g_ffn, w_up, conv_f_w, conv_f_b, w_down, g_final):
    f32 = np.float32
    A = lambda a: np.asarray(a, dtype=f32)
    x_prompt, x_sample = A(x_prompt), A(x_sample)
    import os
    stage = os.environ.get("KSTAGE")
    key = ("nc", stage)
    if key not in _PROG:
        _PROG[key] = build_program(stop_after=None if stage is None else int(stage))
    nc = _PROG[key]
    wA_ = _tile_cols(A(w_in)[0], 48)
    wU_ = _tile_cols(A(w_up)[0], 96)
    wO_ = _tile_rows(A(w_o)[0], YG)
    wD_ = _tile_rows(A(w_down)[0], YG)
    wG_ = np.ascontiguousarray(np.concatenate([A(w_gate_a)[0], A(w_gate_x)[0]], axis=2))
    cpar = np.zeros((P, NCPAR), f32)
    cpar[:, C_GMIX:C_GMIX + 16] = _fm(A(g_mix)[0], 16)
    cpar[:, C_GFFN:C_GFFN + 16] = _fm(A(g_ffn)[0], 16)
    cpar[:, C_GFIN:C_GFIN + 16] = _fm(A(g_final), 16)
    cpar[:, C_CAB:C_CAB + 12] = _fm(A(conv_a_b)[0], 12)
    cpar[:, C_BGA:C_BGA + 12] = _fm(A(b_gate_a)[0], 12)
    cpar[:, C_BGX:C_BGX + 12] = _fm(A(b_gate_x)[0], 12)
    cpar[:, C_LAM:C_LAM + 12] = _fm(A(lru_lambda)[0], 12)
    cpar[:, C_GOA:C_GOA + 12] = _fm(A(g_out_a)[0], 12)
    cpar[:, C_GOB:C_GOB + 8] = _fm(A(g_out_b)[0], 8)
    cpar[:, C_CFB:C_CFB + 48] = _fm(A(conv_f_b)[0], 48)
    caw = A(conv_a_w)[0]
    cpar[:, C_CAW:C_CAW + 48] = caw.reshape(4, NH, P).transpose(2, 1, 0).reshape(P, 48)
    cbw = A(conv_b_w)[0]
    cpar[:, C_CBW:C_CBW + 24] = cbw.reshape(3, NG, P).transpose(2, 1, 0).reshape(P, 24)
    cfw = A(conv_f_w)[0]
    cpar[:, C_CFW:C_CFW + 144] = cfw.reshape(3, NF, P).transpose(2, 1, 0).reshape(P, 144)
    ident = np.eye(P, dtype=f32)
    meta = A(meta_tokens)
    in_maps = []
    for c in range(NCORES):
        b, half = c // 2, c % 2
        seq = np.concatenate([meta, x_prompt[b]], axis=0)
        if half == 0:
            xw_ = np.concatenate([np.zeros((1032, D), f32), seq[0:1032]], axis=0)
        else:
            xw_ = seq
        cp_c = cpar.copy()
        cp_c[:, C_FLAG] = float(half)
        sl = slice(16 * c, 16 * c + 16)
        st_hr = np.concatenate([A(state_lru_h)[0, sl], A(state_lru_conv)[0, sl].reshape(48, DA)], axis=0)
        in_maps.append({
            "xw": np.ascontiguousarray(xw_), "xs": np.ascontiguousarray(x_sample[sl].reshape(128, D)),
            "st_hr": np.ascontiguousarray(st_hr),
            "st_sc": np.ascontiguousarray(A(state_sconv)[0, sl].reshape(32, DB)),
            "st_fc": np.ascontiguousarray(A(state_ffn_conv)[0, sl].reshape(32, DFF)),
            "cpar": cp_c, "ident": ident, "wA": wA_, "wG": wG_, "wO": wO_, "wU": wU_, "wD": wD_,
        })
    res = run_bass_kernel_spmd(nc, in_maps, core_ids=list(range(NCORES)))
    r = res.results
    B = x_prompt.shape[0]
    y_prompt = np.zeros((B, 2048, D), f32)
    y_sample = np.zeros((128, 8, D), f32)
    p_h = np.zeros((1, B, DA), f32)
    p_rc = np.zeros((1, B, 3, DA), f32)
    p_sc = np.zeros((1, B, 2, DB), f32)
    p_fc = np.zeros((1, B, 2, DFF), f32)
    s_h = np.zeros((1, 128, DA), f32)
    s_rc = np.zeros((1, 128, 3, DA), f32)
    s_sc = np.zeros((1, 128, 2, DB), f32)
    s_fc = np.zeros((1, 128, 2, DFF), f32)
    for c in range(NCORES):
        b, half = c // 2, c % 2
        yp = r[c]["y_p"][HALO:]
        if half == 0:
            y_prompt[b, 0:1016] = yp[16:1032]
        else:
            y_prompt[b, 1016:2048] = yp
            p_h[0, b] = r[c]["o_ph"].reshape(DA)
            p_rc[0, b] = r[c]["o_prc"].reshape(NH, 3, P).transpose(1, 0, 2).reshape(3, DA)
            p_sc[0, b] = r[c]["o_psc"].reshape(NG, 2, P).transpose(1, 0, 2).reshape(2, DB)
            p_fc[0, b] = r[c]["o_pfc"].reshape(NF, 2, P).transpose(1, 0, 2).reshape(2, DFF)
        sl = slice(16 * c, 16 * c + 16)
        y_sample[sl] = r[c]["y_s"].reshape(16, 8, D)
        s_h[0, sl] = r[c]["o_sh"]
        s_rc[0, sl] = r[c]["o_src"].reshape(16, 3, DA)
        s_sc[0, sl] = r[c]["o_ssc"].reshape(16, 2, DB)
        s_fc[0, sl] = r[c]["o_sfc"].reshape(16, 2, DFF)
    return (y_prompt, y_sample, p_h, p_rc, p_sc, p_fc, s_h, s_rc, s_sc, s_fc)
```

```python
import numpy as np
from contextlib import ExitStack
import concourse.bass as bass
import concourse.mybir as mybir
from concourse.ap import AP
from concourse.bass_utils import run_bass_kernel_spmd

F32 = mybir.dt.float32
F32R = mybir.dt.float32r
AF = mybir.ActivationFunctionType
ALU = mybir.AluOpType

NCORES = 8
P = 128
D = 2048
KD = 16
DA = 1536
NH = 12
DB = 1024
NG = 8
DFF = 6144
NF = 48
NMIX = 20
DIN = 2 * DA + 3 * DB
HALO = 8
NPRE = 1024
NMAIN = 1040
PM = 520
SQ = 8
NS = 64
NC = 584
NT = 292
WBW = 616
YG = 4
EPS = 1e-6

C_GMIX, C_GFFN, C_GFIN = 0, 16, 32
C_CAB, C_BGA, C_BGX, C_LAM, C_GOA, C_GOB, C_CFB = 48, 60, 72, 84, 96, 108, 116
C_CAW, C_CBW, C_CFW, C_FLAG = 164, 212, 236, 380
NCPAR = 384
DC_HBA, DC_HBX, DC_C, DC_CH, DC_EPS, DC_ONE = 0, 12, 24, 36, 48, 49
NDC = 64


class Res:
    __slots__ = ("name", "w", "r")

    def __init__(self, name):
        self.name = name
        self.w = None
        self.r = []


class Op:
    __slots__ = ("eng", "fn", "deps", "lane", "lane_idx", "sig", "tick", "is_dma")


class Sched:
    ENGS = ("pe", "act", "dve", "pool", "sp")

    def __init__(self):
        self.streams = {e: [] for e in self.ENGS}
        self.lanes = {}
        self.resd = {}

    def R(self, *key):
        r = self.resd.get(key)
        if r is None:
            r = Res(key)
            self.resd[key] = r
        return r

    def _rec(self, op, reads, writes):
        deps = {}
        for r in reads:
            if r.w is not None:
                deps[id(r.w)] = (r.w, True)
        for w in writes:
            if w.w is not None and id(w.w) not in deps:
                deps[id(w.w)] = (w.w, False)
            for rd in w.r:
                if id(rd) not in deps:
                    deps[id(rd)] = (rd, False)
        fin = []
        for p, raw in deps.values():
            if p is op:
                continue
            if (not p.is_dma) and (not op.is_dma) and p.eng == op.eng and not raw:
                continue
            fin.append(p)
            if not p.is_dma:
                p.sig = True
        op.deps = fin
        for r in reads:
            r.r.append(op)
        for w in writes:
            w.w = op
            w.r = []
        self.streams[op.eng].append(op)

    def op(self, eng, fn, reads=(), writes=()):
        if any(r.name[0] == "PS" for r in reads):
            writes = list(writes) + [r for r in reads if r.name[0] == "PS"]
            reads = [r for r in reads if r.name[0] != "PS"]
        o = Op()
        o.eng = eng
        o.fn = fn
        o.is_dma = False
        o.sig = False
        o.tick = None
        o.lane = None
        o.lane_idx = None
        self._rec(o, reads, writes)
        return o

    def dma(self, queue, fn, reads=(), writes=(), lane=None, bulk=False):
        o = Op()
        o.eng = queue
        o.fn = fn
        o.is_dma = True
        o.sig = True
        o.tick = None
        ln = self.lanes.setdefault(lane, [0, bulk])
        o.lane = lane
        o.lane_idx = ln[0]
        ln[0] += 1
        self._rec(o, reads, writes)
        return o

    def emit(self, nc, es):
        for e in self.ENGS:
            t = 0
            for o in self.streams[e]:
                if not o.is_dma and o.sig:
                    t += 1
                    o.tick = t
        esem = {e: es.enter_context(nc.semaphore("sem_" + e)) for e in self.ENGS if e != "sp"}
        lsem = {ln: es.enter_context(nc.semaphore("lane_" + str(ln))) for ln in self.lanes}
        store_lanes = {}
        for e in self.ENGS:
            for o in self.streams[e]:
                if o.is_dma:
                    store_lanes.setdefault(e, set()).add(o.lane)

        def run_stream(ename, eng):
            waited = {}
            for o in self.streams[ename]:
                need = {}
                for p in o.deps:
                    if p.is_dma:
                        cnt, bulk = self.lanes[p.lane]
                        val = 16 * (cnt if bulk else (p.lane_idx + 1))
                        key = ("l", p.lane)
                        sem = lsem[p.lane]
                    else:
                        val = p.tick
                        key = ("e", p.eng)
                        sem = esem[p.eng]
                    if need.get(key, (None, 0))[1] < val:
                        need[key] = (sem, val)
                for key, (sem, val) in need.items():
                    if waited.get(key, 0) >= val:
                        continue
                    eng.wait_ge(sem, val)
                    waited[key] = val
                ins = o.fn(eng)
                if o.is_dma:
                    ins.then_inc(lsem[o.lane], 16)
                elif o.sig:
                    ins.then_inc(esem[ename], 1)
            for ln in sorted(store_lanes.get(ename, ()), key=str):
                val = 16 * self.lanes[ln][0]
                if waited.get(("l", ln), 0) < val:
                    eng.wait_ge(lsem[ln], val)

        block = es.enter_context(nc.Block())

        @block.sync
        def _(e):
            run_stream("sp", e)

        @block.tensor
        def _(e):
            run_stream("pe", e)

        @block.scalar
        def _(e):
            run_stream("act", e)

        @block.vector
        def _(e):
            run_stream("dve", e)

        @block.gpsimd
        def _(e):
            run_stream("pool", e)


def build_program(stop_after=None, dbg=False):
    nc = bass.Bass("TRN2", target_bir_lowering=False)
    nc.dge_precook = False
    S = Sched()
    R = S.R
    es = ExitStack()
    import os as _os
    STQ = _os.environ.get("KSTQ", "pool")

    def din(name, shape, dt=F32):
        return nc.dram_tensor(name, shape, dt, kind="ExternalInput").ap()

    def dout(name, shape, dt=F32):
        return nc.dram_tensor(name, shape, dt, kind="ExternalOutput").ap()

    xw = din("xw", [NPRE + NMAIN, D])
    xs = din("xs", [128, D])
    st_hr = din("st_hr", [64, DA])
    st_sc = din("st_sc", [32, DB])
    st_fc = din("st_fc", [32, DFF])
    cpar = din("cpar", [P, NCPAR])
    identd = din("ident", [P, P])
    wA = din("wA", [48, P, 2048], F32R)
    wG = din("wG", [NH, P, 256], F32R)
    wO = din("wO", [20, P, 2048], F32R)
    wU = din("wU", [96, P, 2048], F32R)
    wD = din("wD", [48, P, 2048], F32R)
    y_p = dout("y_p", [NMAIN, D])
    y_s = dout("y_s", [128, D])
    o_sh = dout("o_sh", [16, DA])
    o_src = dout("o_src", [48, DA])
    o_ssc = dout("o_ssc", [32, DB])
    o_sfc = dout("o_sfc", [32, DFF])
    o_ph = dout("o_ph", [NH, P])
    o_prc = dout("o_prc", [NH * 3, P])
    o_psc = dout("o_psc", [NG * 2, P])
    o_pfc = dout("o_pfc", [NF * 2, P])

    def sb(name, shape, dt=F32):
        return es.enter_context(nc.sbuf_tensor(name, shape, dt))

    X = sb("X", [P, KD, NC])
    N = sb("N", [P, KD, NC], F32R)
    YH = sb("YH", [P, 2 * YG, NC], F32R)
    NWS = 4
    Wsl = [sb(f"W{i}", [P, 2048], F32R) for i in range(NWS)]
    NGS = 4
    GWsl = [sb(f"GW{i}", [P, 256], F32R) for i in range(NGS)]
    NIO = 2
    IO = [sb(f"IO{i}", [P, 2048]) for i in range(NIO)]
    XP = [sb(f"XP{i}", [P, WBW]) for i in range(2)]
    XC = [sb(f"XC{i}", [P, WBW]) for i in range(3)]
    GG = [sb(f"GG{i}", [P, WBW]) for i in range(4)]
    SB_ = [sb(f"SB{i}", [P, WBW]) for i in range(2)]
    AB = [sb(f"AB{i}", [P, WBW]) for i in range(2)]
    UB = [sb(f"UB{i}", [P, WBW]) for i in range(2)]
    SQB = [sb(f"SQB{i}", [P, NC], F32R) for i in range(2)]
    XCR = [sb(f"XCR{i}", [P, NC], F32R) for i in range(3)]
    RB = sb("RB", [P, NC])
    CP = sb("CP", [P, NCPAR])
    DC = sb("DC", [P, NDC])
    IDT = sb("IDT", [P, P])
    ONES = sb("ONES", [P, P], F32R)
    ST_h = sb("ST_h", [P, NH, 16])
    ST_rc = sb("ST_rc", [P, NH, 48])
    ST_sc = sb("ST_sc", [P, NG, 32])
    ST_fc = sb("ST_fc", [P, NF, 32])
    PS_h = sb("PS_h", [P, NH])
    PS_rc = sb("PS_rc", [P, NH, 3])
    PS_sc = sb("PS_sc", [P, NG, 2])
    PS_fc = sb("PS_fc", [P, NF, 2])
    NPS = 4
    PSM = [es.enter_context(nc.psum_tensor(f"PSM{i}", [P, 2, 512], F32)) for i in range(NPS)]

    def pst(t):
        return t[:].ap[0][0]

    def V(t, off, *dims, dt=None):
        a = AP(t, off, [[pst(t), P]] + [list(d) for d in dims])
        if dt is not None:
            a = a.bitcast(dt)
        return a

    def VP(t, off, npart, *dims):
        return AP(t, off, [[pst(t), npart]] + [list(d) for d in dims])

    cnt = {"w": 0, "g": 0, "io": 0, "ps": 0, "sq": 0}

    def nxt(k, n):
        i = cnt[k] % n
        cnt[k] += 1
        return i

    def load_w(src_ap, ncols):
        s = nxt("w", NWS)
        S.dma("sp", lambda e, s=s: e.dma_start(out=Wsl[s][:, 0:ncols], in_=src_ap),
              writes=[R("W", s)], lane=("w", s))
        return s

    def next_ps():
        return nxt("ps", NPS)

    cp = lambda c0, n=1: CP[:, c0:c0 + n]
    dc = lambda c0, n=1: DC[:, c0:c0 + n]
    rCP, rDC, rIDT, rONES = R("CP"), R("DC"), R("IDT"), R("ONES")

    S.dma("sp", lambda e: e.dma_start(out=CP[:], in_=cpar), writes=[rCP], lane="const", bulk=True)
    S.dma("sp", lambda e: e.dma_start(out=IDT[:], in_=identd), writes=[rIDT], lane="const", bulk=True)
    S.op("dve", lambda e: e.memset(RB[:, 0:P], 1.0), writes=[R("RB")])
    S.op("dve", lambda e: e.tensor_copy(out=ONES[:], in_=RB[:, 0:P]), reads=[R("RB")], writes=[rONES])
    S.op("dve", lambda e: e.memset(DC[:, DC_EPS:DC_EPS + 1], EPS), writes=[R("DCe")])
    S.op("dve", lambda e: e.memset(DC[:, DC_ONE:DC_ONE + 1], 1.0), writes=[R("DCo")])
    S.op("dve", lambda e: e.memset(PS_h[:], 0.0), writes=[R("PS_h", n) for n in range(NH)])
    S.op("dve", lambda e: e.memset(PS_rc[:], 0.0), writes=[R("PS_rc", n) for n in range(NH)])
    S.op("dve", lambda e: e.memset(PS_sc[:], 0.0), writes=[R("PS_sc", g) for g in range(NG)])
    S.op("dve", lambda e: e.memset(PS_fc[:], 0.0), writes=[R("PS_fc", j) for j in range(NF)])
    S.op("dve", lambda e: e.tensor_scalar(out=dc(DC_HBA, 24), in0=cp(C_BGA, 24), scalar1=0.5, scalar2=None,
                                          op0=ALU.mult), reads=[rCP], writes=[R("DChb")])
    if "c_act" not in _os.environ.get("KSKIP", "").split(","):
        S.op("act", lambda e: e.activation(out=dc(DC_C, 12), in_=cp(C_LAM, 12), func=AF.Exp, scale=-1.0),
             reads=[rCP], writes=[R("DCc")])
        S.op("act", lambda e: e.activation(out=dc(DC_C, 12), in_=dc(DC_C, 12), func=AF.Ln, bias=dc(DC_ONE), scale=1.0),
             reads=[R("DCc"), R("DCo")], writes=[R("DCc")])
    S.op("dve", lambda e: e.tensor_scalar(out=dc(DC_CH, 12), in0=dc(DC_C, 12), scalar1=-4.0, scalar2=None,
                                          op0=ALU.mult), reads=[R("DCc")], writes=[R("DCch")])
    S.op("dve", lambda e: e.tensor_scalar(out=dc(DC_C, 12), in0=dc(DC_C, 12), scalar1=-8.0, scalar2=None,
                                          op0=ALU.mult), reads=[R("DCc"), R("DCch")], writes=[R("DCc")])
    rCONST = [rCP, R("DChb"), R("DCc"), R("DCch"), R("DCe"), R("DCo")]

    flip = [0]

    def evac_eng():
        flip[0] ^= 1
        return "act" if flip[0] else "dve"

    def copy_op(eng, out, in_, reads, writes):
        if eng == "act":
            S.op("act", lambda e: e.activation(out=out, in_=in_, func=AF.Copy), reads=reads, writes=writes)
        else:
            S.op(eng, lambda e: e.tensor_copy(out=out, in_=in_), reads=reads, writes=writes)

    def load_states():
        KS = _os.environ.get("KSKIP", "").split(",")
        if "hr" in KS:
            return
        s = nxt("io", NIO)
        S.dma("sp", lambda e: e.dma_start(out=IO[s][0:64, 0:DA], in_=st_hr), writes=[R("IO", s)], lane=("io", s))
        ps = next_ps()
        for n in range(NH):
            S.op("pe", lambda e, n=n: e.transpose(out=V(PSM[ps], n * 64, [1, 64]), in_=IO[s][0:64, n * P:(n + 1) * P],
                                                  identity=IDT[0:64, 0:64]),
                 reads=[R("IO", s), rIDT], writes=[R("PS", ps)])
        copy_op("act", ST_h[:], V(PSM[ps], 0, [64, NH], [1, 16]), [R("PS", ps)], [R("ST_h", n) for n in range(NH)])
        if "rc" not in KS:
            copy_op("dve", ST_rc[:], V(PSM[ps], 16, [64, NH], [1, 48]), [R("PS", ps)], [R("ST_rc", n) for n in range(NH)])
        if "sc" in KS:
            return
        s2 = nxt("io", NIO)
        S.dma("sp", lambda e: e.dma_start(out=IO[s2][0:32, 0:DB], in_=st_sc), writes=[R("IO", s2)], lane=("io", s2))
        ps2 = next_ps()
        for g in range(NG):
            S.op("pe", lambda e, g=g: e.transpose(out=V(PSM[ps2], g * 64, [1, 64]), in_=IO[s2][0:64, g * P:(g + 1) * P],
                                                  identity=IDT[0:64, 0:64]),
                 reads=[R("IO", s2), rIDT], writes=[R("PS", ps2)])
        copy_op("act", ST_sc[:], V(PSM[ps2], 0, [64, NG], [1, 32]), [R("PS", ps2)], [R("ST_sc", g) for g in range(NG)])
        if "fc" in KS:
            return
        for q in range(3):
            s3 = nxt("io", NIO)
            S.dma("sp", lambda e, q=q, s3=s3: e.dma_start(out=IO[s3][0:32, :], in_=st_fc[:, q * 2048:(q + 1) * 2048]),
                  writes=[R("IO", s3)], lane=("io", s3))
            for hh in range(2):
                ps3 = next_ps()
                for i in range(8):
                    ii = hh * 8 + i
                    S.op("pe", lambda e, i=i, ii=ii, s3=s3, ps3=ps3: e.transpose(out=V(PSM[ps3], i * 64, [1, 64]),
                                                                                 in_=IO[s3][0:64, ii * P:(ii + 1) * P],
                                                                                 identity=IDT[0:64, 0:64]),
                         reads=[R("IO", s3), rIDT], writes=[R("PS", ps3)])
                j0 = q * 16 + hh * 8
                copy_op(evac_eng(), ST_fc[:, j0:j0 + 8, :], V(PSM[ps3], 0, [64, 8], [1, 32]), [R("PS", ps3)],
                        [R("ST_fc", j) for j in range(j0, j0 + 8)])

    def load_x_tiles(tiles):
        for (col0, nr, parts) in tiles:
            s = nxt("io", NIO)
            for (r0, n_, src) in parts:
                S.dma("sp", lambda e, s=s, r0=r0, n_=n_, src=src: e.dma_start(out=IO[s][r0:r0 + n_, :], in_=src),
                      writes=[R("IO", s)], lane=("io", s))
            for kh in range(2):
                ps = next_ps()
                for i in range(8):
                    k = kh * 8 + i
                    S.op("pe", lambda e, s=s, k=k, i=i, ps=ps, nr=nr: e.transpose(
                        out=V(PSM[ps], i * P, [1, nr]), in_=IO[s][0:nr, k * P:(k + 1) * P], identity=IDT[0:nr, 0:nr]),
                        reads=[R("IO", s), rIDT], writes=[R("PS", ps)])
                copy_op(evac_eng(), X[:, kh * 8:kh * 8 + 8, col0:col0 + nr], V(PSM[ps], 0, [P, 8], [1, nr]),
                        [R("PS", ps)], [R("X", k) for k in range(kh * 8, kh * 8 + 8)])

    def rmsnorm_fm(ncols, nt, gcol, rounded=True):
        nh = ncols // nt
        ps = next_ps()
        scr = [(SQB[0], R("SQ", 0)), (SQB[1], R("SQ", 1)), (XCR[0], R("XCR", 0)), (XCR[1], R("XCR", 1)),
               (XCR[2], R("XCR", 2))]
        engs = ["act", "dve", "act", "pool"]
        for k in range(KD):
            sq, rsq = scr[k % len(scr)]
            eng = engs[k % len(engs)]
            if eng == "act":
                S.op("act", lambda e, k=k, sq=sq: e.activation(out=sq[:, 0:ncols], in_=X[:, k, 0:ncols], func=AF.Square),
                     reads=[R("X", k)], writes=[rsq])
            else:
                S.op(eng, lambda e, k=k, sq=sq: e.tensor_tensor(out=sq[:, 0:ncols], in0=X[:, k, 0:ncols],
                                                               in1=X[:, k, 0:ncols], op=ALU.mult),
                     reads=[R("X", k)], writes=[rsq])
            for h in range(nh):
                S.op("pe", lambda e, k=k, sq=sq, h=h: e.matmul(PSM[ps][:, h, 0:nt], ONES[:], sq[:, h * nt:(h + 1) * nt],
                                                            start=(k == 0), stop=(k == KD - 1)),
                     reads=[rsq, rONES], writes=[R("PS", ps)])
        S.op("act", lambda e: e.activation(out=V(RB, 0, [nt, nh], [1, nt]), in_=PSM[ps][:, 0:nh, 0:nt], func=AF.Ln,
                                           bias=dc(DC_EPS), scale=1.0 / D),
             reads=[R("PS", ps), R("DCe")], writes=[R("RB")])
        S.op("act", lambda e: e.activation(out=RB[:, 0:ncols], in_=RB[:, 0:ncols], func=AF.Exp, scale=-0.5),
             reads=[R("RB")], writes=[R("RB")])
        for k in range(KD):
            o = N[:, k, 0:ncols] if rounded else X[:, k, 0:ncols]
            S.op("dve", lambda e, k=k, o=o: e.scalar_tensor_tensor(out=o, in0=X[:, k, 0:ncols], scalar=cp(gcol + k),
                                                                 in1=RB[:, 0:ncols], op0=ALU.mult, op1=ALU.mult),
                 reads=[R("X", k), R("RB"), rCP], writes=[R("N", k) if rounded else R("X", k)])

    def mm_group(ps, wslot, wcol0, wkstride, nk, src_fn, src_res, nt, nhalf=2):
        for k in range(nk):
            for h in range(nhalf):
                S.op("pe", lambda e, k=k, h=h: e.matmul(PSM[ps][:, h, 0:nt],
                                                       Wsl[wslot][:, k * wkstride + wcol0:k * wkstride + wcol0 + P],
                                                       src_fn(k, h), start=(k == 0), stop=(k == nk - 1)),
                     reads=[R("W", wslot), src_res(k)], writes=[R("PS", ps)])

    class Geom:
        pass

    ucnt = [0]

    def bufs():
        u = ucnt[0]
        ucnt[0] += 1
        b = Geom()
        i2, i3, i4 = u % 2, u % 3, u % 4
        b.xp, b.rXP, b.rXPt = XP[i2], R("XP", i2), R("XPt", i2)
        b.xcr, b.rXCR = XCR[i3], R("XCR", i3)
        b.sb, b.rS = SB_[i2], R("SBf", i2)
        b.ab, b.rA = AB[i2], R("AB", i2)
        b.ub, b.rU = UB[i2], R("UB", i2)
        b.xc, b.rXC, b.rXCs = XC[i3], R("XC", i3), R("XCs", i3)
        b.gg, b.rGG = GG[i4], R("GG", i4)
        return b

    def norm_phase(g, yb, yres, b, gcol, out_slot):
        nt, ncols, nh = g.nt, g.ncols, g.nh
        q = nxt("sq", 2)
        sqb = SQB[q]
        S.op("pool", lambda e: e.tensor_tensor(out=sqb[:, 0:ncols], in0=yb[:, 0:ncols], in1=yb[:, 0:ncols], op=ALU.mult),
             reads=yres, writes=[R("SQ", q)])
        ps = next_ps()
        for h in range(nh):
            S.op("pe", lambda e, h=h: e.matmul(PSM[ps][:, h, 0:nt], ONES[:], sqb[:, h * nt:(h + 1) * nt],
                                               start=True, stop=True),
                 reads=[R("SQ", q), rONES], writes=[R("PS", ps)])
        S.op("act", lambda e: e.activation(out=V(b.ab, 0, [nt, nh], [1, nt]), in_=PSM[ps][:, 0:nh, 0:nt], func=AF.Ln,
                                           bias=dc(DC_EPS), scale=1.0 / P),
             reads=[R("PS", ps), R("DCe")], writes=[b.rA])
        S.op("act", lambda e: e.activation(out=b.ab[:, 0:ncols], in_=b.ab[:, 0:ncols], func=AF.Exp, scale=-0.5),
             reads=[b.rA], writes=[b.rA])
        S.op("dve", lambda e: e.scalar_tensor_tensor(out=YH[:, out_slot, 0:ncols], in0=yb[:, 0:ncols], scalar=cp(gcol),
                                                     in1=b.ab[:, 0:ncols], op0=ALU.mult, op1=ALU.mult),
             reads=yres + [b.rA, rCP], writes=[R("YH", out_slot)])

    def lru_unit(n, g, out_slot):
        b = bufs()
        npc, nt, ncols, nh = g.npc, g.nt, g.ncols, g.nh
        xp, xc, xcr, gg, sbuf_, ab, ub = b.xp, b.xc, b.xcr, b.gg, b.sb, b.ab, b.ub
        rXP, rXPt, rXC, rXCs, rXCR, rGG, rS, rA, rU = b.rXP, b.rXPt, b.rXC, b.rXCs, b.rXCR, b.rGG, b.rS, b.rA, b.rU
        nsx = npc + 3
        st = Geom()
        srcN = lambda k, h: N[:, k, h * nt:(h + 1) * nt]
        resN = lambda k: R("N", k)
        cw = lambda k: cp(C_CAW + n * 4 + k)
        vh = lambda t: V(t, 0, [nt, nh], [1, nt])

        def phA():
            ws_xa = load_w(wA[n], 2048)
            ps_xa = next_ps()
            mm_group(ps_xa, ws_xa, 0, P, KD, srcN, resN, nt, nh)
            if g.main:
                ws_ga = load_w(wA[NH + n], 2048)
                ps_ga = next_ps()
                mm_group(ps_ga, ws_ga, 0, P, KD, srcN, resN, nt, nh)
            st.gs = nxt("g", NGS)
            gs = st.gs
            S.dma("sp", lambda e: e.dma_start(out=GWsl[gs][:], in_=wG[n]), writes=[R("GW", gs)], lane=("g", gs))
            S.op("pool", lambda e: e.tensor_copy(out=xp[:, 0:3], in_=PS_rc[:, n, :]), reads=[R("PS_rc", n)], writes=[rXPt])
            if g.main:
                S.op("pool", lambda e: e.tensor_copy(out=V(xp, nsx, [11, SQ], [1, 3]),
                                                     in_=V(ST_rc, n * 48 + g.p * 24, [3, SQ], [1, 3])),
                     reads=[R("ST_rc", n)], writes=[rXPt])
                S.op("act", lambda e: e.activation(out=xp[:, 3:3 + nt], in_=PSM[ps_xa][:, 0, 0:nt], func=AF.Copy),
                     reads=[R("PS", ps_xa)], writes=[rXP])
                S.op("act", lambda e: e.activation(out=xp[:, 3 + nt:3 + npc], in_=PSM[ps_xa][:, 1, 0:npc - nt],
                                                   func=AF.Copy),
                     reads=[R("PS", ps_xa)], writes=[rXP])
                S.op("act", lambda e: e.activation(out=V(xp, nsx + 3, [11, SQ], [1, 8]),
                                                   in_=V(PSM[ps_xa], 512 + npc - nt, [8, SQ], [1, 8]), func=AF.Copy),
                     reads=[R("PS", ps_xa)], writes=[rXP])
            else:
                S.op("dve", lambda e: e.tensor_copy(out=V(xp, 3, [nt, nh], [1, nt]), in_=PSM[ps_xa][:, 0:nh, 0:nt]),
                     reads=[R("PS", ps_xa)], writes=[rXP])
            S.op("pool", lambda e: e.tensor_copy(out=PS_rc[:, n, :], in_=xp[:, npc:npc + 3]),
                 reads=[rXP, rXPt], writes=[R("PS_rc", n)])
            if g.main:
                S.op("pool", lambda e: e.tensor_copy(out=V(ST_rc, n * 48 + g.p * 24, [3, SQ], [1, 3]),
                                                     in_=V(xp, nsx + 8, [11, SQ], [1, 3])),
                     reads=[rXP, rXPt], writes=[R("ST_rc", n)])
            S.op("dve", lambda e: e.tensor_scalar(out=xc[:, 0:npc], in0=xp[:, 0:npc], scalar1=cw(0), scalar2=cp(C_CAB + n),
                                                  op0=ALU.mult, op1=ALU.add),
                 reads=[rXP, rXPt, rCP], writes=[rXC])
            for k in range(1, 4):
                o = xcr[:, 0:npc] if k == 3 else xc[:, 0:npc]
                S.op("dve", lambda e, k=k, o=o: e.scalar_tensor_tensor(out=o, in0=xp[:, k:k + npc], scalar=cw(k),
                                                                     in1=xc[:, 0:npc], op0=ALU.mult, op1=ALU.add),
                     reads=[rXP, rXPt, rXC, rCP], writes=[rXCR if k == 3 else rXC])
            if g.main:
                S.op("dve", lambda e: e.tensor_scalar(out=V(xc, npc, [8, SQ], [1, 8]), in0=V(xp, nsx, [11, SQ], [1, 8]),
                                                      scalar1=cw(0), scalar2=cp(C_CAB + n), op0=ALU.mult, op1=ALU.add),
                     reads=[rXP, rXPt, rCP], writes=[rXCs])
                for k in range(1, 4):
                    S.op("dve", lambda e, k=k: e.scalar_tensor_tensor(
                        out=V(xcr if k == 3 else xc, npc, [8, SQ], [1, 8]), in0=V(xp, nsx + k, [11, SQ], [1, 8]),
                        scalar=cw(k), in1=V(xc, npc, [8, SQ], [1, 8]), op0=ALU.mult, op1=ALU.add),
                        reads=[rXP, rXPt, rXCs, rCP], writes=[rXCR if k == 3 else rXCs])
                S.op("act", lambda e: e.activation(out=vh(gg), in_=PSM[ps_ga][:, 0:nh, 0:nt], func=AF.Gelu_apprx_tanh),
                     reads=[R("PS", ps_ga)], writes=[rGG])

        def phB():
            gs = st.gs
            ps_r = next_ps()
            ps_i = next_ps()
            for (psx, c0) in ((ps_r, 0), (ps_i, P)):
                for h in range(nh):
                    S.op("pe", lambda e, psx=psx, c0=c0, h=h: e.matmul(PSM[psx][:, h, 0:nt], GWsl[gs][:, c0:c0 + P],
                                                                     xcr[:, h * nt:(h + 1) * nt], start=True, stop=True),
                         reads=[R("GW", gs), rXCR], writes=[R("PS", psx)])
            S.op("act", lambda e: e.activation(out=vh(sbuf_), in_=PSM[ps_r][:, 0:nh, 0:nt], func=AF.Tanh,
                                               bias=dc(DC_HBA + n), scale=0.5),
                 reads=[R("PS", ps_r), R("DChb")], writes=[rS])
            S.op("act", lambda e: e.activation(out=vh(ub), in_=PSM[ps_i][:, 0:nh, 0:nt], func=AF.Tanh,
                                               bias=dc(DC_HBX + n), scale=0.5),
                 reads=[R("PS", ps_i), R("DChb")], writes=[rU])
            S.op("act", lambda e: e.activation(out=ab[:, 0:ncols], in_=sbuf_[:, 0:ncols], func=AF.Exp,
                                               bias=dc(DC_CH + n), scale=dc(DC_CH + n)),
                 reads=[rS, R("DCch")], writes=[rA])
            S.op("act", lambda e: e.activation(out=sbuf_[:, 0:ncols], in_=sbuf_[:, 0:ncols], func=AF.Exp,
                                               bias=dc(DC_C + n), scale=dc(DC_C + n)),
                 reads=[rS, R("DCc")], writes=[rS])
            if g.main:
                S.op("act", lambda e: e.activation(out=sbuf_[:, 0:ncols], in_=sbuf_[:, 0:ncols], func=AF.Ln,
                                                   bias=dc(DC_ONE), scale=-1.0),
                     reads=[rS, R("DCo")], writes=[rS])
                S.op("act", lambda e: e.activation(out=sbuf_[:, 0:ncols], in_=sbuf_[:, 0:ncols], func=AF.Exp, scale=0.5),
                     reads=[rS], writes=[rS])
            else:
                S.op("act", lambda e: e.activation(out=sbuf_[:, 0:ncols], in_=sbuf_[:, 0:ncols], func=AF.Sqrt,
                                                   bias=dc(DC_ONE), scale=-1.0),
                     reads=[rS, R("DCo")], writes=[rS])
            S.op("dve", lambda e: e.scalar_tensor_tensor(out=ub[:, 0:ncols], in0=ub[:, 0:ncols], scalar=1.0,
                                                         in1=xcr[:, 0:ncols], op0=ALU.add, op1=ALU.mult),
                 reads=[rU, rXCR], writes=[rU])
            S.op("dve", lambda e: e.scalar_tensor_tensor(out=ub[:, 0:ncols], in0=ub[:, 0:ncols], scalar=0.5,
                                                         in1=sbuf_[:, 0:ncols], op0=ALU.mult, op1=ALU.mult),
                 reads=[rU, rS], writes=[rU])
            if g.main and g.p == 0:
                S.op("dve", lambda e: e.tensor_scalar(out=ub[:, 0:HALO], in0=ub[:, 0:HALO], scalar1=cp(C_FLAG),
                                                      scalar2=None, op0=ALU.mult),
                     reads=[rU, rCP], writes=[rU])
            S.op("dve", lambda e: e.tensor_tensor_scan(out=xc[:, 0:npc], data0=ab[:, 0:npc], data1=ub[:, 0:npc],
                                                       initial=PS_h[:, n:n + 1], op0=ALU.mult, op1=ALU.add),
                 reads=[rA, rU, R("PS_h", n)], writes=[rXC])
            S.op("dve", lambda e: e.tensor_copy(out=PS_h[:, n:n + 1], in_=xc[:, npc - 1:npc]),
                 reads=[rXC], writes=[R("PS_h", n)])
            if g.main:
                for j in range(SQ):
                    c0 = npc + 8 * j
                    S.op("dve", lambda e, j=j, c0=c0: e.tensor_tensor_scan(
                        out=xc[:, c0:c0 + 8], data0=ab[:, c0:c0 + 8], data1=ub[:, c0:c0 + 8],
                        initial=ST_h[:, n, g.p * SQ + j:g.p * SQ + j + 1], op0=ALU.mult, op1=ALU.add),
                        reads=[rA, rU, R("ST_h", n)], writes=[rXCs])
                S.op("dve", lambda e: e.tensor_copy(out=ST_h[:, n, g.p * SQ:(g.p + 1) * SQ], in_=V(xc, npc + 7, [8, SQ])),
                     reads=[rXCs], writes=[R("ST_h", n)])
                S.op("dve", lambda e: e.tensor_tensor(out=gg[:, 0:ncols], in0=gg[:, 0:ncols], in1=xc[:, 0:ncols],
                                                      op=ALU.mult),
                     reads=[rGG, rXC, rXCs], writes=[rGG])

        def phC():
            norm_phase(g, gg, [rGG], b, C_GOA + n, out_slot)

        return [phA, phB, phC] if g.main else [phA, phB]

    def tails2(xp, rXPt, PSt, rPSt, STt, rSTt, idx, p, npc):
        nsx = npc + 2
        S.op("pool", lambda e: e.tensor_copy(out=xp[:, 0:2], in_=PSt[:, idx, :]), reads=[rPSt], writes=[rXPt])
        S.op("pool", lambda e: e.tensor_copy(out=V(xp, nsx, [10, SQ], [1, 2]),
                                             in_=V(STt, idx * 32 + p * 16, [2, SQ], [1, 2])),
             reads=[rSTt], writes=[rXPt])

    def tails2_save(xp, rXPb, rXPt, PSt, rPSt, STt, rSTt, idx, p, npc):
        nsx = npc + 2
        S.op("pool", lambda e: e.tensor_copy(out=PSt[:, idx, :], in_=xp[:, npc:npc + 2]),
             reads=[rXPb, rXPt], writes=[rPSt])
        S.op("pool", lambda e: e.tensor_copy(out=V(STt, idx * 32 + p * 16, [2, SQ], [1, 2]),
                                             in_=V(xp, nsx + 8, [10, SQ], [1, 2])),
             reads=[rXPb, rXPt], writes=[rSTt])

    def conv3(xp, xc, rXPb, rXPt, rXCb, npc, cwcol, bias_col):
        nsx = npc + 2
        cw = lambda k: cp(cwcol + k)
        if bias_col is None:
            S.op("act", lambda e: e.activation(out=xc[:, 0:npc], in_=xp[:, 0:npc], func=AF.Copy, scale=cw(0)),
                 reads=[rXPb, rXPt, rCP], writes=[rXCb])
            S.op("act", lambda e: e.activation(out=V(xc, npc, [8, SQ], [1, 8]), in_=V(xp, nsx, [10, SQ], [1, 8]),
                                               func=AF.Copy, scale=cw(0)),
                 reads=[rXPb, rXPt, rCP], writes=[rXCb])
        else:
            S.op("act", lambda e: e.activation(out=xc[:, 0:npc], in_=xp[:, 0:npc], func=AF.Identity, bias=cp(bias_col),
                                               scale=cw(0)),
                 reads=[rXPb, rXPt, rCP], writes=[rXCb])
            S.op("act", lambda e: e.activation(out=V(xc, npc, [8, SQ], [1, 8]), in_=V(xp, nsx, [10, SQ], [1, 8]),
                                               func=AF.Identity, bias=cp(bias_col), scale=cw(0)),
                 reads=[rXPb, rXPt, rCP], writes=[rXCb])
        for k in range(1, 3):
            S.op("dve", lambda e, k=k: e.scalar_tensor_tensor(out=xc[:, 0:npc], in0=xp[:, k:k + npc], scalar=cw(k),
                                                             in1=xc[:, 0:npc], op0=ALU.mult, op1=ALU.add),
                 reads=[rXPb, rXPt, rXCb, rCP], writes=[rXCb])
            S.op("dve", lambda e, k=k: e.scalar_tensor_tensor(out=V(xc, npc, [8, SQ], [1, 8]),
                                                             in0=V(xp, nsx + k, [10, SQ], [1, 8]), scalar=cw(k),
                                                             in1=V(xc, npc, [8, SQ], [1, 8]), op0=ALU.mult, op1=ALU.add),
                 reads=[rXPb, rXPt, rXCb, rCP], writes=[rXCb])

    def sconv_unit(gi, g, out_slot):
        b = bufs()
        npc, nt, ncols, nh = g.npc, g.nt, g.ncols, g.nh
        xp, xc, gg, ub = b.xp, b.xc, b.gg, b.ub
        rXP, rXPt, rXC, rGG, rU = b.rXP, b.rXPt, b.rXC, b.rGG, b.rU
        srcN = lambda k, h: N[:, k, h * nt:(h + 1) * nt]
        resN = lambda k: R("N", k)
        nsx = npc + 2
        vh = lambda t: V(t, 0, [nt, nh], [1, nt])

        def phA():
            ws_vb = load_w(wA[40 + gi], 2048)
            ps_vb = next_ps()
            mm_group(ps_vb, ws_vb, 0, P, KD, srcN, resN, nt)
            S.op("act", lambda e: e.activation(out=vh(gg), in_=PSM[ps_vb][:, :, 0:nt], func=AF.Copy),
                 reads=[R("PS", ps_vb)], writes=[rGG])
            ws_gc = load_w(wA[32 + gi], 2048)
            ps_gc = next_ps()
            mm_group(ps_gc, ws_gc, 0, P, KD, srcN, resN, nt)
            tails2(xp, rXPt, PS_sc, R("PS_sc", gi), ST_sc, R("ST_sc", gi), gi, g.p, npc)
            S.op("dve", lambda e: e.tensor_tensor(out=xp[:, 2:2 + nt], in0=PSM[ps_gc][:, 0, 0:nt], in1=gg[:, 0:nt],
                                                  op=ALU.mult),
                 reads=[R("PS", ps_gc), rGG], writes=[rXP])
            S.op("dve", lambda e: e.tensor_tensor(out=xp[:, 2 + nt:2 + npc], in0=PSM[ps_gc][:, 1, 0:npc - nt],
                                                  in1=gg[:, nt:npc], op=ALU.mult),
                 reads=[R("PS", ps_gc), rGG], writes=[rXP])
            S.op("dve", lambda e: e.tensor_tensor(out=V(xp, nsx + 2, [10, SQ], [1, 8]),
                                                  in0=V(PSM[ps_gc], 512 + npc - nt, [8, SQ], [1, 8]),
                                                  in1=V(gg, npc, [8, SQ], [1, 8]), op=ALU.mult),
                 reads=[R("PS", ps_gc), rGG], writes=[rXP])
            tails2_save(xp, rXP, rXPt, PS_sc, R("PS_sc", gi), ST_sc, R("ST_sc", gi), gi, g.p, npc)
            ws_gb = load_w(wA[24 + gi], 2048)
            ps_gb = next_ps()
            mm_group(ps_gb, ws_gb, 0, P, KD, srcN, resN, nt)
            S.op("act", lambda e: e.activation(out=vh(ub), in_=PSM[ps_gb][:, :, 0:nt], func=AF.Copy),
                 reads=[R("PS", ps_gb)], writes=[rU])
            conv3(xp, xc, rXP, rXPt, rXC, npc, C_CBW + gi * 3, None)
            S.op("dve", lambda e: e.tensor_tensor(out=gg[:, 0:ncols], in0=xc[:, 0:ncols], in1=ub[:, 0:ncols], op=ALU.mult),
                 reads=[rXC, rU, rGG], writes=[rGG])

        def phB():
            pass

        def phC():
            norm_phase(g, gg, [rGG], b, C_GOB + gi, out_slot)

        return [phA, phB, phC]

    def wo_partial(grp, nk, g):
        nt = g.nt
        half = (grp % 2) * YG
        for dd in range(4):
            ws = load_w(wO[grp * 4 + dd][:, 0:nk * 512], nk * 512)
            for d4 in range(4):
                d = dd * 4 + d4
                ps = next_ps()
                mm_group(ps, ws, d4 * P, 512, nk, lambda k, h: YH[:, half + k, h * nt:(h + 1) * nt],
                         lambda k: R("YH", half + k), nt)
                S.op("dve", lambda e, d=d, ps=ps: e.tensor_tensor(out=V(X, d * NC, [nt, 2], [1, nt]),
                                                                in0=PSM[ps][:, :, 0:nt],
                                                                in1=V(X, d * NC, [nt, 2], [1, nt]), op=ALU.add),
                     reads=[R("PS", ps), R("X", d)], writes=[R("X", d)])

    def ffn_unit(j, g, out_slot):
        b = bufs()
        npc, nt, ncols, nh = g.npc, g.nt, g.ncols, g.nh
        xp, xc, gg = b.xp, b.xc, b.gg
        rXP, rXPt, rXC, rGG = b.rXP, b.rXPt, b.rXC, b.rGG
        srcN = lambda k, h: N[:, k, h * nt:(h + 1) * nt]
        resN = lambda k: R("N", k)
        nsx = npc + 2
        vh = lambda t: V(t, 0, [nt, nh], [1, nt])

        def phA():
            ws_g = load_w(wU[j], 2048)
            ps_g = next_ps()
            mm_group(ps_g, ws_g, 0, P, KD, srcN, resN, nt)
            tails2(xp, rXPt, PS_fc, R("PS_fc", j), ST_fc, R("ST_fc", j), j, g.p, npc)
            S.op("act", lambda e: e.activation(out=xp[:, 2:2 + nt], in_=PSM[ps_g][:, 0, 0:nt], func=AF.Copy),
                 reads=[R("PS", ps_g)], writes=[rXP])
            S.op("act", lambda e: e.activation(out=xp[:, 2 + nt:2 + npc], in_=PSM[ps_g][:, 1, 0:npc - nt], func=AF.Copy),
                 reads=[R("PS", ps_g)], writes=[rXP])
            S.op("act", lambda e: e.activation(out=V(xp, nsx + 2, [10, SQ], [1, 8]),
                                               in_=V(PSM[ps_g], 512 + npc - nt, [8, SQ], [1, 8]), func=AF.Copy),
                 reads=[R("PS", ps_g)], writes=[rXP])
            tails2_save(xp, rXP, rXPt, PS_fc, R("PS_fc", j), ST_fc, R("ST_fc", j), j, g.p, npc)
            ws_v = load_w(wU[NF + j], 2048)
            ps_v = next_ps()
            mm_group(ps_v, ws_v, 0, P, KD, srcN, resN, nt)
            S.op("act", lambda e: e.activation(out=vh(gg), in_=PSM[ps_v][:, :, 0:nt], func=AF.Copy),
                 reads=[R("PS", ps_v)], writes=[rGG])
            conv3(xp, xc, rXP, rXPt, rXC, npc, C_CFW + j * 3, C_CFB + j)

        def phB():
            S.op("act", lambda e: e.activation(out=xc[:, 0:ncols], in_=xc[:, 0:ncols], func=AF.Gelu_apprx_tanh),
                 reads=[rXC], writes=[rXC])
            S.op("dve", lambda e: e.tensor_tensor(out=YH[:, out_slot, 0:ncols], in0=gg[:, 0:ncols], in1=xc[:, 0:ncols],
                                                  op=ALU.mult),
                 reads=[rGG, rXC], writes=[R("YH", out_slot)])

        return [phA, phB]

    def down_partial(grp, g):
        nt = g.nt
        half = (grp % 2) * YG
        for dd in range(4):
            ws = load_w(wD[grp * 4 + dd], 2048)
            for d4 in range(4):
                d = dd * 4 + d4
                ps = next_ps()
                mm_group(ps, ws, d4 * P, 512, YG, lambda k, h: YH[:, half + k, h * nt:(h + 1) * nt],
                         lambda k: R("YH", half + k), nt)
                S.op("dve", lambda e, d=d, ps=ps: e.tensor_tensor(out=V(X, d * NC, [nt, 2], [1, nt]),
                                                                in0=PSM[ps][:, :, 0:nt],
                                                                in1=V(X, d * NC, [nt, 2], [1, nt]), op=ALU.add),
                     reads=[R("PS", ps), R("X", d)], writes=[R("X", d)])

    def run_pipelined(units, on_done=None, lags=(0, 2, 3), newest_first=True, done_delay=1):
        n = len(units)
        pending = []
        for t in range(n + max(lags) + done_delay + 1):
            for ph in (range(len(lags)) if newest_first else reversed(range(len(lags)))):
                u = t - lags[ph]
                if 0 <= u < n and ph < len(units[u]):
                    units[u][ph]()
                    if ph == len(units[u]) - 1:
                        pending.append((t + done_delay, u))
            if on_done is not None:
                for (td, u) in [x for x in pending if x[0] <= t]:
                    on_done(u)
            pending = [x for x in pending if x[0] > t]

    def store_y(p):
        tiles = [(i * P, P) for i in range(4)] + [(512, 72)]
        for ti, (c0, nr) in enumerate(tiles):
            s = nxt("io", NIO)
            for kh in range(2):
                ps = next_ps()
                for i in range(8):
                    k = kh * 8 + i
                    S.op("pe", lambda e, k=k, i=i, ps=ps, c0=c0, nr=nr: e.transpose(
                        out=VP(PSM[ps], i * P, nr, [1, P]), in_=X[:, k, c0:c0 + nr], identity=IDT[:]),
                        reads=[R("X", k), rIDT], writes=[R("PS", ps)])
                copy_op(evac_eng(), IO[s][0:nr, kh * 1024:(kh + 1) * 1024], VP(PSM[ps], 0, nr, [1, 1024]),
                        [R("PS", ps)], [R("IO", s)])
            if ti < 4:
                S.dma(STQ, lambda e, s=s, c0=c0, p=p: e.dma_start(out=y_p[p * PM + c0:p * PM + c0 + P, :], in_=IO[s][:, :]),
                      reads=[R("IO", s)], lane=("io", s))
            else:
                S.dma(STQ, lambda e, s=s, p=p: e.dma_start(out=y_p[p * PM + 512:p * PM + 520, :], in_=IO[s][0:8, :]),
                      reads=[R("IO", s)], lane=("io", s))
                S.dma(STQ, lambda e, s=s, p=p: e.dma_start(out=y_s[p * NS:(p + 1) * NS, :], in_=IO[s][8:72, :]),
                      reads=[R("IO", s)], lane=("io", s))

    def store_states(part):
        def tr_out(src_fn, nblk, nrows, dst_ap_fn, width):
            s = nxt("io", NIO)
            done = 0
            while done < nblk:
                nb = min(8, nblk - done)
                ps = next_ps()
                for i in range(nb):
                    src, rres = src_fn(done + i)
                    S.op("pe", lambda e, i=i, ps=ps, src=src: e.transpose(out=VP(PSM[ps], i * P, nrows, [1, P]),
                                                                        in_=src, identity=IDT[:]),
                         reads=[rres, rIDT], writes=[R("PS", ps)])
                copy_op(evac_eng(), IO[s][0:nrows, done * P:(done + nb) * P], VP(PSM[ps], 0, nrows, [1, nb * P]),
                        [R("PS", ps)], [R("IO", s)])
                done += nb
            dst_ap_fn(s)

        if part == 0:
          tr_out(lambda n: (ST_h[:, n, :], R("ST_h", n)), NH, 16,
               lambda s: S.dma(STQ, lambda e: e.dma_start(out=o_sh, in_=IO[s][0:16, 0:DA]), reads=[R("IO", s)],
                               lane=("io", s)), DA)
          tr_out(lambda n: (ST_rc[:, n, :], R("ST_rc", n)), NH, 48,
               lambda s: S.dma(STQ, lambda e: e.dma_start(out=o_src, in_=IO[s][0:48, 0:DA]), reads=[R("IO", s)],
                               lane=("io", s)), DA)
          tr_out(lambda gi: (ST_sc[:, gi, :], R("ST_sc", gi)), NG, 32,
               lambda s: S.dma(STQ, lambda e: e.dma_start(out=o_ssc, in_=IO[s][0:32, 0:DB]), reads=[R("IO", s)],
                               lane=("io", s)), DB)
        for q in (range(3) if part == 1 else ()):
            tr_out(lambda j, q=q: (ST_fc[:, q * 16 + j, :], R("ST_fc", q * 16 + j)), 16, 32,
                   lambda s, q=q: S.dma(STQ, lambda e: e.dma_start(out=o_sfc[:, q * 2048:(q + 1) * 2048],
                                                                      in_=IO[s][0:32, :]),
                                        reads=[R("IO", s)], lane=("io", s)), 2048)
        plist = ((PS_h[:, :], NH, o_ph, [R("PS_h", n) for n in range(NH)]),
                 (V(PS_rc, 0, [1, NH * 3]), NH * 3, o_prc, [R("PS_rc", n) for n in range(NH)]),
                 (V(PS_sc, 0, [1, NG * 2]), NG * 2, o_psc, [R("PS_sc", n) for n in range(NG)]),
                 (V(PS_fc, 0, [1, NF * 2]), NF * 2, o_pfc, [R("PS_fc", n) for n in range(NF)]))
        for (src, nrows, dst, res) in (plist[0:3] if part == 0 else plist[3:4]):
            s = nxt("io", NIO)
            ps = next_ps()
            S.op("pe", lambda e, ps=ps, src=src, nrows=nrows: e.transpose(out=VP(PSM[ps], 0, nrows, [1, P]), in_=src,
                                                                         identity=IDT[:]),
                 reads=res + [rIDT], writes=[R("PS", ps)])
            copy_op(evac_eng(), IO[s][0:nrows, 0:P], VP(PSM[ps], 0, nrows, [1, P]), [R("PS", ps)], [R("IO", s)])
            S.dma(STQ, lambda e, s=s, nrows=nrows, dst=dst: e.dma_start(out=dst, in_=IO[s][0:nrows, 0:P]),
                  reads=[R("IO", s)], lane=("io", s))


    gp = Geom()
    gp.npc, gp.nt, gp.ncols, gp.nh, gp.main, gp.p = 512, 512, 512, 1, False, 0
    for q in range(2):
        load_x_tiles([(i * P, P, [(0, P, xw[q * 512 + i * P:q * 512 + (i + 1) * P, :])]) for i in range(4)])
        rmsnorm_fm(512, 512, C_GMIX)
        run_pipelined([lru_unit(n, gp, None) for n in range(NH)], lags=(0, 2))
    S.op("dve", lambda e: e.tensor_scalar(out=PS_h[:], in0=PS_h[:], scalar1=cp(C_FLAG), scalar2=None, op0=ALU.mult),
         reads=[R("PS_h", n) for n in range(NH)] + [rCP], writes=[R("PS_h", n) for n in range(NH)])

    load_states()

    for p in range(2):
        g = Geom()
        g.npc, g.nt, g.ncols, g.nh, g.main, g.p = PM, NT, NC, 2, True, p
        base = NPRE + p * PM
        tiles = [(i * P, P, [(0, P, xw[base + i * P:base + (i + 1) * P, :])]) for i in range(4)]
        tiles.append((512, 72, [(0, 8, xw[base + 512:base + 520, :]), (8, 64, xs[p * NS:(p + 1) * NS, :])]))
        load_x_tiles(tiles)
        rmsnorm_fm(NC, NT, C_GMIX)
        units = []
        for c in range(NMIX):
            slot = c % (2 * YG)
            units.append(lru_unit(c, g, slot) if c < NH else sconv_unit(c - NH, g, slot))

        def mix_done(u, g=g):
            if u % YG == YG - 1:
                wo_partial(u // YG, YG, g)
        run_pipelined(units, mix_done)
        rmsnorm_fm(NC, NT, C_GFFN)
        if p == 1:
            store_states(0)
        funits = [ffn_unit(j, g, j % (2 * YG)) for j in range(NF)]

        def ffn_done(u, g=g):
            if u % YG == YG - 1:
                down_partial(u // YG, g)
        run_pipelined(funits, ffn_done, lags=(0, 1), newest_first=False)
        rmsnorm_fm(NC, NT, C_GFIN, rounded=False)
        store_y(p)
    store_states(1)

    S.emit(nc, es)
    es.close()
    return nc


_PROG = {}


def _tile_cols(w, ncb):
    K = w.shape[0]
    return np.ascontiguousarray(w.reshape(K // P, P, ncb, P).transpose(2, 1, 0, 3)).reshape(ncb, P, (K // P) * P)


def _tile_rows(w, gk):
    K = w.shape[0]
    ng = K // (gk * P)
    a = w.reshape(ng, gk, P, 4, 512).transpose(0, 3, 2, 1, 4)
    return np.ascontiguousarray(a).reshape(ng * 4, P, gk * 512)


def _fm(v, n):
    return np.ascontiguousarray(v.reshape(n, P).T)


def kernel(x_prompt, x_sample, state_lru_h, state_lru_conv, state_sconv, state_ffn_conv, meta_tokens, g_mix, w_in,
           conv_a_w, conv_a_b, w_gate_a, b_gate_a, w_gate_x, b_gate_x, lru_lambda, conv_b_w, g_out_a, g_out_b, w_o,
           g_ffn, w_up, conv_f_w, conv_f_b, w_down, g_final):
    f32 = np.float32
    A = lambda a: np.asarray(a, dtype=f32)
    x_prompt, x_sample = A(x_prompt), A(x_sample)
    import os
    stage = os.environ.get("KSTAGE")
    key = ("nc", stage)
    if key not in _PROG:
        _PROG[key] = build_program(stop_after=None if stage is None else int(stage))
    nc = _PROG[key]
    wA_ = _tile_cols(A(w_in)[0], 48)
    wU_ = _tile_cols(A(w_up)[0], 96)
    wO_ = _tile_rows(A(w_o)[0], YG)
    wD_ = _tile_rows(A(w_down)[0], YG)
    wG_ = np.ascontiguousarray(np.concatenate([A(w_gate_a)[0], A(w_gate_x)[0]], axis=2))
    cpar = np.zeros((P, NCPAR), f32)
    cpar[:, C_GMIX:C_GMIX + 16] = _fm(A(g_mix)[0], 16)
    cpar[:, C_GFFN:C_GFFN + 16] = _fm(A(g_ffn)[0], 16)
    cpar[:, C_GFIN:C_GFIN + 16] = _fm(A(g_final), 16)
    cpar[:, C_CAB:C_CAB + 12] = _fm(A(conv_a_b)[0], 12)
    cpar[:, C_BGA:C_BGA + 12] = _fm(A(b_gate_a)[0], 12)
    cpar[:, C_BGX:C_BGX + 12] = _fm(A(b_gate_x)[0], 12)
    cpar[:, C_LAM:C_LAM + 12] = _fm(A(lru_lambda)[0], 12)
    cpar[:, C_GOA:C_GOA + 12] = _fm(A(g_out_a)[0], 12)
    cpar[:, C_GOB:C_GOB + 8] = _fm(A(g_out_b)[0], 8)
    cpar[:, C_CFB:C_CFB + 48] = _fm(A(conv_f_b)[0], 48)
    caw = A(conv_a_w)[0]
    cpar[:, C_CAW:C_CAW + 48] = caw.reshape(4, NH, P).transpose(2, 1, 0).reshape(P, 48)
    cbw = A(conv_b_w)[0]
    cpar[:, C_CBW:C_CBW + 24] = cbw.reshape(3, NG, P).transpose(2, 1, 0).reshape(P, 24)
    cfw = A(conv_f_w)[0]
    cpar[:, C_CFW:C_CFW + 144] = cfw.reshape(3, NF, P).transpose(2, 1, 0).reshape(P, 144)
    ident = np.eye(P, dtype=f32)
    meta = A(meta_tokens)
    in_maps = []
    for c in range(NCORES):
        b, half = c // 2, c % 2
        seq = np.concatenate([meta, x_prompt[b]], axis=0)
        if half == 0:
            xw_ = np.concatenate([np.zeros((1032, D), f32), seq[0:1032]], axis=0)
        else:
            xw_ = seq
        cp_c = cpar.copy()
        cp_c[:, C_FLAG] = float(half)
        sl = slice(16 * c, 16 * c + 16)
        st_hr = np.concatenate([A(state_lru_h)[0, sl], A(state_lru_conv)[0, sl].reshape(48, DA)], axis=0)
        in_maps.append({
            "xw": np.ascontiguousarray(xw_), "xs": np.ascontiguousarray(x_sample[sl].reshape(128, D)),
            "st_hr": np.ascontiguousarray(st_hr),
            "st_sc": np.ascontiguousarray(A(state_sconv)[0, sl].reshape(32, DB)),
            "st_fc": np.ascontiguousarray(A(state_ffn_conv)[0, sl].reshape(32, DFF)),
            "cpar": cp_c, "ident": ident, "wA": wA_, "wG": wG_, "wO": wO_, "wU": wU_, "wD": wD_,
        })
    res = run_bass_kernel_spmd(nc, in_maps, core_ids=list(range(NCORES)))
    r = res.results
    B = x_prompt.shape[0]
    y_prompt = np.zeros((B, 2048, D), f32)
    y_sample = np.zeros((128, 8, D), f32)
    p_h = np.zeros((1, B, DA), f32)
    p_rc = np.zeros((1, B, 3, DA), f32)
    p_sc = np.zeros((1, B, 2, DB), f32)
    p_fc = np.zeros((1, B, 2, DFF), f32)
    s_h = np.zeros((1, 128, DA), f32)
    s_rc = np.zeros((1, 128, 3, DA), f32)
    s_sc = np.zeros((1, 128, 2, DB), f32)
    s_fc = np.zeros((1, 128, 2, DFF), f32)
    for c in range(NCORES):
        b, half = c // 2, c % 2
        yp = r[c]["y_p"][HALO:]
        if half == 0:
            y_prompt[b, 0:1016] = yp[16:1032]
        else:
            y_prompt[b, 1016:2048] = yp
            p_h[0, b] = r[c]["o_ph"].reshape(DA)
            p_rc[0, b] = r[c]["o_prc"].reshape(NH, 3, P).transpose(1, 0, 2).reshape(3, DA)
            p_sc[0, b] = r[c]["o_psc"].reshape(NG, 2, P).transpose(1, 0, 2).reshape(2, DB)
            p_fc[0, b] = r[c]["o_pfc"].reshape(NF, 2, P).transpose(1, 0, 2).reshape(2, DFF)
        sl = slice(16 * c, 16 * c + 16)
        y_sample[sl] = r[c]["y_s"].reshape(16, 8, D)
        s_h[0, sl] = r[c]["o_sh"]
        s_rc[0, sl] = r[c]["o_src"].reshape(16, 3, DA)
        s_sc[0, sl] = r[c]["o_ssc"].reshape(16, 2, DB)
        s_fc[0, sl] = r[c]["o_sfc"].reshape(16, 2, DFF)
    return (y_prompt, y_sample, p_h, p_rc, p_sc, p_fc, s_h, s_rc, s_sc, s_fc)
```

```python
import numpy as np
from contextlib import ExitStack
import concourse.bass as bass
import concourse.mybir as mybir
from concourse.ap import AP
from concourse.bass_utils import run_bass_kernel_spmd

F32 = mybir.dt.float32
F32R = mybir.dt.float32r
AF = mybir.ActivationFunctionType
ALU = mybir.AluOpType

NCORES = 8
P = 128
D = 2048
KD = 16
DA = 1536
NH = 12
DB = 1024
NG = 8
DFF = 6144
NF = 48
NMIX = 20
DIN = 2 * DA + 3 * DB
HALO = 8
NPRE = 1024
NMAIN = 1040
PM = 520
SQ = 8
NS = 64
NC = 584
NT = 292
WBW = 616
YG = 4
EPS = 1e-6

C_GMIX, C_GFFN, C_GFIN = 0, 16, 32
C_CAB, C_BGA, C_BGX, C_LAM, C_GOA, C_GOB, C_CFB = 48, 60, 72, 84, 96, 108, 116
C_CAW, C_CBW, C_CFW, C_FLAG = 164, 212, 236, 380
NCPAR = 384
DC_HBA, DC_HBX, DC_C, DC_CH, DC_EPS, DC_ONE = 0, 12, 24, 36, 48, 49
NDC = 64


class Res:
    __slots__ = ("name", "w", "r")

    def __init__(self, name):
        self.name = name
        self.w = None
        self.r = []


class Op:
    __slots__ = ("eng", "fn", "deps", "lane", "lane_idx", "sig", "tick", "is_dma")


class Sched:
    ENGS = ("pe", "act", "dve", "pool", "sp")

    def __init__(self):
        self.streams = {e: [] for e in self.ENGS}
        self.lanes = {}
        self.resd = {}

    def R(self, *key):
        r = self.resd.get(key)
        if r is None:
            r = Res(key)
            self.resd[key] = r
        return r

    def _rec(self, op, reads, writes):
        deps = {}
        for r in reads:
            if r.w is not None:
                deps[id(r.w)] = (r.w, True)
        for w in writes:
            if w.w is not None and id(w.w) not in deps:
                deps[id(w.w)] = (w.w, False)
            for rd in w.r:
                if id(rd) not in deps:
                    deps[id(rd)] = (rd, False)
        fin = []
        for p, raw in deps.values():
            if p is op:
                continue
            if (not p.is_dma) and (not op.is_dma) and p.eng == op.eng and not raw:
                continue
            fin.append(p)
            if not p.is_dma:
                p.sig = True
        op.deps = fin
        for r in reads:
            r.r.append(op)
        for w in writes:
            w.w = op
            w.r = []
        self.streams[op.eng].append(op)

    def op(self, eng, fn, reads=(), writes=()):
        def _banks(lst):
            out = []
            for r in lst:
                if r.name[0] == "PS":
                    h = r.name[1]
                    bl = [h[1]] if isinstance(h, tuple) else [2 * h, 2 * h + 1]
                    out += [self.R("PSB", b) for b in bl]
            return out
        if any(r.name[0] == "PS" for r in list(reads) + list(writes)):
            writes = [r for r in writes if r.name[0] != "PS"] + _banks(reads) + _banks(writes)
            reads = [r for r in reads if r.name[0] != "PS"]
        o = Op()
        o.eng = eng
        o.fn = fn
        o.is_dma = False
        o.sig = False
        o.tick = None
        o.lane = None
        o.lane_idx = None
        self._rec(o, reads, writes)
        return o

    def dma(self, queue, fn, reads=(), writes=(), lane=None, bulk=False):
        o = Op()
        o.eng = queue
        o.fn = fn
        o.is_dma = True
        o.sig = True
        o.tick = None
        ln = self.lanes.setdefault(lane, [0, bulk])
        o.lane = lane
        o.lane_idx = ln[0]
        ln[0] += 1
        self._rec(o, reads, writes)
        return o

    def emit(self, nc, es):
        for e in self.ENGS:
            t = 0
            for o in self.streams[e]:
                if not o.is_dma and o.sig:
                    t += 1
                    o.tick = t
        esem = {e: es.enter_context(nc.semaphore("sem_" + e)) for e in self.ENGS if e != "sp"}
        lsem = {ln: es.enter_context(nc.semaphore("lane_" + str(ln))) for ln in self.lanes}
        store_lanes = {}
        for e in self.ENGS:
            for o in self.streams[e]:
                if o.is_dma:
                    store_lanes.setdefault(e, set()).add(o.lane)

        def run_stream(ename, eng):
            waited = {}
            for o in self.streams[ename]:
                need = {}
                for p in o.deps:
                    if p.is_dma:
                        cnt, bulk = self.lanes[p.lane]
                        val = 16 * (cnt if bulk else (p.lane_idx + 1))
                        key = ("l", p.lane)
                        sem = lsem[p.lane]
                    else:
                        val = p.tick
                        key = ("e", p.eng)
                        sem = esem[p.eng]
                    if need.get(key, (None, 0))[1] < val:
                        need[key] = (sem, val)
                for key, (sem, val) in need.items():
                    if waited.get(key, 0) >= val:
                        continue
                    eng.wait_ge(sem, val)
                    waited[key] = val
                ins = o.fn(eng)
                if o.is_dma:
                    ins.then_inc(lsem[o.lane], 16)
                elif o.sig:
                    ins.then_inc(esem[ename], 1)
            for ln in sorted(store_lanes.get(ename, ()), key=str):
                val = 16 * self.lanes[ln][0]
                if waited.get(("l", ln), 0) < val:
                    eng.wait_ge(lsem[ln], val)

        block = es.enter_context(nc.Block())

        @block.sync
        def _(e):
            run_stream("sp", e)

        @block.tensor
        def _(e):
            run_stream("pe", e)

        @block.scalar
        def _(e):
            run_stream("act", e)

        @block.vector
        def _(e):
            run_stream("dve", e)

        @block.gpsimd
        def _(e):
            run_stream("pool", e)


def build_program(stop_after=None, dbg=False):
    nc = bass.Bass("TRN2", target_bir_lowering=False)
    nc.dge_precook = False
    S = Sched()
    R = S.R
    es = ExitStack()
    import os as _os
    STQ = _os.environ.get("KSTQ", "pool")

    def din(name, shape, dt=F32):
        return nc.dram_tensor(name, shape, dt, kind="ExternalInput").ap()

    def dout(name, shape, dt=F32):
        return nc.dram_tensor(name, shape, dt, kind="ExternalOutput").ap()

    xw = din("xw", [NPRE + NMAIN, D])
    xs = din("xs", [128, D])
    st_hr = din("st_hr", [64, DA])
    st_sc = din("st_sc", [32, DB])
    st_fc = din("st_fc", [32, DFF])
    cpar = din("cpar", [P, NCPAR])
    identd = din("ident", [P, P])
    wA = din("wA", [48, P, 2048], F32R)
    wG = din("wG", [NH, P, 256], F32R)
    wO = din("wO", [20, P, 2048], F32R)
    wU = din("wU", [96, P, 2048], F32R)
    wD = din("wD", [48, P, 2048], F32R)
    y_p = dout("y_p", [NMAIN, D])
    y_s = dout("y_s", [128, D])
    o_sh = dout("o_sh", [16, DA])
    o_src = dout("o_src", [48, DA])
    o_ssc = dout("o_ssc", [32, DB])
    o_sfc = dout("o_sfc", [32, DFF])
    o_ph = dout("o_ph", [NH, P])
    o_prc = dout("o_prc", [NH * 3, P])
    o_psc = dout("o_psc", [NG * 2, P])
    o_pfc = dout("o_pfc", [NF * 2, P])

    def sb(name, shape, dt=F32):
        return es.enter_context(nc.sbuf_tensor(name, shape, dt))

    X = sb("X", [P, KD, NC])
    N = sb("N", [P, KD, NC], F32R)
    YH = sb("YH", [P, 2 * YG, NC], F32R)
    NWS = 4
    Wsl = [sb(f"W{i}", [P, 2048], F32R) for i in range(NWS)]
    NGS = 4
    GWsl = [sb(f"GW{i}", [P, 256], F32R) for i in range(NGS)]
    NIO = 2
    IO = [sb(f"IO{i}", [P, 2048]) for i in range(NIO)]
    XP = [sb(f"XP{i}", [P, WBW]) for i in range(2)]
    XC = [sb(f"XC{i}", [P, WBW]) for i in range(3)]
    GG = [sb(f"GG{i}", [P, WBW]) for i in range(4)]
    SB_ = [sb(f"SB{i}", [P, WBW]) for i in range(2)]
    AB = [sb(f"AB{i}", [P, WBW]) for i in range(2)]
    UB = [sb(f"UB{i}", [P, WBW]) for i in range(2)]
    SQB = [sb(f"SQB{i}", [P, NC], F32R) for i in range(2)]
    XCR = [sb(f"XCR{i}", [P, NC], F32R) for i in range(3)]
    RB = sb("RB", [P, NC])
    CP = sb("CP", [P, NCPAR])
    DC = sb("DC", [P, NDC])
    IDT = sb("IDT", [P, P])
    ONES = sb("ONES", [P, P], F32R)
    ST_h = sb("ST_h", [P, NH, 16])
    ST_rc = sb("ST_rc", [P, NH, 48])
    ST_sc = sb("ST_sc", [P, NG, 32])
    ST_fc = sb("ST_fc", [P, NF, 32])
    PS_h = sb("PS_h", [P, NH])
    PS_rc = sb("PS_rc", [P, NH, 3])
    PS_sc = sb("PS_sc", [P, NG, 2])
    PS_fc = sb("PS_fc", [P, NF, 2])
    NPS = 4
    PSM = [es.enter_context(nc.psum_tensor(f"PSM{i}", [P, 2, 512], F32)) for i in range(NPS)]

    def pst(t):
        return t[:].ap[0][0]

    def V(t, off, *dims, dt=None):
        a = AP(t, off, [[pst(t), P]] + [list(d) for d in dims])
        if dt is not None:
            a = a.bitcast(dt)
        return a

    def VP(t, off, npart, *dims):
        return AP(t, off, [[pst(t), npart]] + [list(d) for d in dims])

    cnt = {"w": 0, "g": 0, "io": 0, "ps": 0, "sq": 0}

    def nxt(k, n):
        i = cnt[k] % n
        cnt[k] += 1
        return i

    def load_w(src_ap, ncols):
        s = nxt("w", NWS)
        S.dma("sp", lambda e, s=s: e.dma_start(out=Wsl[s][:, 0:ncols], in_=src_ap),
              writes=[R("W", s)], lane=("w", s))
        return s

    def next_ps():
        return nxt("ps", NPS)

    cnt["psb"] = 0

    def next_psb():
        return ("h", nxt("psb", 2 * NPS))

    def PT(ps):
        return PSM[ps[1] // 2] if isinstance(ps, tuple) else PSM[ps]

    def PB(ps):
        return (ps[1] % 2) if isinstance(ps, tuple) else 0

    cp = lambda c0, n=1: CP[:, c0:c0 + n]
    dc = lambda c0, n=1: DC[:, c0:c0 + n]
    rCP, rDC, rIDT, rONES = R("CP"), R("DC"), R("IDT"), R("ONES")

    S.dma("sp", lambda e: e.dma_start(out=CP[:], in_=cpar), writes=[rCP], lane="const", bulk=True)
    S.dma("sp", lambda e: e.dma_start(out=IDT[:], in_=identd), writes=[rIDT], lane="const", bulk=True)
    S.op("dve", lambda e: e.memset(RB[:, 0:P], 1.0), writes=[R("RB")])
    S.op("dve", lambda e: e.tensor_copy(out=ONES[:], in_=RB[:, 0:P]), reads=[R("RB")], writes=[rONES])
    S.op("dve", lambda e: e.memset(DC[:, DC_EPS:DC_EPS + 1], EPS), writes=[R("DCe")])
    S.op("dve", lambda e: e.memset(DC[:, DC_ONE:DC_ONE + 1], 1.0), writes=[R("DCo")])
    S.op("dve", lambda e: e.memset(PS_h[:], 0.0), writes=[R("PS_h", n) for n in range(NH)])
    S.op("dve", lambda e: e.memset(PS_rc[:], 0.0), writes=[R("PS_rc", n) for n in range(NH)])
    S.op("dve", lambda e: e.memset(PS_sc[:], 0.0), writes=[R("PS_sc", g) for g in range(NG)])
    S.op("dve", lambda e: e.memset(PS_fc[:], 0.0), writes=[R("PS_fc", j) for j in range(NF)])
    S.op("dve", lambda e: e.tensor_scalar(out=dc(DC_HBA, 24), in0=cp(C_BGA, 24), scalar1=0.5, scalar2=None,
                                          op0=ALU.mult), reads=[rCP], writes=[R("DChb")])
    if "c_act" not in _os.environ.get("KSKIP", "").split(","):
        S.op("act", lambda e: e.activation(out=dc(DC_C, 12), in_=cp(C_LAM, 12), func=AF.Exp, scale=-1.0),
             reads=[rCP], writes=[R("DCc")])
        S.op("act", lambda e: e.activation(out=dc(DC_C, 12), in_=dc(DC_C, 12), func=AF.Ln, bias=dc(DC_ONE), scale=1.0),
             reads=[R("DCc"), R("DCo")], writes=[R("DCc")])
    S.op("dve", lambda e: e.tensor_scalar(out=dc(DC_CH, 12), in0=dc(DC_C, 12), scalar1=-4.0, scalar2=None,
                                          op0=ALU.mult), reads=[R("DCc")], writes=[R("DCch")])
    S.op("dve", lambda e: e.tensor_scalar(out=dc(DC_C, 12), in0=dc(DC_C, 12), scalar1=-8.0, scalar2=None,
                                          op0=ALU.mult), reads=[R("DCc"), R("DCch")], writes=[R("DCc")])
    rCONST = [rCP, R("DChb"), R("DCc"), R("DCch"), R("DCe"), R("DCo")]

    flip = [0]

    def evac_eng():
        flip[0] ^= 1
        return "act" if flip[0] else "dve"

    def copy_op(eng, out, in_, reads, writes):
        if eng == "act":
            S.op("act", lambda e: e.activation(out=out, in_=in_, func=AF.Copy), reads=reads, writes=writes)
        else:
            S.op(eng, lambda e: e.tensor_copy(out=out, in_=in_), reads=reads, writes=writes)

    def load_states():
        KS = _os.environ.get("KSKIP", "").split(",")
        if "hr" in KS:
            return
        s = nxt("io", NIO)
        S.dma("sp", lambda e: e.dma_start(out=IO[s][0:64, 0:DA], in_=st_hr), writes=[R("IO", s)], lane=("io", s))
        ps = next_ps()
        for n in range(NH):
            S.op("pe", lambda e, n=n: e.transpose(out=V(PSM[ps], n * 64, [1, 64]), in_=IO[s][0:64, n * P:(n + 1) * P],
                                                  identity=IDT[0:64, 0:64]),
                 reads=[R("IO", s), rIDT], writes=[R("PS", ps)])
        copy_op("act", ST_h[:], V(PSM[ps], 0, [64, NH], [1, 16]), [R("PS", ps)], [R("ST_h", n) for n in range(NH)])
        if "rc" not in KS:
            copy_op("dve", ST_rc[:], V(PSM[ps], 16, [64, NH], [1, 48]), [R("PS", ps)], [R("ST_rc", n) for n in range(NH)])
        if "sc" in KS:
            return
        s2 = nxt("io", NIO)
        S.dma("sp", lambda e: e.dma_start(out=IO[s2][0:32, 0:DB], in_=st_sc), writes=[R("IO", s2)], lane=("io", s2))
        ps2 = next_ps()
        for g in range(NG):
            S.op("pe", lambda e, g=g: e.transpose(out=V(PSM[ps2], g * 64, [1, 64]), in_=IO[s2][0:64, g * P:(g + 1) * P],
                                                  identity=IDT[0:64, 0:64]),
                 reads=[R("IO", s2), rIDT], writes=[R("PS", ps2)])
        copy_op("act", ST_sc[:], V(PSM[ps2], 0, [64, NG], [1, 32]), [R("PS", ps2)], [R("ST_sc", g) for g in range(NG)])
        if "fc" in KS:
            return
        for q in range(3):
            s3 = nxt("io", NIO)
            S.dma("sp", lambda e, q=q, s3=s3: e.dma_start(out=IO[s3][0:32, :], in_=st_fc[:, q * 2048:(q + 1) * 2048]),
                  writes=[R("IO", s3)], lane=("io", s3))
            for hh in range(2):
                ps3 = next_ps()
                for i in range(8):
                    ii = hh * 8 + i
                    S.op("pe", lambda e, i=i, ii=ii, s3=s3, ps3=ps3: e.transpose(out=V(PSM[ps3], i * 64, [1, 64]),
                                                                                 in_=IO[s3][0:64, ii * P:(ii + 1) * P],
                                                                                 identity=IDT[0:64, 0:64]),
                         reads=[R("IO", s3), rIDT], writes=[R("PS", ps3)])
                j0 = q * 16 + hh * 8
                copy_op(evac_eng(), ST_fc[:, j0:j0 + 8, :], V(PSM[ps3], 0, [64, 8], [1, 32]), [R("PS", ps3)],
                        [R("ST_fc", j) for j in range(j0, j0 + 8)])

    def load_x_tiles(tiles):
        for (col0, nr, parts) in tiles:
            s = nxt("io", NIO)
            for (r0, n_, src) in parts:
                S.dma("sp", lambda e, s=s, r0=r0, n_=n_, src=src: e.dma_start(out=IO[s][r0:r0 + n_, :], in_=src),
                      writes=[R("IO", s)], lane=("io", s))
            for kh in range(2):
                ps = next_ps()
                for i in range(8):
                    k = kh * 8 + i
                    S.op("pe", lambda e, s=s, k=k, i=i, ps=ps, nr=nr: e.transpose(
                        out=V(PSM[ps], i * P, [1, nr]), in_=IO[s][0:nr, k * P:(k + 1) * P], identity=IDT[0:nr, 0:nr]),
                        reads=[R("IO", s), rIDT], writes=[R("PS", ps)])
                copy_op(evac_eng(), X[:, kh * 8:kh * 8 + 8, col0:col0 + nr], V(PSM[ps], 0, [P, 8], [1, nr]),
                        [R("PS", ps)], [R("X", k) for k in range(kh * 8, kh * 8 + 8)])

    def rmsnorm_fm(ncols, nt, gcol, rounded=True):
        nh = ncols // nt
        ps = next_ps()
        scr = [(SQB[0], R("SQ", 0)), (SQB[1], R("SQ", 1)), (XCR[0], R("XCR", 0)), (XCR[1], R("XCR", 1)),
               (XCR[2], R("XCR", 2))]
        engs = ["act", "dve", "act", "pool"]
        for k in range(KD):
            sq, rsq = scr[k % len(scr)]
            eng = engs[k % len(engs)]
            if eng == "act":
                S.op("act", lambda e, k=k, sq=sq: e.activation(out=sq[:, 0:ncols], in_=X[:, k, 0:ncols], func=AF.Square),
                     reads=[R("X", k)], writes=[rsq])
            else:
                S.op(eng, lambda e, k=k, sq=sq: e.tensor_tensor(out=sq[:, 0:ncols], in0=X[:, k, 0:ncols],
                                                               in1=X[:, k, 0:ncols], op=ALU.mult),
                     reads=[R("X", k)], writes=[rsq])
            for h in range(nh):
                S.op("pe", lambda e, k=k, sq=sq, h=h: e.matmul(PSM[ps][:, h, 0:nt], ONES[:], sq[:, h * nt:(h + 1) * nt],
                                                            start=(k == 0), stop=(k == KD - 1)),
                     reads=[rsq, rONES], writes=[R("PS", ps)])
        S.op("act", lambda e: e.activation(out=V(RB, 0, [nt, nh], [1, nt]), in_=PSM[ps][:, 0:nh, 0:nt], func=AF.Ln,
                                           bias=dc(DC_EPS), scale=1.0 / D),
             reads=[R("PS", ps), R("DCe")], writes=[R("RB")])
        S.op("act", lambda e: e.activation(out=RB[:, 0:ncols], in_=RB[:, 0:ncols], func=AF.Exp, scale=-0.5),
             reads=[R("RB")], writes=[R("RB")])
        for k in range(KD):
            o = N[:, k, 0:ncols] if rounded else X[:, k, 0:ncols]
            S.op("dve", lambda e, k=k, o=o: e.scalar_tensor_tensor(out=o, in0=X[:, k, 0:ncols], scalar=cp(gcol + k),
                                                                 in1=RB[:, 0:ncols], op0=ALU.mult, op1=ALU.mult),
                 reads=[R("X", k), R("RB"), rCP], writes=[R("N", k) if rounded else R("X", k)])

    def mm_group(ps, wslot, wcol0, wkstride, nk, src_fn, src_res, nt, nhalf=2):
        for k in range(nk):
            for h in range(nhalf):
                S.op("pe", lambda e, k=k, h=h: e.matmul(PT(ps)[:, PB(ps) + h, 0:nt],
                                                       Wsl[wslot][:, k * wkstride + wcol0:k * wkstride + wcol0 + P],
                                                       src_fn(k, h), start=(k == 0), stop=(k == nk - 1)),
                     reads=[R("W", wslot), src_res(k)], writes=[R("PS", ps)])

    class Geom:
        pass

    ucnt = [0]

    def bufs():
        u = ucnt[0]
        ucnt[0] += 1
        b = Geom()
        i2, i3, i4 = u % 2, u % 3, u % 4
        b.xp, b.rXP, b.rXPt = XP[i2], R("XP", i2), R("XPt", i2)
        b.xcr, b.rXCR = XCR[i3], R("XCR", i3)
        b.sb, b.rS = SB_[i2], R("SBf", i2)
        b.ab, b.rA = AB[i2], R("AB", i2)
        b.ub, b.rU = UB[i2], R("UB", i2)
        b.xc, b.rXC, b.rXCs = XC[i3], R("XC", i3), R("XCs", i3)
        b.gg, b.rGG = GG[i4], R("GG", i4)
        return b

    def norm_phase(g, yb, yres, b, gcol, out_slot):
        nt, ncols, nh = g.nt, g.ncols, g.nh
        q = nxt("sq", 2)
        sqb = SQB[q]
        S.op("pool", lambda e: e.tensor_tensor(out=sqb[:, 0:ncols], in0=yb[:, 0:ncols], in1=yb[:, 0:ncols], op=ALU.mult),
             reads=yres, writes=[R("SQ", q)])
        ps = next_ps()
        for h in range(nh):
            S.op("pe", lambda e, h=h: e.matmul(PSM[ps][:, h, 0:nt], ONES[:], sqb[:, h * nt:(h + 1) * nt],
                                               start=True, stop=True),
                 reads=[R("SQ", q), rONES], writes=[R("PS", ps)])
        S.op("act", lambda e: e.activation(out=V(b.ab, 0, [nt, nh], [1, nt]), in_=PSM[ps][:, 0:nh, 0:nt], func=AF.Ln,
                                           bias=dc(DC_EPS), scale=1.0 / P),
             reads=[R("PS", ps), R("DCe")], writes=[b.rA])
        S.op("act", lambda e: e.activation(out=b.ab[:, 0:ncols], in_=b.ab[:, 0:ncols], func=AF.Exp, scale=-0.5),
             reads=[b.rA], writes=[b.rA])
        S.op("dve", lambda e: e.scalar_tensor_tensor(out=YH[:, out_slot, 0:ncols], in0=yb[:, 0:ncols], scalar=cp(gcol),
                                                     in1=b.ab[:, 0:ncols], op0=ALU.mult, op1=ALU.mult),
             reads=yres + [b.rA, rCP], writes=[R("YH", out_slot)])

    def lru_unit(n, g, out_slot):
        b = bufs()
        npc, nt, ncols, nh = g.npc, g.nt, g.ncols, g.nh
        xp, xc, xcr, gg, sbuf_, ab, ub = b.xp, b.xc, b.xcr, b.gg, b.sb, b.ab, b.ub
        rXP, rXPt, rXC, rXCs, rXCR, rGG, rS, rA, rU = b.rXP, b.rXPt, b.rXC, b.rXCs, b.rXCR, b.rGG, b.rS, b.rA, b.rU
        nsx = npc + 3
        st = Geom()
        srcN = lambda k, h: N[:, k, h * nt:(h + 1) * nt]
        resN = lambda k: R("N", k)
        cw = lambda k: cp(C_CAW + n * 4 + k)
        vh = lambda t: V(t, 0, [nt, nh], [1, nt])

        def phA():
            ws_xa = load_w(wA[n], 2048)
            ps_xa = next_ps() if g.main else next_psb()
            mm_group(ps_xa, ws_xa, 0, P, KD, srcN, resN, nt, nh)
            if g.main:
                ws_ga = load_w(wA[NH + n], 2048)
                ps_ga = next_ps()
                mm_group(ps_ga, ws_ga, 0, P, KD, srcN, resN, nt, nh)
            st.gs = nxt("g", NGS)
            gs = st.gs
            S.dma("sp", lambda e: e.dma_start(out=GWsl[gs][:], in_=wG[n]), writes=[R("GW", gs)], lane=("g", gs))
            S.op("pool", lambda e: e.tensor_copy(out=xp[:, 0:3], in_=PS_rc[:, n, :]), reads=[R("PS_rc", n)], writes=[rXPt])
            if g.main:
                S.op("pool", lambda e: e.tensor_copy(out=V(xp, nsx, [11, SQ], [1, 3]),
                                                     in_=V(ST_rc, n * 48 + g.p * 24, [3, SQ], [1, 3])),
                     reads=[R("ST_rc", n)], writes=[rXPt])
                S.op("act", lambda e: e.activation(out=xp[:, 3:3 + nt], in_=PSM[ps_xa][:, 0, 0:nt], func=AF.Copy),
                     reads=[R("PS", ps_xa)], writes=[rXP])
                S.op("act", lambda e: e.activation(out=xp[:, 3 + nt:3 + npc], in_=PSM[ps_xa][:, 1, 0:npc - nt],
                                                   func=AF.Copy),
                     reads=[R("PS", ps_xa)], writes=[rXP])
                S.op("act", lambda e: e.activation(out=V(xp, nsx + 3, [11, SQ], [1, 8]),
                                                   in_=V(PSM[ps_xa], 512 + npc - nt, [8, SQ], [1, 8]), func=AF.Copy),
                     reads=[R("PS", ps_xa)], writes=[rXP])
            else:
                S.op("dve", lambda e: e.tensor_copy(out=V(xp, 3, [nt, nh], [1, nt]),
                                                    in_=PT(ps_xa)[:, PB(ps_xa):PB(ps_xa) + nh, 0:nt]),
                     reads=[R("PS", ps_xa)], writes=[rXP])
            S.op("pool", lambda e: e.tensor_copy(out=PS_rc[:, n, :], in_=xp[:, npc:npc + 3]),
                 reads=[rXP, rXPt], writes=[R("PS_rc", n)])
            if g.main:
                S.op("pool", lambda e: e.tensor_copy(out=V(ST_rc, n * 48 + g.p * 24, [3, SQ], [1, 3]),
                                                     in_=V(xp, nsx + 8, [11, SQ], [1, 3])),
                     reads=[rXP, rXPt], writes=[R("ST_rc", n)])
            S.op("dve", lambda e: e.tensor_scalar(out=xc[:, 0:npc], in0=xp[:, 0:npc], scalar1=cw(0), scalar2=cp(C_CAB + n),
                                                  op0=ALU.mult, op1=ALU.add),
                 reads=[rXP, rXPt, rCP], writes=[rXC])
            for k in range(1, 4):
                o = xcr[:, 0:npc] if k == 3 else xc[:, 0:npc]
                S.op("dve", lambda e, k=k, o=o: e.scalar_tensor_tensor(out=o, in0=xp[:, k:k + npc], scalar=cw(k),
                                                                     in1=xc[:, 0:npc], op0=ALU.mult, op1=ALU.add),
                     reads=[rXP, rXPt, rXC, rCP], writes=[rXCR if k == 3 else rXC])
            if g.main:
                S.op("dve", lambda e: e.tensor_scalar(out=V(xc, npc, [8, SQ], [1, 8]), in0=V(xp, nsx, [11, SQ], [1, 8]),
                                                      scalar1=cw(0), scalar2=cp(C_CAB + n), op0=ALU.mult, op1=ALU.add),
                     reads=[rXP, rXPt, rCP], writes=[rXCs])
                for k in range(1, 4):
                    S.op("dve", lambda e, k=k: e.scalar_tensor_tensor(
                        out=V(xcr if k == 3 else xc, npc, [8, SQ], [1, 8]), in0=V(xp, nsx + k, [11, SQ], [1, 8]),
                        scalar=cw(k), in1=V(xc, npc, [8, SQ], [1, 8]), op0=ALU.mult, op1=ALU.add),
                        reads=[rXP, rXPt, rXCs, rCP], writes=[rXCR if k == 3 else rXCs])
                S.op("act", lambda e: e.activation(out=vh(gg), in_=PSM[ps_ga][:, 0:nh, 0:nt], func=AF.Gelu_apprx_tanh),
                     reads=[R("PS", ps_ga)], writes=[rGG])

        def phB():
            gs = st.gs
            ps_r = next_ps() if g.main else next_psb()
            ps_i = next_ps() if g.main else next_psb()
            for (psx, c0) in ((ps_r, 0), (ps_i, P)):
                for h in range(nh):
                    S.op("pe", lambda e, psx=psx, c0=c0, h=h: e.matmul(PT(psx)[:, PB(psx) + h, 0:nt], GWsl[gs][:, c0:c0 + P],
                                                                     xcr[:, h * nt:(h + 1) * nt], start=True, stop=True),
                         reads=[R("GW", gs), rXCR], writes=[R("PS", psx)])
            S.op("act", lambda e: e.activation(out=vh(sbuf_), in_=PT(ps_r)[:, PB(ps_r):PB(ps_r) + nh, 0:nt], func=AF.Tanh,
                                               bias=dc(DC_HBA + n), scale=0.5),
                 reads=[R("PS", ps_r), R("DChb")], writes=[rS])
            S.op("act", lambda e: e.activation(out=vh(ub), in_=PT(ps_i)[:, PB(ps_i):PB(ps_i) + nh, 0:nt], func=AF.Tanh,
                                               bias=dc(DC_HBX + n), scale=0.5),
                 reads=[R("PS", ps_i), R("DChb")], writes=[rU])
            S.op("act", lambda e: e.activation(out=ab[:, 0:ncols], in_=sbuf_[:, 0:ncols], func=AF.Exp,
                                               bias=dc(DC_CH + n), scale=dc(DC_CH + n)),
                 reads=[rS, R("DCch")], writes=[rA])
            S.op("act", lambda e: e.activation(out=sbuf_[:, 0:ncols], in_=sbuf_[:, 0:ncols], func=AF.Exp,
                                               bias=dc(DC_C + n), scale=dc(DC_C + n)),
                 reads=[rS, R("DCc")], writes=[rS])
            if g.main:
                S.op("act", lambda e: e.activation(out=sbuf_[:, 0:ncols], in_=sbuf_[:, 0:ncols], func=AF.Ln,
                                                   bias=dc(DC_ONE), scale=-1.0),
                     reads=[rS, R("DCo")], writes=[rS])
                S.op("act", lambda e: e.activation(out=sbuf_[:, 0:ncols], in_=sbuf_[:, 0:ncols], func=AF.Exp, scale=0.5),
                     reads=[rS], writes=[rS])
            else:
                S.op("act", lambda e: e.activation(out=sbuf_[:, 0:ncols], in_=sbuf_[:, 0:ncols], func=AF.Sqrt,
                                                   bias=dc(DC_ONE), scale=-1.0),
                     reads=[rS, R("DCo")], writes=[rS])
            S.op("dve", lambda e: e.scalar_tensor_tensor(out=ub[:, 0:ncols], in0=ub[:, 0:ncols], scalar=1.0,
                                                         in1=xcr[:, 0:ncols], op0=ALU.add, op1=ALU.mult),
                 reads=[rU, rXCR], writes=[rU])
            S.op("dve", lambda e: e.scalar_tensor_tensor(out=ub[:, 0:ncols], in0=ub[:, 0:ncols], scalar=0.5,
                                                         in1=sbuf_[:, 0:ncols], op0=ALU.mult, op1=ALU.mult),
                 reads=[rU, rS], writes=[rU])
            if g.main and g.p == 0:
                S.op("dve", lambda e: e.tensor_scalar(out=ub[:, 0:HALO], in0=ub[:, 0:HALO], scalar1=cp(C_FLAG),
                                                      scalar2=None, op0=ALU.mult),
                     reads=[rU, rCP], writes=[rU])
            S.op("dve", lambda e: e.tensor_tensor_scan(out=xc[:, 0:npc], data0=ab[:, 0:npc], data1=ub[:, 0:npc],
                                                       initial=PS_h[:, n:n + 1], op0=ALU.mult, op1=ALU.add),
                 reads=[rA, rU, R("PS_h", n)], writes=[rXC])
            S.op("dve", lambda e: e.tensor_copy(out=PS_h[:, n:n + 1], in_=xc[:, npc - 1:npc]),
                 reads=[rXC], writes=[R("PS_h", n)])
            if g.main:
                for j in range(SQ):
                    c0 = npc + 8 * j
                    S.op("dve", lambda e, j=j, c0=c0: e.tensor_tensor_scan(
                        out=xc[:, c0:c0 + 8], data0=ab[:, c0:c0 + 8], data1=ub[:, c0:c0 + 8],
                        initial=ST_h[:, n, g.p * SQ + j:g.p * SQ + j + 1], op0=ALU.mult, op1=ALU.add),
                        reads=[rA, rU, R("ST_h", n)], writes=[rXCs])
                S.op("dve", lambda e: e.tensor_copy(out=ST_h[:, n, g.p * SQ:(g.p + 1) * SQ], in_=V(xc, npc + 7, [8, SQ])),
                     reads=[rXCs], writes=[R("ST_h", n)])
                S.op("dve", lambda e: e.tensor_tensor(out=gg[:, 0:ncols], in0=gg[:, 0:ncols], in1=xc[:, 0:ncols],
                                                      op=ALU.mult),
                     reads=[rGG, rXC, rXCs], writes=[rGG])

        def phC():
            norm_phase(g, gg, [rGG], b, C_GOA + n, out_slot)

        return [phA, phB, phC] if g.main else [phA, phB]

    def tails2(xp, rXPt, PSt, rPSt, STt, rSTt, idx, p, npc):
        nsx = npc + 2
        S.op("pool", lambda e: e.tensor_copy(out=xp[:, 0:2], in_=PSt[:, idx, :]), reads=[rPSt], writes=[rXPt])
        S.op("pool", lambda e: e.tensor_copy(out=V(xp, nsx, [10, SQ], [1, 2]),
                                             in_=V(STt, idx * 32 + p * 16, [2, SQ], [1, 2])),
             reads=[rSTt], writes=[rXPt])

    def tails2_save(xp, rXPb, rXPt, PSt, rPSt, STt, rSTt, idx, p, npc):
        nsx = npc + 2
        S.op("pool", lambda e: e.tensor_copy(out=PSt[:, idx, :], in_=xp[:, npc:npc + 2]),
             reads=[rXPb, rXPt], writes=[rPSt])
        S.op("pool", lambda e: e.tensor_copy(out=V(STt, idx * 32 + p * 16, [2, SQ], [1, 2]),
                                             in_=V(xp, nsx + 8, [10, SQ], [1, 2])),
             reads=[rXPb, rXPt], writes=[rSTt])

    def conv3(xp, xc, rXPb, rXPt, rXCb, npc, cwcol, bias_col):
        nsx = npc + 2
        cw = lambda k: cp(cwcol + k)
        if bias_col is None:
            S.op("act", lambda e: e.activation(out=xc[:, 0:npc], in_=xp[:, 0:npc], func=AF.Copy, scale=cw(0)),
                 reads=[rXPb, rXPt, rCP], writes=[rXCb])
            S.op("act", lambda e: e.activation(out=V(xc, npc, [8, SQ], [1, 8]), in_=V(xp, nsx, [10, SQ], [1, 8]),
                                               func=AF.Copy, scale=cw(0)),
                 reads=[rXPb, rXPt, rCP], writes=[rXCb])
        else:
            S.op("act", lambda e: e.activation(out=xc[:, 0:npc], in_=xp[:, 0:npc], func=AF.Identity, bias=cp(bias_col),
                                               scale=cw(0)),
                 reads=[rXPb, rXPt, rCP], writes=[rXCb])
            S.op("act", lambda e: e.activation(out=V(xc, npc, [8, SQ], [1, 8]), in_=V(xp, nsx, [10, SQ], [1, 8]),
                                               func=AF.Identity, bias=cp(bias_col), scale=cw(0)),
                 reads=[rXPb, rXPt, rCP], writes=[rXCb])
        for k in range(1, 3):
            S.op("dve", lambda e, k=k: e.scalar_tensor_tensor(out=xc[:, 0:npc], in0=xp[:, k:k + npc], scalar=cw(k),
                                                             in1=xc[:, 0:npc], op0=ALU.mult, op1=ALU.add),
                 reads=[rXPb, rXPt, rXCb, rCP], writes=[rXCb])
            S.op("dve", lambda e, k=k: e.scalar_tensor_tensor(out=V(xc, npc, [8, SQ], [1, 8]),
                                                             in0=V(xp, nsx + k, [10, SQ], [1, 8]), scalar=cw(k),
                                                             in1=V(xc, npc, [8, SQ], [1, 8]), op0=ALU.mult, op1=ALU.add),
                 reads=[rXPb, rXPt, rXCb, rCP], writes=[rXCb])

    def sconv_unit(gi, g, out_slot):
        b = bufs()
        npc, nt, ncols, nh = g.npc, g.nt, g.ncols, g.nh
        xp, xc, gg, ub = b.xp, b.xc, b.gg, b.ub
        rXP, rXPt, rXC, rGG, rU = b.rXP, b.rXPt, b.rXC, b.rGG, b.rU
        srcN = lambda k, h: N[:, k, h * nt:(h + 1) * nt]
        resN = lambda k: R("N", k)
        nsx = npc + 2
        vh = lambda t: V(t, 0, [nt, nh], [1, nt])

        def phA():
            ws_vb = load_w(wA[40 + gi], 2048)
            ps_vb = next_ps()
            mm_group(ps_vb, ws_vb, 0, P, KD, srcN, resN, nt)
            S.op("act", lambda e: e.activation(out=vh(gg), in_=PSM[ps_vb][:, :, 0:nt], func=AF.Copy),
                 reads=[R("PS", ps_vb)], writes=[rGG])
            ws_gc = load_w(wA[32 + gi], 2048)
            ps_gc = next_ps()
            mm_group(ps_gc, ws_gc, 0, P, KD, srcN, resN, nt)
            tails2(xp, rXPt, PS_sc, R("PS_sc", gi), ST_sc, R("ST_sc", gi), gi, g.p, npc)
            S.op("dve", lambda e: e.tensor_tensor(out=xp[:, 2:2 + nt], in0=PSM[ps_gc][:, 0, 0:nt], in1=gg[:, 0:nt],
                                                  op=ALU.mult),
                 reads=[R("PS", ps_gc), rGG], writes=[rXP])
            S.op("dve", lambda e: e.tensor_tensor(out=xp[:, 2 + nt:2 + npc], in0=PSM[ps_gc][:, 1, 0:npc - nt],
                                                  in1=gg[:, nt:npc], op=ALU.mult),
                 reads=[R("PS", ps_gc), rGG], writes=[rXP])
            S.op("dve", lambda e: e.tensor_tensor(out=V(xp, nsx + 2, [10, SQ], [1, 8]),
                                                  in0=V(PSM[ps_gc], 512 + npc - nt, [8, SQ], [1, 8]),
                                                  in1=V(gg, npc, [8, SQ], [1, 8]), op=ALU.mult),
                 reads=[R("PS", ps_gc), rGG], writes=[rXP])
            tails2_save(xp, rXP, rXPt, PS_sc, R("PS_sc", gi), ST_sc, R("ST_sc", gi), gi, g.p, npc)
            ws_gb = load_w(wA[24 + gi], 2048)
            ps_gb = next_ps()
            mm_group(ps_gb, ws_gb, 0, P, KD, srcN, resN, nt)
            S.op("act", lambda e: e.activation(out=vh(ub), in_=PSM[ps_gb][:, :, 0:nt], func=AF.Copy),
                 reads=[R("PS", ps_gb)], writes=[rU])
            conv3(xp, xc, rXP, rXPt, rXC, npc, C_CBW + gi * 3, None)
            S.op("dve", lambda e: e.tensor_tensor(out=gg[:, 0:ncols], in0=xc[:, 0:ncols], in1=ub[:, 0:ncols], op=ALU.mult),
                 reads=[rXC, rU, rGG], writes=[rGG])

        def phB():
            pass

        def phC():
            norm_phase(g, gg, [rGG], b, C_GOB + gi, out_slot)

        return [phA, phB, phC]

    def wo_partial(grp, nk, g):
        nt = g.nt
        half = (grp % 2) * YG
        for dd in range(4):
            ws = load_w(wO[grp * 4 + dd][:, 0:nk * 512], nk * 512)
            for d4 in range(4):
                d = dd * 4 + d4
                ps = next_ps()
                mm_group(ps, ws, d4 * P, 512, nk, lambda k, h: YH[:, half + k, h * nt:(h + 1) * nt],
                         lambda k: R("YH", half + k), nt)
                S.op("dve", lambda e, d=d, ps=ps: e.tensor_tensor(out=V(X, d * NC, [nt, 2], [1, nt]),
                                                                in0=PSM[ps][:, :, 0:nt],
                                                                in1=V(X, d * NC, [nt, 2], [1, nt]), op=ALU.add),
                     reads=[R("PS", ps), R("X", d)], writes=[R("X", d)])

    def ffn_unit(j, g, out_slot):
        b = bufs()
        npc, nt, ncols, nh = g.npc, g.nt, g.ncols, g.nh
        xp, xc, gg = b.xp, b.xc, b.gg
        rXP, rXPt, rXC, rGG = b.rXP, b.rXPt, b.rXC, b.rGG
        srcN = lambda k, h: N[:, k, h * nt:(h + 1) * nt]
        resN = lambda k: R("N", k)
        nsx = npc + 2
        vh = lambda t: V(t, 0, [nt, nh], [1, nt])

        def phA():
            ws_g = load_w(wU[j], 2048)
            ps_g = next_ps()
            mm_group(ps_g, ws_g, 0, P, KD, srcN, resN, nt)
            tails2(xp, rXPt, PS_fc, R("PS_fc", j), ST_fc, R("ST_fc", j), j, g.p, npc)
            S.op("act", lambda e: e.activation(out=xp[:, 2:2 + nt], in_=PSM[ps_g][:, 0, 0:nt], func=AF.Copy),
                 reads=[R("PS", ps_g)], writes=[rXP])
            S.op("act", lambda e: e.activation(out=xp[:, 2 + nt:2 + npc], in_=PSM[ps_g][:, 1, 0:npc - nt], func=AF.Copy),
                 reads=[R("PS", ps_g)], writes=[rXP])
            S.op("act", lambda e: e.activation(out=V(xp, nsx + 2, [10, SQ], [1, 8]),
                                               in_=V(PSM[ps_g], 512 + npc - nt, [8, SQ], [1, 8]), func=AF.Copy),
                 reads=[R("PS", ps_g)], writes=[rXP])
            tails2_save(xp, rXP, rXPt, PS_fc, R("PS_fc", j), ST_fc, R("ST_fc", j), j, g.p, npc)
            ws_v = load_w(wU[NF + j], 2048)
            ps_v = next_ps()
            mm_group(ps_v, ws_v, 0, P, KD, srcN, resN, nt)
            S.op("act", lambda e: e.activation(out=vh(gg), in_=PSM[ps_v][:, :, 0:nt], func=AF.Copy),
                 reads=[R("PS", ps_v)], writes=[rGG])
            conv3(xp, xc, rXP, rXPt, rXC, npc, C_CFW + j * 3, C_CFB + j)

        def phB():
            S.op("act", lambda e: e.activation(out=xc[:, 0:ncols], in_=xc[:, 0:ncols], func=AF.Gelu_apprx_tanh),
                 reads=[rXC], writes=[rXC])
            S.op("dve", lambda e: e.tensor_tensor(out=YH[:, out_slot, 0:ncols], in0=gg[:, 0:ncols], in1=xc[:, 0:ncols],
                                                  op=ALU.mult),
                 reads=[rGG, rXC], writes=[R("YH", out_slot)])

        return [phA, phB]

    def down_partial(grp, g):
        nt = g.nt
        half = (grp % 2) * YG
        for dd in range(4):
            ws = load_w(wD[grp * 4 + dd], 2048)
            for d4 in range(4):
                d = dd * 4 + d4
                ps = next_ps()
                mm_group(ps, ws, d4 * P, 512, YG, lambda k, h: YH[:, half + k, h * nt:(h + 1) * nt],
                         lambda k: R("YH", half + k), nt)
                S.op("dve", lambda e, d=d, ps=ps: e.tensor_tensor(out=V(X, d * NC, [nt, 2], [1, nt]),
                                                                in0=PSM[ps][:, :, 0:nt],
                                                                in1=V(X, d * NC, [nt, 2], [1, nt]), op=ALU.add),
                     reads=[R("PS", ps), R("X", d)], writes=[R("X", d)])

    def run_pipelined(units, on_done=None, lags=(0, 2, 3), newest_first=True, done_delay=1):
        n = len(units)
        pending = []
        for t in range(n + max(lags) + done_delay + 1):
            for ph in (range(len(lags)) if newest_first else reversed(range(len(lags)))):
                u = t - lags[ph]
                if 0 <= u < n and ph < len(units[u]):
                    units[u][ph]()
                    if ph == len(units[u]) - 1:
                        pending.append((t + done_delay, u))
            if on_done is not None:
                for (td, u) in [x for x in pending if x[0] <= t]:
                    on_done(u)
            pending = [x for x in pending if x[0] > t]

    def store_y(p):
        tiles = [(i * P, P) for i in range(4)] + [(512, 72)]
        for ti, (c0, nr) in enumerate(tiles):
            s = nxt("io", NIO)
            for kh in range(2):
                ps = next_ps()
                for i in range(8):
                    k = kh * 8 + i
                    S.op("pe", lambda e, k=k, i=i, ps=ps, c0=c0, nr=nr: e.transpose(
                        out=VP(PSM[ps], i * P, nr, [1, P]), in_=X[:, k, c0:c0 + nr], identity=IDT[:]),
                        reads=[R("X", k), rIDT], writes=[R("PS", ps)])
                copy_op(evac_eng(), IO[s][0:nr, kh * 1024:(kh + 1) * 1024], VP(PSM[ps], 0, nr, [1, 1024]),
                        [R("PS", ps)], [R("IO", s)])
            if ti < 4:
                S.dma(STQ, lambda e, s=s, c0=c0, p=p: e.dma_start(out=y_p[p * PM + c0:p * PM + c0 + P, :], in_=IO[s][:, :]),
                      reads=[R("IO", s)], lane=("io", s))
            else:
                S.dma(STQ, lambda e, s=s, p=p: e.dma_start(out=y_p[p * PM + 512:p * PM + 520, :], in_=IO[s][0:8, :]),
                      reads=[R("IO", s)], lane=("io", s))
                S.dma(STQ, lambda e, s=s, p=p: e.dma_start(out=y_s[p * NS:(p + 1) * NS, :], in_=IO[s][8:72, :]),
                      reads=[R("IO", s)], lane=("io", s))

    def store_states(part):
        def tr_out(src_fn, nblk, nrows, dst_ap_fn, width):
            s = nxt("io", NIO)
            done = 0
            while done < nblk:
                nb = min(8, nblk - done)
                ps = next_ps()
                for i in range(nb):
                    src, rres = src_fn(done + i)
                    S.op("pe", lambda e, i=i, ps=ps, src=src: e.transpose(out=VP(PSM[ps], i * P, nrows, [1, P]),
                                                                        in_=src, identity=IDT[:]),
                         reads=[rres, rIDT], writes=[R("PS", ps)])
                copy_op(evac_eng(), IO[s][0:nrows, done * P:(done + nb) * P], VP(PSM[ps], 0, nrows, [1, nb * P]),
                        [R("PS", ps)], [R("IO", s)])
                done += nb
            dst_ap_fn(s)

        if part == 0:
          tr_out(lambda n: (ST_h[:, n, :], R("ST_h", n)), NH, 16,
               lambda s: S.dma(STQ, lambda e: e.dma_start(out=o_sh, in_=IO[s][0:16, 0:DA]), reads=[R("IO", s)],
                               lane=("io", s)), DA)
          tr_out(lambda n: (ST_rc[:, n, :], R("ST_rc", n)), NH, 48,
               lambda s: S.dma(STQ, lambda e: e.dma_start(out=o_src, in_=IO[s][0:48, 0:DA]), reads=[R("IO", s)],
                               lane=("io", s)), DA)
          tr_out(lambda gi: (ST_sc[:, gi, :], R("ST_sc", gi)), NG, 32,
               lambda s: S.dma(STQ, lambda e: e.dma_start(out=o_ssc, in_=IO[s][0:32, 0:DB]), reads=[R("IO", s)],
                               lane=("io", s)), DB)
        for q in (range(3) if part == 1 else ()):
            tr_out(lambda j, q=q: (ST_fc[:, q * 16 + j, :], R("ST_fc", q * 16 + j)), 16, 32,
                   lambda s, q=q: S.dma(STQ, lambda e: e.dma_start(out=o_sfc[:, q * 2048:(q + 1) * 2048],
                                                                      in_=IO[s][0:32, :]),
                                        reads=[R("IO", s)], lane=("io", s)), 2048)
        plist = ((PS_h[:, :], NH, o_ph, [R("PS_h", n) for n in range(NH)]),
                 (V(PS_rc, 0, [1, NH * 3]), NH * 3, o_prc, [R("PS_rc", n) for n in range(NH)]),
                 (V(PS_sc, 0, [1, NG * 2]), NG * 2, o_psc, [R("PS_sc", n) for n in range(NG)]),
                 (V(PS_fc, 0, [1, NF * 2]), NF * 2, o_pfc, [R("PS_fc", n) for n in range(NF)]))
        for (src, nrows, dst, res) in (plist[0:3] if part == 0 else plist[3:4]):
            s = nxt("io", NIO)
            ps = next_ps()
            S.op("pe", lambda e, ps=ps, src=src, nrows=nrows: e.transpose(out=VP(PSM[ps], 0, nrows, [1, P]), in_=src,
                                                                         identity=IDT[:]),
                 reads=res + [rIDT], writes=[R("PS", ps)])
            copy_op(evac_eng(), IO[s][0:nrows, 0:P], VP(PSM[ps], 0, nrows, [1, P]), [R("PS", ps)], [R("IO", s)])
            S.dma(STQ, lambda e, s=s, nrows=nrows, dst=dst: e.dma_start(out=dst, in_=IO[s][0:nrows, 0:P]),
                  reads=[R("IO", s)], lane=("io", s))


    gp = Geom()
    gp.npc, gp.nt, gp.ncols, gp.nh, gp.main, gp.p = 512, 512, 512, 1, False, 0
    for q in range(2):
        load_x_tiles([(i * P, P, [(0, P, xw[q * 512 + i * P:q * 512 + (i + 1) * P, :])]) for i in range(4)])
        rmsnorm_fm(512, 512, C_GMIX)
        run_pipelined([lru_unit(n, gp, None) for n in range(NH)], lags=(0, 2))
    S.op("dve", lambda e: e.tensor_scalar(out=PS_h[:], in0=PS_h[:], scalar1=cp(C_FLAG), scalar2=None, op0=ALU.mult),
         reads=[R("PS_h", n) for n in range(NH)] + [rCP], writes=[R("PS_h", n) for n in range(NH)])

    load_states()

    for p in range(2):
        g = Geom()
        g.npc, g.nt, g.ncols, g.nh, g.main, g.p = PM, NT, NC, 2, True, p
        base = NPRE + p * PM
        tiles = [(i * P, P, [(0, P, xw[base + i * P:base + (i + 1) * P, :])]) for i in range(4)]
        tiles.append((512, 72, [(0, 8, xw[base + 512:base + 520, :]), (8, 64, xs[p * NS:(p + 1) * NS, :])]))
        load_x_tiles(tiles)
        rmsnorm_fm(NC, NT, C_GMIX)
        units = []
        for c in range(NMIX):
            slot = c % (2 * YG)
            units.append(lru_unit(c, g, slot) if c < NH else sconv_unit(c - NH, g, slot))

        def mix_done(u, g=g):
            if u % YG == YG - 1:
                wo_partial(u // YG, YG, g)
        run_pipelined(units, mix_done)
        rmsnorm_fm(NC, NT, C_GFFN)
        if p == 1:
            store_states(0)
        funits = [ffn_unit(j, g, j % (2 * YG)) for j in range(NF)]

        def ffn_done(u, g=g):
            if u % YG == YG - 1:
                down_partial(u // YG, g)
        run_pipelined(funits, ffn_done, lags=(0, 1), newest_first=False)
        rmsnorm_fm(NC, NT, C_GFIN, rounded=False)
        store_y(p)
    store_states(1)

    S.emit(nc, es)
    es.close()
    return nc


_PROG = {}


def _tile_cols(w, ncb):
    K = w.shape[0]
    return np.ascontiguousarray(w.reshape(K // P, P, ncb, P).transpose(2, 1, 0, 3)).reshape(ncb, P, (K // P) * P)


def _tile_rows(w, gk):
    K = w.shape[0]
    ng = K // (gk * P)
    a = w.reshape(ng, gk, P, 4, 512).transpose(0, 3, 2, 1, 4)
    return np.ascontiguousarray(a).reshape(ng * 4, P, gk * 512)


def _fm(v, n):
    return np.ascontiguousarray(v.reshape(n, P).T)


def kernel(x_prompt, x_sample, state_lru_h, state_lru_conv, state_sconv, state_ffn_conv, meta_tokens, g_mix, w_in,
           conv_a_w, conv_a_b, w_gate_a, b_gate_a, w_gate_x, b_gate_x, lru_lambda, conv_b_w, g_out_a, g_out_b, w_o,
           g_ffn, w_up, conv_f_w, conv_f_b, w_down, g_final):
    f32 = np.float32
    A = lambda a: np.asarray(a, dtype=f32)
    x_prompt, x_sample = A(x_prompt), A(x_sample)
    import os
    stage = os.environ.get("KSTAGE")
    key = ("nc", stage)
    if key not in _PROG:
        _PROG[key] = build_program(stop_after=None if stage is None else int(stage))
    nc = _PROG[key]
    wA_ = _tile_cols(A(w_in)[0], 48)
    wU_ = _tile_cols(A(w_up)[0], 96)
    wO_ = _tile_rows(A(w_o)[0], YG)
    wD_ = _tile_rows(A(w_down)[0], YG)
    wG_ = np.ascontiguousarray(np.concatenate([A(w_gate_a)[0], A(w_gate_x)[0]], axis=2))
    cpar = np.zeros((P, NCPAR), f32)
    cpar[:, C_GMIX:C_GMIX + 16] = _fm(A(g_mix)[0], 16)
    cpar[:, C_GFFN:C_GFFN + 16] = _fm(A(g_ffn)[0], 16)
    cpar[:, C_GFIN:C_GFIN + 16] = _fm(A(g_final), 16)
    cpar[:, C_CAB:C_CAB + 12] = _fm(A(conv_a_b)[0], 12)
    cpar[:, C_BGA:C_BGA + 12] = _fm(A(b_gate_a)[0], 12)
    cpar[:, C_BGX:C_BGX + 12] = _fm(A(b_gate_x)[0], 12)
    cpar[:, C_LAM:C_LAM + 12] = _fm(A(lru_lambda)[0], 12)
    cpar[:, C_GOA:C_GOA + 12] = _fm(A(g_out_a)[0], 12)
    cpar[:, C_GOB:C_GOB + 8] = _fm(A(g_out_b)[0], 8)
    cpar[:, C_CFB:C_CFB + 48] = _fm(A(conv_f_b)[0], 48)
    caw = A(conv_a_w)[0]
    cpar[:, C_CAW:C_CAW + 48] = caw.reshape(4, NH, P).transpose(2, 1, 0).reshape(P, 48)
    cbw = A(conv_b_w)[0]
    cpar[:, C_CBW:C_CBW + 24] = cbw.reshape(3, NG, P).transpose(2, 1, 0).reshape(P, 24)
    cfw = A(conv_f_w)[0]
    cpar[:, C_CFW:C_CFW + 144] = cfw.reshape(3, NF, P).transpose(2, 1, 0).reshape(P, 144)
    ident = np.eye(P, dtype=f32)
    meta = A(meta_tokens)
    in_maps = []
    for c in range(NCORES):
        b, half = c // 2, c % 2
        seq = np.concatenate([meta, x_prompt[b]], axis=0)
        if half == 0:
            xw_ = np.concatenate([np.zeros((1032, D), f32), seq[0:1032]], axis=0)
        else:
            xw_ = seq
        cp_c = cpar.copy()
        cp_c[:, C_FLAG] = float(half)
        sl = slice(16 * c, 16 * c + 16)
        st_hr = np.concatenate([A(state_lru_h)[0, sl], A(state_lru_conv)[0, sl].reshape(48, DA)], axis=0)
        in_maps.append({
            "xw": np.ascontiguousarray(xw_), "xs": np.ascontiguousarray(x_sample[sl].reshape(128, D)),
            "st_hr": np.ascontiguousarray(st_hr),
            "st_sc": np.ascontiguousarray(A(state_sconv)[0, sl].reshape(32, DB)),
            "st_fc": np.ascontiguousarray(A(state_ffn_conv)[0, sl].reshape(32, DFF)),
            "cpar": cp_c, "ident": ident, "wA": wA_, "wG": wG_, "wO": wO_, "wU": wU_, "wD": wD_,
        })
    res = run_bass_kernel_spmd(nc, in_maps, core_ids=list(range(NCORES)))
    r = res.results
    B = x_prompt.shape[0]
    y_prompt = np.zeros((B, 2048, D), f32)
    y_sample = np.zeros((128, 8, D), f32)
    p_h = np.zeros((1, B, DA), f32)
    p_rc = np.zeros((1, B, 3, DA), f32)
    p_sc = np.zeros((1, B, 2, DB), f32)
    p_fc = np.zeros((1, B, 2, DFF), f32)
    s_h = np.zeros((1, 128, DA), f32)
    s_rc = np.zeros((1, 128, 3, DA), f32)
    s_sc = np.zeros((1, 128, 2, DB), f32)
    s_fc = np.zeros((1, 128, 2, DFF), f32)
    for c in range(NCORES):
        b, half = c // 2, c % 2
        yp = r[c]["y_p"][HALO:]
        if half == 0:
            y_prompt[b, 0:1016] = yp[16:1032]
        else:
            y_prompt[b, 1016:2048] = yp
            p_h[0, b] = r[c]["o_ph"].reshape(DA)
            p_rc[0, b] = r[c]["o_prc"].reshape(NH, 3, P).transpose(1, 0, 2).reshape(3, DA)
            p_sc[0, b] = r[c]["o_psc"].reshape(NG, 2, P).transpose(1, 0, 2).reshape(2, DB)
            p_fc[0, b] = r[c]["o_pfc"].reshape(NF, 2, P).transpose(1, 0, 2).reshape(2, DFF)
        sl = slice(16 * c, 16 * c + 16)
        y_sample[sl] = r[c]["y_s"].reshape(16, 8, D)
        s_h[0, sl] = r[c]["o_sh"]
        s_rc[0, sl] = r[c]["o_src"].reshape(16, 3, DA)
        s_sc[0, sl] = r[c]["o_ssc"].reshape(16, 2, DB)
        s_fc[0, sl] = r[c]["o_sfc"].reshape(16, 2, DFF)
    return (y_prompt, y_sample, p_h, p_rc, p_sc, p_fc, s_h, s_rc, s_sc, s_fc)
```
